# Optimizing a Trainium2 kernel written in Bass

```python
import math
import jax, jax.numpy as jnp
from jax import lax
import numpy as np


D_MODEL = 1024
BATCH = 16
SEQ = 256
DEPTH = 4
DEC_BATCH = 8
DEC_SEQ = 4096
PAST_LEN = 512

GRID_W = 64
CHUNK = 128
Q_BLOCK = 128
HEAD_DIM = 64
D_A = D_MODEL // 2
SG_GROUP_W = 128
G_A = D_A // SG_GROUP_W
H_B = D_MODEL // 256
D_B = H_B * 2 * HEAD_DIM
H_C = D_MODEL // 128
D_C = H_C * HEAD_DIM
WIN_R = 8
WIN_W = 16
ROPE_THETA = 10000.0
ALPHA = (2 * DEPTH) ** 0.25
BETA = (8 * DEPTH) ** -0.25
EPS = 1e-6
NEG_INF = -1e30
SPLIT_SIZES = (D_A,) * 3 + (D_B,) * 4 + (D_C,) * 4
SPLIT_POINTS = tuple(int(s) for s in np.cumsum(SPLIT_SIZES)[:-1])
D_IN = int(sum(SPLIT_SIZES))

kernel_name = "hybrid_diffusion_gmlp_diffattn_natten_step"


def layernorm(x):
    xf = x.astype(jnp.float32)
    mu = jnp.mean(xf, axis=-1, keepdims=True)
    var = jnp.mean(jnp.square(xf - mu), axis=-1, keepdims=True)
    return ((xf - mu) * lax.rsqrt(var + EPS)).astype(x.dtype)


def modulate(x, mod):
    shift, scale, gate = jnp.split(mod, 3, axis=-1)
    return layernorm(x) * (1 + scale) + shift, gate


def post_norm(x, out, gate, g, b):
    return layernorm(ALPHA * x + gate * out) * g + b


def axial_rope_tables(n_tok):
    t = jnp.arange(n_tok)
    half = HEAD_DIM // 2
    inv = 1.0 / (ROPE_THETA ** (jnp.arange(0, half, 2, dtype=jnp.float32) / half))
    ang_r = (t // GRID_W).astype(jnp.float32)[:, None] * inv
    ang_c = (t % GRID_W).astype(jnp.float32)[:, None] * inv
    ang = jnp.concatenate([ang_r, ang_c], axis=-1)
    return jnp.cos(ang), jnp.sin(ang)


def apply_axial_rope(x, cos, sin):
    L = x.shape[-2]
    quarter = HEAD_DIM // 4
    xr = x.astype(jnp.float32).reshape(*x.shape[:-1], 2, 2, quarter)
    x1, x2 = xr[..., 0, :], xr[..., 1, :]
    c = cos.reshape(L, 2, quarter)
    s = sin.reshape(L, 2, quarter)
    out = jnp.stack([x1 * c - x2 * s, x2 * c + x1 * s], axis=-2)
    return out.reshape(x.shape).astype(x.dtype)


def to_heads(z, n_heads):
    B, L, _ = z.shape
    return z.reshape(B, L, n_heads, -1).transpose(0, 2, 1, 3)


def from_heads(o):
    B, H, L, d = o.shape
    return o.transpose(0, 2, 1, 3).reshape(B, L, H * d)


def diff_heads(zq, zk, zv):
    B, L, _ = zq.shape
    q = zq.reshape(B, L, 2, H_B, HEAD_DIM).transpose(0, 2, 3, 1, 4)
    k = zk.reshape(B, L, 2, H_B, HEAD_DIM).transpose(0, 2, 3, 1, 4)
    v = zv.reshape(B, L, H_B, 2 * HEAD_DIM).transpose(0, 2, 1, 3)
    return q, k, v


def to_blocks(q):
    *lead, L, d = q.shape
    return jnp.moveaxis(q.reshape(*lead, L // Q_BLOCK, Q_BLOCK, d), -3, 0)


def from_blocks(o):
    o = jnp.moveaxis(o, 0, -3)
    return o.reshape(*o.shape[:-3], -1, o.shape[-1])


def chunk_spatial_gate(u, v, g, b, w_s, b_s):
    B, L, _ = v.shape
    vn = layernorm(v) * g + b
    vc = vn.reshape(B, L // CHUNK, CHUNK, G_A, SG_GROUP_W)
    sv = jnp.einsum("gpq,bnqgc->bnpgc", w_s, vc) + b_s.T[None, None, :, :, None]
    return u * sv.reshape(B, L, D_A)


def diff_lambda(lq1, lk1, lq2, lk2, lam_init):
    f = lambda a: a.astype(jnp.float32)
    return jnp.exp(jnp.sum(f(lq1) * f(lk1))) - jnp.exp(jnp.sum(f(lq2) * f(lk2))) + lam_init


def diff_attention(q, k, v, lam):
    scale = HEAD_DIM ** -0.5

    def block(qb):
        s = jnp.einsum("bmhqd,bmhkd->bmhqk", qb, k, preferred_element_type=jnp.float32) * scale
        a = jax.nn.softmax(s, axis=-1)
        w = a[:, 0] - lam * a[:, 1]
        return jnp.einsum("bhqk,bhkd->bhqd", w.astype(v.dtype), v)

    return from_blocks(lax.map(block, to_blocks(q)))


def diff_finish(o, g, lam_init):
    of = o.astype(jnp.float32)
    of = of * lax.rsqrt(jnp.mean(of * of, axis=-1, keepdims=True) + EPS)
    of = of * g.astype(jnp.float32) * (1.0 - lam_init)
    return from_heads(of.astype(o.dtype))


def softmax_attention(q, k, v):
    scale = HEAD_DIM ** -0.5

    def block(qb):
        s = jnp.einsum("bhqd,bhkd->bhqk", qb, k, preferred_element_type=jnp.float32) * scale
        p = jax.nn.softmax(s, axis=-1)
        return jnp.einsum("bhqk,bhkd->bhqd", p.astype(v.dtype), v)

    return from_blocks(lax.map(block, to_blocks(q)))


def neighbourhood_attention(q, k, v, kc, vc, rel_bias):
    B, H, L, d = q.shape
    rows = L // GRID_W
    wr = min(WIN_R, rows)
    scale = d ** -0.5
    qg = q.reshape(B, H, rows, GRID_W, d)
    kg = k.reshape(B, H, rows, GRID_W, d)
    vg = v.reshape(B, H, rows, GRID_W, d)
    cpos = jnp.arange(GRID_W)
    cstart = jnp.clip(cpos - WIN_W // 2, 0, GRID_W - WIN_W)
    colmask = (cpos[None, :] >= cstart[:, None]) & (cpos[None, :] < cstart[:, None] + WIN_W)
    dx_idx = jnp.clip(cpos[None, :] - cpos[:, None], -(WIN_W - 1), WIN_W - 1) + WIN_W - 1

    def row_block(r):
        rs = jnp.clip(r - wr // 2, 0, rows - wr)
        qr = lax.dynamic_index_in_dim(qg, r, axis=2, keepdims=False)
        kb = lax.dynamic_slice_in_dim(kg, rs, wr, axis=2)
        vb = lax.dynamic_slice_in_dim(vg, rs, wr, axis=2)
        dy_idx = rs + jnp.arange(wr) - r + WIN_R - 1
        bias = rel_bias[:, dy_idx[None, :, None], dx_idx[:, None, :]]
        s_loc = jnp.einsum("bhqd,bhjkd->bhqjk", qr, kb, preferred_element_type=jnp.float32) * scale + bias
        s_loc = jnp.where(colmask[:, None, :], s_loc, NEG_INF)
        s_ctx = jnp.einsum("bhqd,bhkd->bhqk", qr, kc, preferred_element_type=jnp.float32) * scale
        n_loc = wr * GRID_W
        s = jnp.concatenate([s_loc.reshape(B, H, GRID_W, n_loc), s_ctx], axis=-1)
        p = jax.nn.softmax(s, axis=-1).astype(v.dtype)
        p_loc = p[..., :n_loc].reshape(B, H, GRID_W, wr, GRID_W)
        p_ctx = p[..., n_loc:]
        return (jnp.einsum("bhqjk,bhjkd->bhqd", p_loc, vb)
                + jnp.einsum("bhqk,bhkd->bhqd", p_ctx, vc))

    out = lax.map(row_block, jnp.arange(rows))
    return jnp.moveaxis(out, 0, 2).reshape(B, H, L, d)


def merge_out(h, y_a, y_b, y_c, w_mg, b_mg, w_a, w_b, w_c, w_o):
    g_a, g_b, g_c = jnp.split(jax.nn.sigmoid(h @ w_mg + b_mg), 3, axis=-1)
    m = g_a * (y_a @ w_a) + g_b * (y_b @ w_b) + g_c * (y_c @ w_c)
    return m @ w_o


def setup_inputs(seed: int = 0) -> dict:
    key = jax.random.key(seed)
    ks = jax.random.split(key, 32)
    nrm = lambda k, shape, s: jax.random.normal(k, shape, jnp.float32) * s
    D = D_MODEL
    return {
        "x_prompt": nrm(ks[0], (BATCH, SEQ, D), 1.0),
        "x_sample": nrm(ks[1], (DEC_BATCH, DEC_SEQ, D), 1.0),
        "c": nrm(ks[2], (DEC_BATCH, D), 1.0),
        "cache_diff_k": nrm(ks[3], (DEC_BATCH, DEPTH, 2, H_B, PAST_LEN, HEAD_DIM), 1.0),
        "cache_diff_v": nrm(ks[4], (DEC_BATCH, DEPTH, H_B, PAST_LEN, 2 * HEAD_DIM), 1.0),
        "cache_na_k": nrm(ks[5], (DEC_BATCH, DEPTH, H_C, PAST_LEN, HEAD_DIM), 1.0),
        "cache_na_v": nrm(ks[6], (DEC_BATCH, DEPTH, H_C, PAST_LEN, HEAD_DIM), 1.0),
        "c_ctx": nrm(ks[7], (D,), 1.0),
        "w_ada": nrm(ks[8], (DEPTH, D, 3 * D), 0.5 * D ** -0.5),
        "b_ada": nrm(ks[9], (DEPTH, 3 * D), 0.02),
        "w_in": nrm(ks[10], (DEPTH, D, D_IN), D ** -0.5),
        "sg_norm_g": 1.0 + nrm(ks[11], (DEPTH, D_A), 0.02),
        "sg_norm_b": nrm(ks[12], (DEPTH, D_A), 0.02),
        "w_spatial": nrm(ks[13], (DEPTH, G_A, CHUNK, CHUNK), CHUNK ** -0.5),
        "b_spatial": 1.0 + nrm(ks[14], (DEPTH, G_A, CHUNK), 0.02),
        "lambda_q1": nrm(ks[15], (DEPTH, HEAD_DIM), 0.1),
        "lambda_k1": nrm(ks[16], (DEPTH, HEAD_DIM), 0.1),
        "lambda_q2": nrm(ks[17], (DEPTH, HEAD_DIM), 0.1),
        "lambda_k2": nrm(ks[18], (DEPTH, HEAD_DIM), 0.1),
        "diff_subln_g": 1.0 + nrm(ks[19], (DEPTH, 2 * HEAD_DIM), 0.02),
        "na_rel_bias": nrm(ks[20], (DEPTH, H_C, 2 * WIN_R - 1, 2 * WIN_W - 1), 0.1),
        "w_br_a": nrm(ks[21], (DEPTH, D_A, D), D_A ** -0.5),
        "w_br_b": nrm(ks[22], (DEPTH, D_B, D), D_B ** -0.5),
        "w_br_c": nrm(ks[23], (DEPTH, D_C, D), D_C ** -0.5),
        "w_mgate": nrm(ks[24], (DEPTH, D, 3 * D), D ** -0.5),
        "b_mgate": nrm(ks[25], (DEPTH, 3 * D), 0.02),
        "w_out": nrm(ks[26], (DEPTH, D, D), BETA * D ** -0.5),
        "ln_g": 1.0 + nrm(ks[27], (DEPTH, D), 0.02),
        "ln_b": nrm(ks[28], (DEPTH, D), 0.02),
    }


def reference(x_prompt, x_sample, c, cache_diff_k, cache_diff_v, cache_na_k, cache_na_v, c_ctx,
              w_ada, b_ada, w_in, sg_norm_g, sg_norm_b, w_spatial, b_spatial,
              lambda_q1, lambda_k1, lambda_q2, lambda_k2, diff_subln_g, na_rel_bias,
              w_br_a, w_br_b, w_br_c, w_mgate, b_mgate, w_out, ln_g, ln_b):
    cos, sin = axial_rope_tables(x_sample.shape[1])
    xp, xs = x_prompt, x_sample
    new_dk, new_dv, new_nk, new_nv = [], [], [], []
    for l in range(DEPTH):
        lam_init = 0.8 - 0.6 * math.exp(-0.3 * l)
        lam = diff_lambda(lambda_q1[l], lambda_k1[l], lambda_q2[l], lambda_k2[l], lam_init)
        merge_w = (w_mgate[l], b_mgate[l], w_br_a[l], w_br_b[l], w_br_c[l], w_out[l])

        h, gate = modulate(xp, jax.nn.silu(c_ctx) @ w_ada[l] + b_ada[l])
        u_a, v_a, g_a, q_b, k_b, v_b, g_b, q_c, k_c, v_c, g_c = jnp.split(h @ w_in[l], SPLIT_POINTS, axis=-1)
        y_a = chunk_spatial_gate(u_a, v_a, sg_norm_g[l], sg_norm_b[l], w_spatial[l], b_spatial[l]) * jax.nn.silu(g_a)
        q, k, v = diff_heads(q_b, k_b, v_b)
        y_b = diff_finish(diff_attention(q, k, v, lam), diff_subln_g[l], lam_init) * jax.nn.silu(g_b)
        qc, kc, vc = to_heads(q_c, H_C), to_heads(k_c, H_C), to_heads(v_c, H_C)
        y_c = from_heads(softmax_attention(qc, kc, vc)) * jax.nn.silu(g_c)
        out = merge_out(h, y_a, y_b, y_c, *merge_w)
        xp = post_norm(xp, out, gate, ln_g[l], ln_b[l])
        new_dk.append(k)
        new_dv.append(v)
        new_nk.append(kc)
        new_nv.append(vc)

        h, gate = modulate(xs, (jax.nn.silu(c) @ w_ada[l] + b_ada[l])[:, None, :])
        u_a, v_a, g_a, q_b, k_b, v_b, g_b, q_c, k_c, v_c, g_c = jnp.split(h @ w_in[l], SPLIT_POINTS, axis=-1)
        y_a = chunk_spatial_gate(u_a, v_a, sg_norm_g[l], sg_norm_b[l], w_spatial[l], b_spatial[l]) * jax.nn.silu(g_a)
        q, k, v = diff_heads(q_b, k_b, v_b)
        q = apply_axial_rope(q, cos, sin)
        k = apply_axial_rope(k, cos, sin)
        k_all = jnp.concatenate([k, cache_diff_k[:, l].astype(k.dtype)], axis=3)
        v_all = jnp.concatenate([v, cache_diff_v[:, l].astype(v.dtype)], axis=2)
        y_b = diff_finish(diff_attention(q, k_all, v_all, lam), diff_subln_g[l], lam_init) * jax.nn.silu(g_b)
        qc, kc, vc = to_heads(q_c, H_C), to_heads(k_c, H_C), to_heads(v_c, H_C)
        y_c = from_heads(neighbourhood_attention(qc, kc, vc, cache_na_k[:, l].astype(kc.dtype),
                                                 cache_na_v[:, l].astype(vc.dtype), na_rel_bias[l])) * jax.nn.silu(g_c)
        out = merge_out(h, y_a, y_b, y_c, *merge_w)
        xs = post_norm(xs, out, gate, ln_g[l], ln_b[l])

    new_diff_k = jnp.stack(new_dk, axis=1)
    new_diff_v = jnp.stack(new_dv, axis=1)
    new_na_k = jnp.stack(new_nk, axis=1)
    new_na_v = jnp.stack(new_nv, axis=1)
    return (xp, xs, new_diff_k, new_diff_v, new_na_k, new_na_v)
```

```python
import math
import contextlib
import numpy as np
import concourse.bass as bass
import concourse.mybir as mybir
from concourse.bass_utils import run_bass_kernel_spmd

F32 = mybir.dt.float32
BF16 = mybir.dt.bfloat16
AF = mybir.ActivationFunctionType
ALU = mybir.AluOpType
AX = mybir.AxisListType

D = 1024
DIN = 5632
GW = 64
SEQ = 256
NPS = 2
NP = NPS * SEQ
PAST = 512
EPS = 1e-6
NEG = -30000.0
OFF = dict(ua=0, va=512, ga=1024, qb=1536, kb=2048, vb=2560, gb=3072, qc=3584, kc=4096, vc=4608, gc=5120)
ALPHA_FULL = (2 * 4) ** 0.25
ARENA_KB = 170
N_DMA_SEMS = 72


class Sem:
    def __init__(self, handle):
        self.h = handle
        self.count = 0


class Op:
    __slots__ = ("eng", "fn", "deps", "signal", "sem", "val", "is_dma")

    def __init__(self, eng, fn, is_dma, sem):
        self.eng = eng
        self.fn = fn
        self.deps = []
        self.signal = is_dma
        self.sem = sem
        self.val = None
        self.is_dma = is_dma


class Prog:
    ENGS = ("pe", "act", "dve", "pool", "sp")

    def __init__(self, nc, stack):
        self.nc = nc
        self.ops = {e: [] for e in self.ENGS}
        self.eng_sem = {e: Sem(stack.enter_context(nc.semaphore("es_" + e))) for e in self.ENGS}
        self.dma_sems = [Sem(stack.enter_context(nc.semaphore("ds%d" % i))) for i in range(N_DMA_SEMS)]
        self.sem_i = 0
        self.pool_sems = [Sem(stack.enter_context(nc.semaphore("pq%d" % i))) for i in range(12)]
        self.pool_prev = [None] * 12
        self.pool_i = 0
        self.last_w = {}
        self.readers = {}
        self.last_dma = {}
        self.barrier_ops = {e: [] for e in self.ENGS}

    def reset_sems(self):
        self.sem_i = 0

    def new_sem(self):
        s = self.dma_sems[self.sem_i]
        self.sem_i += 1
        return s

    def barrier(self):
        ops = [self.ops[e][-1] for e in self.ENGS if self.ops[e]]
        ops += list(self.last_dma.values())
        for e in self.ENGS:
            self.barrier_ops[e] = list(ops)

    def add(self, eng, fn, reads=(), writes=(), dma_sem=None):
        is_dma = dma_sem is not None
        deps = {}
        if is_dma and eng == "pool":
            pi = self.pool_i % len(self.pool_sems)
            self.pool_i += 1
            dma_sem = self.pool_sems[pi]
            if self.pool_prev[pi] is not None:
                deps[id(self.pool_prev[pi])] = (self.pool_prev[pi], True)
        op = Op(eng, fn, is_dma, dma_sem if is_dma else self.eng_sem[eng])
        if is_dma and eng == "pool":
            self.pool_prev[pi] = op
        for r in reads:
            w = self.last_w.get(r)
            if w is not None:
                deps[id(w)] = (w, True)
        for wr in writes:
            w = self.last_w.get(wr)
            if w is not None and id(w) not in deps:
                deps[id(w)] = (w, False)
            for rd in self.readers.get(wr, ()):
                if id(rd) not in deps:
                    deps[id(rd)] = (rd, False)
        if self.barrier_ops[eng]:
            for x in self.barrier_ops[eng]:
                deps[id(x)] = (x, True)
            self.barrier_ops[eng] = []
        for d, raw in deps.values():
            if d is op:
                continue
            if d.eng == eng and not d.is_dma and not is_dma:
                if eng == "pe":
                    continue
            d.signal = True
            op.deps.append((d, d.sem.count if d.is_dma else None))
        for r in reads:
            self.readers.setdefault(r, []).append(op)
        for wr in writes:
            self.last_w[wr] = op
            self.readers[wr] = []
        if is_dma:
            self.last_dma[id(dma_sem)] = op
            dma_sem.count += 16
            op.val = dma_sem.count
        self.ops[eng].append(op)
        return op

    def emit(self):
        nc = self.nc
        for e in self.ENGS:
            for op in self.ops[e]:
                if op.is_dma:
                    pass
                elif op.signal:
                    op.sem.count += 1
                    op.val = op.sem.count
        all_sems = self.dma_sems + self.pool_sems

        def run(e, engine):
            waited = {}
            for op in self.ops[e]:
                for d, ov in op.deps:
                    k = id(d.sem)
                    v = d.val if ov is None else ov
                    if waited.get(k, 0) >= v:
                        continue
                    waited[k] = v
                    engine.wait_ge(d.sem.h, v)
                inst = op.fn(engine)
                if op.is_dma:
                    inst.then_inc(op.sem.h, 16)
                elif op.signal:
                    inst.then_inc(op.sem.h, 1)
            if e == "sp":
                for s in all_sems:
                    if s.count > 0 and waited.get(id(s), 0) < s.count:
                        engine.wait_ge(s.h, s.count)
                for e2 in ("pe", "act", "dve", "pool"):
                    s = self.eng_sem[e2]
                    if s.count > 0:
                        engine.wait_ge(s.h, s.count)

        with nc.Block() as block:
            @block.tensor
            def _(eng):
                run("pe", eng)

            @block.scalar
            def _(eng):
                run("act", eng)

            @block.vector
            def _(eng):
                run("dve", eng)

            @block.gpsimd
            def _(eng):
                run("pool", eng)

            @block.sync
            def _(eng):
                run("sp", eng)


class Arena:
    def __init__(self, big, nelem):
        self.big = big
        self.n = nelem
        self.off = 0

    def reset(self):
        self.off = 0

    def alloc(self, shape, dtype):
        nfree = 1
        for s in shape[1:]:
            nfree *= s
        n16 = nfree * (2 if dtype == F32 else 1)
        n16 = (n16 + 15) // 16 * 16
        assert self.off + n16 <= self.n, ("arena overflow", self.off, n16, self.n)
        ap = self.big[:, self.off:self.off + nfree * (2 if dtype == F32 else 1)]
        self.off += n16
        if dtype == F32:
            ap = ap.bitcast(F32)
        if len(shape) == 3:
            ap = ap.rearrange("p (a b) -> p a b", a=shape[1], b=shape[2])
        elif len(shape) == 4:
            ap = ap.rearrange("p (a b c) -> p a b c", a=shape[1], b=shape[2], c=shape[3])
        if shape[0] < 128:
            ap = ap[0:shape[0]]
        return ap


def build_program(nrows, depth, stop_after=99, stage=99):
    NT = nrows * GW
    NG = NT // 512
    NTOT = NT + NP
    NKS = NT // 128
    NKC = PAST // 128
    NKP = NP // 128
    NKB = NKS + NKC + NKP
    NPAIR = nrows // 2

    nc = bass.Bass("TRN2", target_bir_lowering=False)

    def din(name, shape):
        return nc.dram_tensor(name, list(shape), F32, kind="ExternalInput").ap()

    def dout(name, shape):
        return nc.dram_tensor(name, list(shape), F32, kind="ExternalOutput").ap()

    def dscr(name, shape, dt):
        return nc.dram_tensor(name, list(shape), dt).ap()

    xin = din("xin", [NTOT, D])
    cvec = din("cvec", [2, D])
    cdk = din("cdk", [depth, 2, 4, PAST, 64])
    cdv = din("cdv", [depth, 4, PAST, 128])
    cnk = din("cnk", [depth, 8, PAST, 64])
    cnv = din("cnv", [depth, 8, PAST, 64])
    w_ada = din("w_ada", [depth, D, 3 * D])
    b_ada = din("b_ada", [depth, 3 * D])
    w_in = din("w_in", [depth, D, DIN])
    sg_g = din("sg_norm_g", [depth, 512])
    sg_b = din("sg_norm_b", [depth, 512])
    w_sp = din("w_spatial", [depth, 4, 128, 128])
    b_sp = din("b_spatial", [depth, 4, 128])
    lamv = din("lamv", [depth, 4, 64])
    subg = din("diff_subln_g", [depth, 128])
    tb = din("tb", [depth, 8, 9, 128, 128])
    maskc = din("maskc", [5, 128, 640])
    ropec = din("ropec", [128, NT])
    ropes = din("ropes", [128, NT])
    w_bra = din("w_br_a", [depth, 512, D])
    w_brb = din("w_br_b", [depth, 512, D])
    w_brc = din("w_br_c", [depth, 512, D])
    w_mg = din("w_mgate", [depth, D, 3 * D])
    b_mg = din("b_mgate", [depth, 3 * D])
    w_o = din("w_out", [depth, D, D])
    ln_g = din("ln_g", [depth, D])
    ln_b = din("ln_b", [depth, D])

    y_out = dout("y", [NTOT, D])
    o_dk = dout("o_dk", [NPS, depth, 2, 4, SEQ, 64])
    o_dv = dout("o_dv", [NPS, depth, 4, SEQ, 128])
    o_nk = dout("o_nk", [NPS, depth, 8, SEQ, 64])
    o_nv = dout("o_nv", [NPS, depth, 8, SEQ, 64])

    xbuf = dscr("xbuf", [NTOT, D], F32)
    hT_d = dscr("hT_d", [D, NTOT], BF16)
    yaT_d = dscr("yaT_d", [512, NTOT], BF16)
    ybT_d = dscr("ybT_d", [512, NTOT], BF16)
    ycT_d = dscr("ycT_d", [512, NTOT], BF16)
    qbT_d = dscr("qbT_d", [128, 4, NTOT], BF16)
    kbT_d = dscr("kbT_d", [128, 4, NTOT], BF16)
    qcT_d = dscr("qcT_d", [128, 4, NTOT], BF16)
    kcT_d = dscr("kcT_d", [128, 4, NTOT], BF16)
    vb_d = dscr("vb_d", [NTOT, 520], BF16)
    gb_d = dscr("gb_d", [NTOT, 512], BF16)
    vc_d = dscr("vc_d", [NTOT, 528], BF16)
    gc_d = dscr("gc_d", [NTOT, 512], BF16)
    mod_d = dscr("mod_d", [depth, 2, 3 * D], F32)
    M_d = dscr("M_d", [128, 8 * 5 * 640], BF16)

    wperm_d = dscr("wperm_d", [depth, D, 1024], F32)
    groups = [(g * 512, False) for g in range(NG)] + [(NT, True)]

    with contextlib.ExitStack() as st:
        P = Prog(nc, st)
        big = st.enter_context(nc.sbuf_tensor("arena", [128, ARENA_KB * 512], BF16))
        A = Arena(big, ARENA_KB * 512)
        ps = st.enter_context(nc.psum_tensor("ps", [128, 4096], F32))
        identf = st.enter_context(nc.sbuf_tensor("identf", [128, 128], F32))
        ident = st.enter_context(nc.sbuf_tensor("ident", [128, 128], BF16))
        protf = st.enter_context(nc.sbuf_tensor("protf", [128, 128], F32))
        prot = st.enter_context(nc.sbuf_tensor("prot", [128, 128], BF16))
        nhalf = st.enter_context(nc.sbuf_tensor("nhalf", [128, 1], F32))
        ccol = st.enter_context(nc.sbuf_tensor("ccol", [128, 2, 8], F32))
        ctmp = st.enter_context(nc.sbuf_tensor("ctmp", [128, 2, 8], F32))
        sc_col = st.enter_context(nc.sbuf_tensor("sc_col", [128, 8, 2], BF16))
        wsT = st.enter_context(nc.sbuf_tensor("wsT", [128, 4, 128], BF16))
        neglam = st.enter_context(nc.sbuf_tensor("neglam", [128, 1], F32))
        small = st.enter_context(nc.sbuf_tensor("small", [128, 64], F32))

        def bank(i, n=512):
            return ps[:, i * 512:i * 512 + n]

        def bank16(i, n=1024):
            return ps[:, i * 512:(i + 1) * 512].bitcast(BF16)[:, 0:n]

        def mm(out, lhsT, rhs, start, stop, reads, writes):
            return P.add("pe", lambda e: e.matmul(out, lhsT=lhsT, rhs=rhs, start=start, stop=stop,
                                                   skip_group_check=True), reads, writes)

        def tr(out, in_, idn, reads, writes):
            return P.add("pe", lambda e: e.transpose(out=out, in_=in_, identity=idn), reads, writes)

        def act(out, in_, func, reads, writes, scale=1.0, bias=None, accum=None):
            def f(e):
                kw = dict(out=out, in_=in_, func=func, scale=scale)
                if bias is not None:
                    kw["bias"] = bias
                if accum is not None:
                    kw["accum_out"] = accum
                return e.activation(**kw)
            return P.add("act", f, reads, writes)

        def tt(eng, out, in0, in1, op, reads, writes):
            return P.add(eng, lambda e: e.tensor_tensor(out=out, in0=in0, in1=in1, op=op), reads, writes)

        def ts(eng, out, in0, s1, s2, op0, op1, reads, writes):
            if op1 is None:
                return P.add(eng, lambda e: e.tensor_scalar(out=out, in0=in0, scalar1=s1, scalar2=None, op0=op0),
                             reads, writes)
            return P.add(eng, lambda e: e.tensor_scalar(out=out, in0=in0, scalar1=s1, scalar2=s2, op0=op0, op1=op1),
                         reads, writes)

        def stt(out, in0, scalar, in1, op0, op1, reads, writes):
            return P.add("dve", lambda e: e.scalar_tensor_tensor(out=out, in0=in0, scalar=scalar, in1=in1,
                                                                 op0=op0, op1=op1), reads, writes)

        def cp(eng, out, in_, reads, writes):
            return P.add(eng, lambda e: e.tensor_copy(out=out, in_=in_), reads, writes)

        def dma(q, out, in_, reads, writes, sem, slow=False):
            if sem is None and q != "pool":
                sem = P.new_sem()
            elif sem is None:
                sem = P.dma_sems[0]
            if slow:
                return P.add(q, lambda e: e.dma_start(out=out, in_=in_, allow_slow_non_contiguous=True),
                             reads, writes, dma_sem=sem)
            return P.add(q, lambda e: e.dma_start(out=out, in_=in_), reads, writes, dma_sem=sem)

        def rstd_pow(out, in_, reads, writes):
            return P.add("pool", lambda e: e.tensor_tensor(out=out, in0=in_, in1=nhalf[:], op=ALU.pow),
                         reads, writes)

        P.add("pool", lambda e: e.memset(identf[:], 0.0), (), ["identf"])
        P.add("pool", lambda e: e.affine_select(out=identf[:], in_=identf[:], pattern=[[-1, 128]],
                                                compare_op=ALU.not_equal, fill=1.0, base=0,
                                                channel_multiplier=1), ["identf"], ["identf"])
        cp("dve", ident[:], identf[:], ["identf"], ["ident"])
        P.add("pool", lambda e: e.memset(protf[:], 0.0), (), ["protf"])
        for blk in range(4):
            lo = blk * 32
            sl = protf[:, lo:lo + 16]
            P.add("pool", lambda e, sl=sl, lo=lo: e.affine_select(
                out=sl, in_=sl, pattern=[[-1, 16]], compare_op=ALU.not_equal, fill=-1.0,
                base=-(lo + 16), channel_multiplier=1), ["protf"], ["protf"])
            sh = protf[:, lo + 16:lo + 32]
            P.add("pool", lambda e, sh=sh, lo=lo: e.affine_select(
                out=sh, in_=sh, pattern=[[-1, 16]], compare_op=ALU.not_equal, fill=1.0,
                base=-lo, channel_multiplier=1), ["protf"], ["protf"])
        cp("dve", prot[:], protf[:], ["protf"], ["prot"])
        P.add("pool", lambda e: e.memset(nhalf[:], -0.5), (), ["nhalf"])
        s0 = P.new_sem()
        dma("sp", ccol[:], cvec.rearrange("s (c p) -> p s c", p=128), (), ["ccol"], s0, slow=True)
        act(ctmp[:], ccol[:], AF.Tanh, ["ccol"], ["ctmp"], scale=0.5)
        stt(ctmp[:], ctmp[:], 1.0, ccol[:], ALU.add, ALU.mult, ["ctmp", "ccol"], ["ctmp"])
        ts("dve", sc_col[:].rearrange("p c s -> p s c"), ctmp[:], 0.5, None, ALU.mult, None, ["ctmp"], ["sc_col"])

        for l_ in range(depth):
            for ri, off in enumerate((OFF["qb"], OFF["kb"])):
                for m in range(2):
                    dst = wperm_d[l_][:, ri * 512:(ri + 1) * 512].rearrange("k (h m d) -> k h m d", h=4, m=2, d=64)[:, :, m, :]
                    src = w_in[l_][:, off + m * 256:off + (m + 1) * 256].rearrange("k (h d) -> k h d", h=4)
                    dma("sp", dst, src, (), ["wperm%d" % l_], None)

        for l in range(depth):
            lam_init = 0.8 - 0.6 * math.exp(-0.3 * l)
            x_src = xin if l == 0 else xbuf
            x_dst = y_out if l == depth - 1 else xbuf

            P.barrier()
            A.reset()
            P.reset_sems()
            wa = [A.alloc([128, 8, 512], BF16) for _ in range(3)]
            wa_s = [P.new_sem() for _ in range(3)]
            brow = A.alloc([2, 3 * D], F32)
            mrow = A.alloc([2, 3 * D], F32)
            s1 = P.new_sem()
            dma("sp", brow, b_ada[l:l + 1, :].partition_broadcast(2), (), ["brow"], s1)
            for cb in range(6):
                k = cb % 3
                dma("pool", wa[k], w_ada[l][:, cb * 512:(cb + 1) * 512].rearrange("(c p) n -> p c n", p=128),
                    (), ["wa%d" % k], wa_s[k])
                for kc in range(8):
                    mm(bank(cb % 2)[0:2, :], sc_col[:, kc, :], wa[k][:, kc, :], kc == 0, kc == 7,
                       ["wa%d" % k, "sc_col"], ["psb%d" % (cb % 2)])
                tt("dve", mrow[:, cb * 512:(cb + 1) * 512], bank(cb % 2)[0:2, :], brow[:, cb * 512:(cb + 1) * 512],
                   ALU.add, ["psb%d" % (cb % 2), "brow"], ["mrow"])
            s2 = P.new_sem()
            dma("sp", mod_d[l], mrow, ["mrow"], ["mod_d"], s2)
            lv = A.alloc([128, 4, 64], F32)
            s3 = P.new_sem()
            dma("sp", lv, lamv[l:l + 1].partition_broadcast(128), (), ["lv"], s3)
            pr = A.alloc([128, 2, 64], F32)
            tt("dve", pr[:, 0, :], lv[:, 0, :], lv[:, 1, :], ALU.mult, ["lv"], ["pr"])
            tt("dve", pr[:, 1, :], lv[:, 2, :], lv[:, 3, :], ALU.mult, ["lv"], ["pr"])
            P.add("dve", lambda e, pr=pr: e.reduce_sum(out=small[:, 0:1], in_=pr[:, 0, :], axis=AX.X), ["pr"], ["sm0"])
            P.add("dve", lambda e, pr=pr: e.reduce_sum(out=small[:, 1:2], in_=pr[:, 1, :], axis=AX.X), ["pr"], ["sm1"])
            act(small[:, 2:4], small[:, 0:2], AF.Exp, ["sm0", "sm1"], ["sm2"])
            tt("dve", small[:, 4:5], small[:, 3:4], small[:, 2:3], ALU.subtract, ["sm2"], ["sm4"])
            ts("dve", neglam[:], small[:, 4:5], -lam_init, None, ALU.add, None, ["sm4"], ["neglam"])
            wsn = A.alloc([128, 4, 128], BF16)
            s4 = P.new_sem()
            dma("pool", wsn, w_sp[l].rearrange("g p q -> p g q"), (), ["wsn"], s4)
            for g in range(4):
                tr(bank16(2)[:, g * 128:(g + 1) * 128], wsn[:, g, :], ident[:], ["wsn", "ident"], ["psb2"])
            cp("dve", wsT[:].rearrange("p g q -> p (g q)"), bank16(2)[:, 0:512], ["psb2"], ["wsT"])
            tbt = A.alloc([128, 8, 9 * 128], F32)
            mk = A.alloc([128, 5, 640], F32)
            mst = A.alloc([128, 8 * 5, 640], BF16)
            s5, s6, s7 = P.new_sem(), P.new_sem(), P.new_sem()
            for h in range(8):
                dma("sp", tbt[:, h, :].rearrange("p (o q) -> p o q", o=9), tb[l, h].rearrange("o k q -> k o q"),
                    (), ["tbt%d" % h], None)
            dma("sp", mk, maskc.rearrange("v k c -> k v c"), (), ["mk"], s6)
            OB = [0, 4, 3, 1, 0]
            for h in range(8):
                for v in range(5):
                    tt("pool" if (h * 5 + v) % 2 else "dve", mst[:, h * 5 + v, :], tbt[:, h, OB[v] * 128:OB[v] * 128 + 640],
                       mk[:, v, :], ALU.add, ["tbt%d" % h, "mk"], ["mst%d" % (h * 5 + v)])
            dma("sp", M_d, mst[:].rearrange("p a b -> p (a b)"), ["mst%d" % i_ for i_ in range(40)], ["M_d"], s7)

            if stop_after < 1:
                break
            P.barrier()
            A.reset()
            P.reset_sems()
            win = A.alloc([128, 8, DIN], BF16)
            wsrc = w_in[l]
            for (a, b, nm) in ((0, 1536, "win_a"), (2560, DIN, "win_b")):
                dma("pool", win[:, :, a:b], wsrc[:, a:b].rearrange("(c p) n -> p c n", p=128), (), [nm], None)
            dma("pool", win[:, :, 1536:2560], wperm_d[l].rearrange("(c p) n -> p c n", p=128), ["wperm%d" % l], ["win_q"], None)
            sgg = A.alloc([128, 512], F32)
            sgb = A.alloc([128, 512], F32)
            brep = A.alloc([128, 4, 128], F32)
            modc = A.alloc([128, 2, 2, 8], F32)
            dma("sp", sgg, sg_g[l:l + 1, :].partition_broadcast(128), (), ["sgg"], None)
            dma("sp", sgb, sg_b[l:l + 1, :].partition_broadcast(128), (), ["sgb"], None)
            dma("sp", brep, b_sp[l:l + 1].partition_broadcast(128), (), ["brep"], None)
            for cs in range(2):
                for k in range(2):
                    dma("sp", modc[:, cs, k, :], mod_d[l, cs, k * D:(k + 1) * D].rearrange("(c p) -> p c", p=128),
                        ["mod_d"], ["modc"], None, slow=True)
            ts("dve", modc[:, :, 1, :], modc[:, :, 1, :], 1.0, None, ALU.add, None, ["modc"], ["modc"])

            xs = [A.alloc([128, D], F32) for _ in range(2)]
            xs_s = [P.new_sem() for _ in range(2)]
            xn = [A.alloc([128, D], BF16) for _ in range(2)]
            hT = [A.alloc([128, 8, 512], BF16) for _ in range(2)]
            hT_s = [P.new_sem() for _ in range(2)]
            rc = A.alloc([128, 512], F32)
            rs = A.alloc([128, 512], F32)
            rope_s = P.new_sem()
            fm = {n: A.alloc([128, 4, 512], BF16) for n in ("qb", "kb", "ya")}
            fm_s = {n: P.new_sem() for n in fm}
            fm["qc"], fm["kc"] = fm["qb"], fm["kb"]
            fm_s["qc"], fm_s["kc"] = fm_s["qb"], fm_s["kb"]
            fmr = dict(qb="fm_qb", kb="fm_kb", qc="fm_qb", kc="fm_kb", ya="fm_ya")
            tm = [[A.alloc([128, w_], BF16) for w_ in (520, 512, 528, 512)] for _ in range(2)]
            for k_ in range(2):
                P.add("pool", lambda e, a=tm[k_][0]: e.memset(a, 1.0), (), ["tm%d" % k_])
                P.add("pool", lambda e, a=tm[k_][2]: e.memset(a, 1.0), (), ["tm%d" % k_])
            tm_s = [P.new_sem() for _ in range(2)]
            of32 = [A.alloc([128, 512], F32)] * 2
            of_s = [P.new_sem()] * 2
            vn = A.alloc([128, 4, 512], BF16)
            tA = [A.alloc([128, 512], F32) for _ in range(2)]
            tB = [A.alloc([128, 512], F32) for _ in range(2)]
            tC = [A.alloc([128, 512], F32) for _ in range(2)]
            qraw = [A.alloc([128, 512], BF16) for _ in range(2)]
            st6 = A.alloc([128, 2, 6], F32)
            mv = A.alloc([128, 8], F32)

            cnt = dict(x=0, bank=0, t=0, q=0, of=0, tm=0)

            def nbank():
                b = 2 + cnt["bank"] % 6
                cnt["bank"] += 1
                return b

            def layer_norm_stats(src_lo, src_hi, rd, tag):
                P.add("dve", lambda e, o=st6[:, 0, :]: e.bn_stats(out=o, in_=src_lo), rd, ["st6"])
                P.add("dve", lambda e, o=st6[:, 1, :]: e.bn_stats(out=o, in_=src_hi), rd, ["st6"])
                P.add("dve", lambda e, o=mv[:, 0:2], i=st6[:].rearrange("p a b -> p (a b)"): e.bn_aggr(out=o, in_=i),
                      ["st6"], ["mv01"])
                ts("dve", mv[:, 2:3], mv[:, 1:2], EPS, None, ALU.add, None, ["mv01"], ["mv2"])
                rstd_pow(mv[:, 3:4], mv[:, 2:3], ["mv2", "nhalf"], ["mv3"])

            flat = [(gi_, sti_) for gi_ in range(len(groups)) for sti_ in range(4)]

            def p1_xload(fi):
                gi_, sti_ = flat[fi]
                k_ = fi % 2
                t0_ = groups[gi_][0] + sti_ * 128
                dma("sp", xs[k_], x_src[t0_:t0_ + 128, :], ["x_d%d" % gi_], ["xs%d" % k_], xs_s[k_])

            def p1_ln(gi_):
                cs_ = 1 if groups[gi_][1] else 0
                tok0_ = groups[gi_][0]
                hk_ = gi_ % 2
                hres_ = "hT%d" % hk_
                for sti_ in range(4):
                    fi = gi_ * 4 + sti_
                    k = fi % 2
                    layer_norm_stats(xs[k][:, 0:512], xs[k][:, 512:1024], ["xs%d" % k], "x")
                    ts("dve", xn[k], xs[k], mv[:, 0:1], mv[:, 3:4], ALU.subtract, ALU.mult,
                       ["xs%d" % k, "mv01", "mv3"], ["xn%d" % k])
                    if fi + 2 < len(flat):
                        p1_xload(fi + 2)
                    tb_ = sti_ % 2
                    for c in range(8):
                        tr(bank16(tb_)[:, c * 128:(c + 1) * 128], xn[k][:, c * 128:(c + 1) * 128], ident[:],
                           ["xn%d" % k, "ident"], ["psb%d" % tb_])
                    for c in range(8):
                        act(hT[hk_][:, c, sti_ * 128:(sti_ + 1) * 128], bank16(tb_)[:, c * 128:(c + 1) * 128], AF.Identity,
                            ["psb%d" % tb_, "modc"], [hres_], scale=modc[:, cs_, 1, c:c + 1], bias=modc[:, cs_, 0, c:c + 1])
                dma("sp", hT_d[:, tok0_:tok0_ + 512].rearrange("(c p) t -> p c t", p=128), hT[hk_],
                    [hres_], ["hT_d%d" % gi_], hT_s[hk_])

            p1_xload(0)
            p1_xload(1)
            p1_ln(0)
            for gi, (tok0, is_p) in enumerate(groups):
                cs = 1 if is_p else 0
                hk = gi % 2
                hres = "hT%d" % hk
                if not is_p:
                    dma("sp", rc, ropec[:, tok0:tok0 + 512], (), ["rc"], rope_s)
                    dma("sp", rs, ropes[:, tok0:tok0 + 512], (), ["rs"], rope_s)

                def win_res(col0, kc):
                    if col0 < 1536:
                        return ["win_a"]
                    if col0 >= 2560:
                        return ["win_b"]
                    return ["win_q"]

                def proj_fm(col0, b):
                    for kc in range(8):
                        mm(bank(b), win[:, kc, col0:col0 + 128], hT[hk][:, kc, :], kc == 0, kc == 7,
                           win_res(col0, kc) + [hres], ["psb%d" % b])

                def proj_tm(col0, sti, b):
                    for kc in range(8):
                        mm(bank(b), hT[hk][:, kc, sti * 128:(sti + 1) * 128], win[:, kc, col0:col0 + 512],
                           kc == 0, kc == 7, win_res(col0, kc) + [hres], ["psb%d" % b])

                for sti in range(4):
                    b = nbank()
                    proj_tm(OFF["va"], sti, b)
                    pb = "psb%d" % b
                    layer_norm_stats(bank(b)[:, 0:256], bank(b)[:, 256:512], [pb], "v")
                    k = cnt["t"] % 2
                    cnt["t"] += 1
                    ts("dve", tA[k], bank(b), mv[:, 0:1], mv[:, 3:4], ALU.subtract, ALU.mult,
                       [pb, "mv01", "mv3"], ["tA%d" % k])
                    tt("pool", tA[k], tA[k], sgg, ALU.mult, ["tA%d" % k, "sgg"], ["tA%d" % k])
                    tt("pool", vn[:, sti, :], tA[k], sgb, ALU.add, ["tA%d" % k, "sgb"], ["vn"])
                for g in range(4):
                    k = cnt["t"] % 2
                    cnt["t"] += 1
                    bu = nbank()
                    proj_fm(OFF["ua"] + g * 128, bu)
                    act(tA[k], bank(bu), AF.Copy, ["psb%d" % bu], ["tA%d" % k], scale=0.5)
                    bg = nbank()
                    proj_fm(OFF["ga"] + g * 128, bg)
                    act(tB[k], bank(bg), AF.Tanh, ["psb%d" % bg], ["tB%d" % k], scale=0.5)
                    stt(tB[k], tB[k], 1.0, bank(bg), ALU.add, ALU.mult, ["tB%d" % k, "psb%d" % bg], ["tB%d" % k])
                    bs_ = nbank()
                    for n in range(4):
                        mm(bank(bs_)[:, n * 128:(n + 1) * 128], vn[:, n, g * 128:(g + 1) * 128], wsT[:, g, :], True, True,
                           ["vn", "wsT"], ["psb%d" % bs_])
                    for n in range(4):
                        tt("dve", tC[k][:, n * 128:(n + 1) * 128], bank(bs_)[:, n * 128:(n + 1) * 128], brep[:, g, :],
                           ALU.add, ["psb%d" % bs_, "brep"], ["tC%d" % k])
                    tt("pool", tA[k], tA[k], tB[k], ALU.mult, ["tA%d" % k, "tB%d" % k], ["tA%d" % k])
                    tt("dve", fm["ya"][:, g, :], tA[k], tC[k], ALU.mult, ["tA%d" % k, "tC%d" % k], ["fm_ya"])
                dma("sp", yaT_d[:, tok0:tok0 + 512].rearrange("(c p) t -> p c t", p=128), fm["ya"],
                    ["fm_ya"], ["yaT_d%d" % gi], fm_s["ya"])

                if gi + 1 < len(groups):
                    p1_ln(gi + 1)
                for name in ("qb", "kb"):
                    for h in range(4):
                        b = nbank()
                        proj_fm(OFF[name] + h * 128, b)
                        pb = "psb%d" % b
                        if is_p:
                            act(fm[name][:, h, :], bank(b), AF.Copy, [pb], [fmr[name]])
                        else:
                            k = cnt["q"] % 2
                            cnt["q"] += 1
                            act(qraw[k], bank(b), AF.Copy, [pb], ["qraw%d" % k])
                            b2 = nbank()
                            mm(bank(b2), prot[:], qraw[k], True, True, ["prot", "qraw%d" % k], ["psb%d" % b2])
                            act(tA[k], bank(b), AF.Copy, [pb], ["tA%d" % k])
                            act(tB[k], bank(b2), AF.Copy, ["psb%d" % b2], ["tB%d" % k])
                            tt("dve", tA[k], tA[k], rc, ALU.mult, ["tA%d" % k, "rc"], ["tA%d" % k])
                            tt("pool", tB[k], tB[k], rs, ALU.mult, ["tB%d" % k, "rs"], ["tB%d" % k])
                            tt("pool", fm[name][:, h, :], tA[k], tB[k], ALU.add, ["tA%d" % k, "tB%d" % k], [fmr[name]])
                    dst = (qbT_d if name == "qb" else kbT_d)[:, :, tok0:tok0 + 512]
                    dma("sp", dst, fm[name], [fmr[name]], ["%sT_d%d" % (name, gi)], fm_s[name])
                for name in ("qc", "kc"):
                    for hp in range(4):
                        b = nbank()
                        proj_fm(OFF[name] + hp * 128, b)
                        act(fm[name][:, hp, :], bank(b), AF.Copy, ["psb%d" % b], [fmr[name]])
                    dst = (qcT_d if name == "qc" else kcT_d)[:, :, tok0:tok0 + 512]
                    dma("sp", dst, fm[name], [fmr[name]], ["%sT_d%d" % (name, gi)], fm_s[name])

                for sti in range(4):
                    k = cnt["tm"] % 2
                    cnt["tm"] += 1
                    tmr = "tm%d" % k
                    t0 = tok0 + sti * 128
                    sq, tq = sti // 2, (sti % 2) * 128

                    def out_f32(b, dst_ap, src_view):
                        ko = 0
                        act(of32[ko], bank(b), AF.Copy, ["psb%d" % b], ["of%d" % ko])
                        dma("sp", dst_ap, src_view(of32[ko]), ["of%d" % ko], ["outs"], of_s[ko])

                    for ai, name in enumerate(("vb", "gb", "vc", "gc")):
                        b = nbank()
                        proj_tm(OFF[name], sti, b)
                        pb = "psb%d" % b
                        if name in ("vb", "vc"):
                            if name == "vb":
                                act(tm[k][0].rearrange("p (h v) -> p h v", h=4)[:, :, 0:128],
                                    bank(b).rearrange("p (h v) -> p h v", h=4), AF.Copy, [pb], [tmr])
                            else:
                                act(tm[k][2].rearrange("p (h v) -> p h v", h=8)[:, :, 0:64],
                                    bank(b).rearrange("p (h v) -> p h v", h=8), AF.Copy, [pb], [tmr])
                            if is_p:
                                if name == "vb":
                                    out_f32(b, o_dv[sq, l, :, tq:tq + 128, :].rearrange("h t v -> t h v"),
                                            lambda a: a.rearrange("p (h v) -> p h v", h=4))
                                else:
                                    out_f32(b, o_nv[sq, l, :, tq:tq + 128, :].rearrange("h t v -> t h v"),
                                            lambda a: a.rearrange("p (h v) -> p h v", h=8))
                        else:
                            kk = cnt["t"] % 2
                            cnt["t"] += 1
                            act(tC[kk], bank(b), AF.Tanh, [pb], ["tC%d" % kk], scale=0.5)
                            stt(tm[k][ai], tC[kk], 1.0, bank(b), ALU.add, ALU.mult, ["tC%d" % kk, pb], [tmr])
                    if is_p:
                        b = nbank()
                        proj_tm(OFF["kb"], sti, b)
                        act(of32[0], bank(b), AF.Copy, ["psb%d" % b], ["of0"])
                        for m_ in range(2):
                            dma("sp", o_dk[sq, l, m_, :, tq:tq + 128, :].rearrange("h t d -> t h d"),
                                of32[0].rearrange("p (h m d) -> p h m d", h=4, m=2)[:, :, m_, :], ["of0"], ["outs"], of_s[0])
                        b = nbank()
                        proj_tm(OFF["kc"], sti, b)
                        out_f32(b, o_nk[sq, l, :, tq:tq + 128, :].rearrange("h t v -> t h v"),
                                lambda a: a.rearrange("p (h v) -> p h v", h=8))
                    for ai, dd in enumerate((vb_d, gb_d, vc_d, gc_d)):
                        dma("sp", dd[t0:t0 + 128, :], tm[k][ai], [tmr], ["tmd%d_%d" % (ai, gi)], tm_s[k])

            if l == 0:
                print("arena P1 KB", A.off / 512.0)
            if stop_after < 2:
                break
            P.barrier()
            A.reset()
            P.reset_sems()
            KbT = A.alloc([128, 4, NKB * 128], BF16)
            Vb = A.alloc([128, NKB, 4 * 130], BF16)
            gsub = A.alloc([128, 128], F32)
            P.add("pool", lambda e, a=Vb[:, NKS:NKS + NKC, :]: e.memset(a, 1.0), (), ["Vb_c"])
            dma("sp", KbT[:, :, 0:NT], kbT_d[:, :, 0:NT], ["kbT_d%d" % g for g in range(NG)], ["KbT_s"], None)
            dma("sp", KbT[:, :, NT + PAST:NT + PAST + NP], kbT_d[:, :, NT:NTOT], ["kbT_d%d" % NG], ["KbT_p"], None)
            Vb4 = Vb.rearrange("p k (h v) -> p k h v", h=4)
            for k0 in range(0, NKS, 8):
                dma("sp", Vb[:, k0:k0 + 8, :], vb_d[k0 * 128:(k0 + 8) * 128, :].rearrange("(k p) f -> p k f", p=128),
                    ["tmd0_%d" % g for g in range(NG)], ["Vb_s%d" % (k0 // 8)], None)
            dma("sp", Vb[:, NKS + NKC:NKB, :], vb_d[NT:NTOT, :].rearrange("(k p) f -> p k f", p=128),
                ["tmd0_%d" % NG], ["Vb_p"], None)
            dma("sp", gsub, subg[l:l + 1, :].partition_broadcast(128), (), ["gsub"], None)
            ts("dve", gsub, gsub, (1.0 - lam_init) * 0.5, None, ALU.mult, None, ["gsub"], ["gsub"])
            for h_ in range(4):
                dma("pool", Vb4[:, NKS:NKS + NKC, h_, 0:128], cdv[l, h_].rearrange("(k p) v -> p k v", p=128),
                    (), ["Vb_c"], None)
            ckt = A.alloc([128, NKC, 4 * 128], BF16)
            ckt5 = ckt.rearrange("p k (h m d) -> p k h m d", h=4, m=2)
            for m in range(2):
                for h in range(4):
                    dma("pool", ckt5[:, :, h, m, :], cdk[l, m, h].rearrange("(k p) d -> p k d", p=128),
                        (), ["ckt"], None)
            for kb in range(NKC):
                for h in range(4):
                    tr(bank16(7)[:, h * 128:(h + 1) * 128], ckt[:, kb, h * 128:(h + 1) * 128], ident[:],
                       ["ckt", "ident"], ["psb7"])
                for h in range(4):
                    cp("dve", KbT[:, h, NT + kb * 128:NT + (kb + 1) * 128], bank16(7)[:, h * 128:(h + 1) * 128],
                       ["psb7"], ["KbT_c"])

            qt = [A.alloc([128, 4, 512], BF16) for _ in range(2)]
            qt_s = [P.new_sem() for _ in range(2)]
            gbt = [A.alloc([128, 4, 512], BF16) for _ in range(2)]
            pT2 = [A.alloc([128, 2, 512], BF16) for _ in range(3)]
            pT = [[pT2[i_][:, m_, :] for i_ in range(3)] for m_ in range(2)]
            ybt2 = [A.alloc([128, 4, 512], BF16) for _ in range(2)]
            ybT_st = A.alloc([128, 4, 512], BF16)
            ybT_s = P.new_sem()
            otmp = [A.alloc([128, 128], F32) for _ in range(4)]
            ofin = [A.alloc([128, 128], F32) for _ in range(4)]
            junk = A.alloc([128, 128], F32)
            sv_ = [A.alloc([128, 8], F32) for _ in range(4)]

            def kres(kb):
                return "KbT_s" if kb < NKS else ("KbT_c" if kb < NKS + NKC else "KbT_p")

            def vres(kb):
                return ("Vb_s%d" % (kb // 8)) if kb < NKS else ("Vb_c" if kb < NKS + NKC else "Vb_p")

            def acc_ap(i, n):
                return ps[:, (4 + i // 3) * 512 + (i % 3) * 130:(4 + i // 3) * 512 + (i % 3) * 130 + n]

            qtiles = [(g * 512, 512, list(range(NKS + NKC)), g) for g in range(NG)]
            for s in range(NPS):
                qtiles.append((NT + s * SEQ, SEQ, [NKS + NKC + 2 * s, NKS + NKC + 2 * s + 1], NG))
            cntp = dict(s=0, p=0, f=0)
            def p2a_loads(ti_):
                tok0_, N_, _, gi_ = qtiles[ti_]
                k_ = ti_ % 2
                dma("sp", qt[k_][:, :, 0:N_], qbT_d[:, :, tok0_:tok0_ + N_], ["qbT_d%d" % gi_], ["qt%d" % k_], qt_s[k_])
                dma("sp", gbt[k_][:, 0:N_ // 128, :], gb_d[tok0_:tok0_ + N_, :].rearrange("(s p) f -> p s f", p=128),
                    ["tmd1_%d" % gi_], ["gbt%d" % k_], qt_s[k_])

            def p2a_tail(ti_):
                tok0_, N_, _, gi_ = qtiles[ti_]
                yb_, ybr_ = ybt2[ti_ % 2], "ybt%d" % (ti_ % 2)
                for hc in range(4):
                    for qs in range(N_ // 128):
                        tr(bank16(7)[:, qs * 128:(qs + 1) * 128], yb_[:, qs, hc * 128:(hc + 1) * 128], ident[:],
                           [ybr_, "ident"], ["psb7"])
                    cp("dve", ybT_st[:, hc, 0:N_], bank16(7)[:, 0:N_], ["psb7"], ["ybT_st"])
                dma("sp", ybT_d[:, tok0_:tok0_ + N_].rearrange("(c p) t -> p c t", p=128), ybT_st[:, :, 0:N_],
                    ["ybT_st"], ["ybT_d%d" % gi_], ybT_s)

            p2a_loads(0)
            for ti, (tok0, N, kblocks, gi) in enumerate(qtiles):
                k = ti % 2
                nsub = N // 128
                ybt, ybr = ybt2[ti % 2], "ybt%d" % (ti % 2)
                if ti + 1 < len(qtiles):
                    p2a_loads(ti + 1)
                items = [(h, kbi, kb) for h in range(4) for kbi, kb in enumerate(kblocks)]
                nlast = len(kblocks) - 1
                started = set()

                def qk_exp(idx):
                    h, kbi, kb = items[idx]
                    sp_ = (cntp["s"] + idx) % 2
                    pk = (cntp["p"] + idx) % 3
                    for m in range(2):
                        r0 = m * 64
                        b = sp_ * 2 + m
                        mm(bank(b)[:, 0:N], KbT[r0:r0 + 64, h, kb * 128:(kb + 1) * 128], qt[k][r0:r0 + 64, h, 0:N],
                           True, True, [kres(kb), "qt%d" % k], ["psb%d" % b])
                    for m in range(2):
                        b = sp_ * 2 + m
                        act(pT[m][pk][:, 0:N], bank(b)[:, 0:N], AF.Exp, ["psb%d" % b], ["pT%d_%d" % (m, pk)], scale=0.125)

                def finalize(h):
                    for qs in range(nsub):
                        f = qs
                        i0, i1 = qs, 4 + qs
                        rd = ["psb%d" % (4 + i0 // 3), "psb%d" % (4 + i1 // 3)]
                        s = sv_[f]
                        sr = "sv%d" % f
                        P.add("dve", lambda e, s=s, a=acc_ap(i0, 130): e.reciprocal(out=s[:, 0:1], in_=a[:, 128:129]), rd, [sr + "a"])
                        P.add("dve", lambda e, s=s, a=acc_ap(i1, 130): e.reciprocal(out=s[:, 1:2], in_=a[:, 128:129]), rd, [sr + "b"])
                        tt("dve", s[:, 2:3], s[:, 1:2], neglam[:], ALU.mult, [sr + "b", "neglam"], [sr + "c"])
                        ts("dve", otmp[f], acc_ap(i1, 128), s[:, 2:3], None, ALU.mult, None, rd + [sr + "c"], ["otmp%d" % f])
                        stt(ofin[f], acc_ap(i0, 128), s[:, 0:1], otmp[f], ALU.mult, ALU.add, rd + [sr + "a", "otmp%d" % f],
                            ["ofin%d" % f])
                    for qs in range(nsub):
                        f = qs
                        s = sv_[f]
                        sr = "sv%d" % f
                        act(junk, ofin[f], AF.Square, ["ofin%d" % f], ["junk", sr + "d"], accum=s[:, 3:4])
                        ts("dve", s[:, 4:5], s[:, 3:4], 1.0 / 128.0, EPS, ALU.mult, ALU.add, [sr + "d"], [sr + "e"])
                        rstd_pow(s[:, 5:6], s[:, 4:5], [sr + "e", "nhalf"], [sr + "f"])
                        stt(ofin[f], ofin[f], s[:, 5:6], gsub, ALU.mult, ALU.mult, ["ofin%d" % f, sr + "f", "gsub"], ["ofin%d" % f])
                        tt("pool", ybt[:, qs, h * 128:(h + 1) * 128], ofin[f], gbt[k][:, qs, h * 128:(h + 1) * 128], ALU.mult,
                           ["ofin%d" % f, "gbt%d" % k], [ybr])

                def pv(idx):
                    h, kbi, kb = items[idx]
                    pk = (cntp["p"] + idx) % 3
                    if kbi == 0:
                        started.clear()
                    for m in range(2):
                        for qs in range(nsub):
                            i = m * 4 + qs
                            bk = 4 + i // 3
                            first = bk not in started
                            started.add(bk)
                            mm(acc_ap(i, 130), pT[m][pk][:, qs * 128:(qs + 1) * 128], Vb[:, kb, h * 130:(h + 1) * 130],
                               first, kbi == nlast, ["pT%d_%d" % (m, pk), vres(kb)], ["psb%d" % bk])
                    if kbi == nlast:
                        finalize(h)

                qk_exp(0)
                for idx in range(len(items)):
                    if idx + 1 < len(items):
                        qk_exp(idx + 1)
                    pv(idx)
                    if idx == min(5, len(items) - 1) and ti > 0:
                        p2a_tail(ti - 1)
                cntp["s"] += len(items)
                cntp["p"] += len(items)
            p2a_tail(len(qtiles) - 1)

            if l == 0:
                print("arena P2a KB", A.off / 512.0)
            if stop_after < 3:
                break
            P.barrier()
            A.reset()
            P.reset_sems()
            Mt = A.alloc([128, 40, 640], BF16)
            dma("sp", Mt[:].rearrange("p a b -> p (a b)"), M_d, ["M_d"], ["Mt"], None)
            kctx = A.alloc([128, 4, PAST + NP], BF16)
            vctx = A.alloc([128, NKC + NKP, 8 * 66], BF16)
            vctx4 = vctx.rearrange("p k (h v) -> p k h v", h=8)
            P.add("pool", lambda e, a=vctx: e.memset(a, 1.0), (), ["vctx"])
            dma("sp", kctx[:, :, PAST:PAST + NP], kcT_d[:, :, NT:NTOT], ["kcT_d%d" % NG], ["kctx"], None)
            dma("sp", vctx[:, NKC:NKC + NKP, :], vc_d[NT:NTOT, :].rearrange("(k p) f -> p k f", p=128),
                ["tmd2_%d" % NG], ["vctx"], None)
            for h_ in range(8):
                dma("pool", vctx4[:, 0:NKC, h_, 0:64], cnv[l, h_].rearrange("(k p) v -> p k v", p=128), (), ["vctx"], None)
            cnt_ = A.alloc([128, NKC, 8 * 64], BF16)
            for h_ in range(8):
                dma("pool", cnt_.rearrange("p k (h d) -> p k h d", h=8)[:, :, h_, :], cnk[l, h_].rearrange("(k p) d -> p k d", p=128),
                    (), ["cnt"], None)
            for kb in range(NKC):
                for hp in range(4):
                    tr(bank16(7)[:, hp * 128:(hp + 1) * 128], cnt_[:, kb, hp * 128:(hp + 1) * 128], ident[:],
                       ["cnt", "ident"], ["psb7"])
                for hp in range(4):
                    cp("dve", kctx[:, hp, kb * 128:(kb + 1) * 128], bank16(7)[:, hp * 128:(hp + 1) * 128],
                       ["psb7"], ["kctx"])
            kw = [A.alloc([128, 4, 1024], BF16) for _ in range(2)]
            vw = [A.alloc([128, 8, 8 * 66], BF16) for _ in range(2)]
            qct = [A.alloc([128, 4, 512], BF16) for _ in range(2)]
            gct = [A.alloc([128, 4, 512], BF16) for _ in range(2)]
            w_s = [P.new_sem() for _ in range(2)]
            pc = [A.alloc([128, 512], BF16) for _ in range(4)]
            pl = [A.alloc([128, 640], BF16) for _ in range(3)]
            slf = [A.alloc([128, 640], F32) for _ in range(3)]
            yct2 = [A.alloc([128, 4, 512], BF16) for _ in range(2)]
            ycT_st = A.alloc([128, 4, 512], BF16)
            ycT_s = P.new_sem()
            rv = [A.alloc([128, 4], F32) for _ in range(2)]
            cq = dict(s=0, p=0, l=0, f=0)

            def accc(hh, qs, n):
                return ps[:, (4 + hh) * 512 + qs * 66:(4 + hh) * 512 + qs * 66 + n]

            ntiles = NG + NPS

            def p2b_geom(ti_):
                if ti_ < NG:
                    g_ = ti_
                    wb0_ = max(0, 4 * g_ - 2)
                    wb1_ = min(NKS - 1, 4 * g_ + 5)
                    return g_ * 512, 512, g_, wb0_, wb1_, list(range(NKC))
                s_ = ti_ - NG
                return NT + s_ * SEQ, SEQ, NG, 0, 0, [NKC + 2 * s_, NKC + 2 * s_ + 1]

            def p2b_loads(ti_):
                tok0_, N_, gi_, wb0_, wb1_, _ = p2b_geom(ti_)
                k_ = ti_ % 2
                if ti_ < NG:
                    nwb_ = wb1_ - wb0_ + 1
                    dma("sp", kw[k_][:, :, 0:nwb_ * 128], kcT_d[:, :, wb0_ * 128:(wb1_ + 1) * 128],
                        ["kcT_d%d" % x for x in range(NG)], ["kw%d" % k_], w_s[k_])
                    dma("sp", vw[k_][:, 0:nwb_, :],
                        vc_d[wb0_ * 128:(wb1_ + 1) * 128, :].rearrange("(k p) f -> p k f", p=128),
                        ["tmd2_%d" % x for x in range(NG)], ["vw%d" % k_], w_s[k_])
                dma("sp", qct[k_][:, :, 0:N_], qcT_d[:, :, tok0_:tok0_ + N_], ["qcT_d%d" % gi_], ["qct%d" % k_], w_s[k_])
                dma("sp", gct[k_][:, 0:N_ // 128, :], gc_d[tok0_:tok0_ + N_, :].rearrange("(s p) f -> p s f", p=128),
                    ["tmd3_%d" % gi_], ["gct%d" % k_], w_s[k_])

            def p2b_tail(ti_):
                tok0_, N_, gi_, _, _, _ = p2b_geom(ti_)
                yc_, ycr_ = yct2[ti_ % 2], "yct%d" % (ti_ % 2)
                for hc in range(4):
                    for qs in range(N_ // 128):
                        tr(bank16(7)[:, qs * 128:(qs + 1) * 128], yc_[:, qs, hc * 128:(hc + 1) * 128], ident[:],
                           [ycr_, "ident"], ["psb7"])
                    cp("dve", ycT_st[:, hc, 0:N_], bank16(7)[:, 0:N_], ["psb7"], ["ycT_st"])
                dma("sp", ycT_d[:, tok0_:tok0_ + N_].rearrange("(c p) t -> p c t", p=128), ycT_st[:, :, 0:N_],
                    ["ycT_st"], ["ycT_d%d" % gi_], ycT_s)

            p2b_loads(0)
            for ti in range(ntiles):
                k = ti % 2
                is_p = ti >= NG
                g = ti
                tok0, N, gi, wb0, wb1, cblocks = p2b_geom(ti)
                nsub = N // 128
                yct, ycr = yct2[ti % 2], "yct%d" % (ti % 2)
                if ti + 1 < ntiles:
                    p2b_loads(ti + 1)
                items = []
                for hp in range(4):
                    for hh in range(2):
                        hitems = [("c", hp, hh, ci, cb) for ci, cb in enumerate(cblocks)]
                        if not is_p:
                            hitems += [("l", hp, hh, qs, None) for qs in range(4)]
                        for ii, it in enumerate(hitems):
                            items.append(it + (ii == 0, ii == len(hitems) - 1))

                def local_geom(qs):
                    j = 4 * g + qs
                    jb = min(max(j - 2, 0), NPAIR - 5)
                    if j == 0:
                        v = 1
                    elif j == 1:
                        v = 2
                    elif j == NPAIR - 2:
                        v = 3
                    elif j == NPAIR - 1:
                        v = 4
                    else:
                        v = 0
                    return jb, v

                def front(idx):
                    kind, hp, hh, a1, a2, isf, isl = items[idx]
                    head = 2 * hp + hh
                    r0 = 64 * hh
                    slot = (cq["s"] + idx) % 3
                    b0 = (0, 2, 6)[slot]
                    if kind == "c":
                        cb = a2
                        pk = (cq["p"] + idx) % 4
                        mm(bank(b0)[:, 0:N], kctx[r0:r0 + 64, hp, cb * 128:(cb + 1) * 128], qct[k][r0:r0 + 64, hp, 0:N],
                           True, True, ["kctx", "qct%d" % k], ["psb%d" % b0])
                        act(pc[pk][:, 0:N], bank(b0)[:, 0:N], AF.Exp, ["psb%d" % b0], ["pc%d" % pk], scale=0.125)
                    else:
                        qs = a1
                        jb, v = local_geom(qs)
                        lk = (cq["l"] + idx) % 3
                        for blk in range(5):
                            wi = jb + blk - wb0
                            mm(ps[:, b0 * 512 + blk * 128:b0 * 512 + (blk + 1) * 128],
                               kw[k][r0:r0 + 64, hp, wi * 128:(wi + 1) * 128], qct[k][r0:r0 + 64, hp, qs * 128:(qs + 1) * 128],
                               True, True, ["kw%d" % k, "qct%d" % k], ["psb%d" % b0, "psb%d" % (b0 + 1)])
                        stt(slf[lk], ps[:, b0 * 512:b0 * 512 + 640], 0.125, Mt[:, head * 5 + v, :], ALU.mult, ALU.add,
                            ["psb%d" % b0, "psb%d" % (b0 + 1), "Mt"], ["slf%d" % lk])
                        act(pl[lk], slf[lk], AF.Exp, ["slf%d" % lk], ["pl%d" % lk])

                def back(idx):
                    kind, hp, hh, a1, a2, isf, isl = items[idx]
                    head = 2 * hp + hh
                    accr = "psb%d" % (4 + hh)
                    if kind == "c":
                        cb = a2
                        pk = (cq["p"] + idx) % 4
                        for qs in range(nsub):
                            mm(accc(hh, qs, 66), pc[pk][:, qs * 128:(qs + 1) * 128], vctx[:, cb, head * 66:(head + 1) * 66],
                               isf and qs == 0, isl, ["pc%d" % pk, "vctx"], [accr])
                    else:
                        qs = a1
                        jb, v = local_geom(qs)
                        lk = (cq["l"] + idx) % 3
                        for blk in range(5):
                            wi = jb + blk - wb0
                            mm(accc(hh, qs, 66), pl[lk][:, blk * 128:(blk + 1) * 128], vw[k][:, wi, head * 66:(head + 1) * 66],
                               False, blk == 4, ["pl%d" % lk, "vw%d" % k], [accr])
                    if isl:
                        for qs in range(nsub):
                            f = cq["f"] % 2
                            cq["f"] += 1
                            ts("dve", rv[f][:, 0:1], accc(hh, qs, 66)[:, 64:65], 2.0, None, ALU.mult, None, [accr], ["rv%da" % f])
                            P.add("dve", lambda e, r=rv[f]: e.reciprocal(out=r[:, 1:2], in_=r[:, 0:1]), ["rv%da" % f], ["rv%db" % f])
                            stt(yct[:, qs, head * 64:(head + 1) * 64], accc(hh, qs, 64), rv[f][:, 1:2],
                                gct[k][:, qs, head * 64:(head + 1) * 64], ALU.mult, ALU.mult, [accr, "rv%db" % f, "gct%d" % k], [ycr])

                front(0)
                if len(items) > 1:
                    front(1)
                for idx in range(len(items)):
                    if idx + 2 < len(items):
                        front(idx + 2)
                    back(idx)
                    if idx == min(5, len(items) - 1) and ti > 0:
                        p2b_tail(ti - 1)
                cq["s"] += len(items)
                cq["p"] += len(items)
                cq["l"] += len(items)
            p2b_tail(ntiles - 1)

            if l == 0:
                print("arena P2b KB", A.off / 512.0)
            if stop_after < 4:
                break
            P.barrier()
            A.reset()
            P.reset_sems()
            wmg = A.alloc([128, 8, 3 * D], BF16)
            wbr = A.alloc([128, 3, 4 * D], BF16)
            wo = A.alloc([128, 8, D], BF16)
            for c3 in range(3):
                dma("pool", wmg[:, :, c3 * D:(c3 + 1) * D], w_mg[l][:, c3 * D:(c3 + 1) * D].rearrange("(c p) n -> p c n", p=128),
                    (), ["wmg%d" % c3], None)
            for bi, wsrc_ in enumerate((w_bra, w_brb, w_brc)):
                dma("pool", wbr[:, bi, :].rearrange("p (c n) -> p c n", c=4), wsrc_[l].rearrange("(c p) n -> p c n", p=128),
                    (), ["wbr%d" % bi], None)
            dma("pool", wo, w_o[l].rearrange("(c p) n -> p c n", p=128), (), ["wo"], None)
            grep = A.alloc([128, 2, D], F32)
            lng = A.alloc([128, D], F32)
            lnb = A.alloc([128, D], F32)
            bmg = A.alloc([128, 24], F32)
            for cs in range(2):
                dma("sp", grep[:, cs, :], mod_d[l, cs:cs + 1, 2 * D:3 * D].partition_broadcast(128), ["mod_d"], ["grep"], None)
            ts("dve", grep, grep, 0.5, None, ALU.mult, None, ["grep"], ["grep"])
            dma("sp", lng, ln_g[l:l + 1, :].partition_broadcast(128), (), ["lng"], None)
            dma("sp", lnb, ln_b[l:l + 1, :].partition_broadcast(128), (), ["lnb"], None)
            dma("sp", bmg, b_mg[l].rearrange("(c p) -> p c", p=128), (), ["bmg"], None, slow=True)
            ts("dve", bmg, bmg, 0.5, None, ALU.mult, None, ["bmg"], ["bmg"])
            h3 = [A.alloc([128, 8, 512], BF16)] * 2
            y3 = [A.alloc([128, 12, 512], BF16)] * 2
            in_s = [P.new_sem()] * 2
            x3 = [A.alloc([128, D], F32) for _ in range(2)]
            x3_s = [P.new_sem() for _ in range(2)]
            tg = [A.alloc([128, 512], F32) for _ in range(3)]
            tp = [A.alloc([128, 512], F32) for _ in range(3)]
            mT = [A.alloc([128, 8, 512], BF16) for _ in range(2)]
            z3 = [A.alloc([128, D], F32) for _ in range(2)]
            o3_s = [P.new_sem() for _ in range(2)]
            st6 = A.alloc([128, 2, 6], F32)
            mv = A.alloc([128, 8], F32)
            c3 = dict(b=0, t=0, x=0)
            def p3_loads(gi_):
                tok0_ = groups[gi_][0]
                k_ = 0
                dma("sp", h3[k_], hT_d[:, tok0_:tok0_ + 512].rearrange("(c p) t -> p c t", p=128), ["hT_d%d" % gi_],
                    ["h3_%d" % k_], in_s[k_])
                for bi_, (dd, nm) in enumerate(((yaT_d, "yaT_d"), (ybT_d, "ybT_d"), (ycT_d, "ycT_d"))):
                    dma("sp", y3[k_][:, bi_ * 4:(bi_ + 1) * 4, :], dd[:, tok0_:tok0_ + 512].rearrange("(c p) t -> p c t", p=128),
                        ["%s%d" % (nm, gi_)], ["y3_%d" % k_], in_s[k_])

            def p3_xload(gi_, sti_):
                t0_ = groups[gi_][0] + sti_ * 128
                kx_ = sti_ % 2
                dma("sp", x3[kx_], x_src[t0_:t0_ + 128, :], ["x_d%d" % gi_], ["x3_%d" % kx_], x3_s[kx_])

            def p3_gates(gi_, oc):
                k = 0
                for bi in range(3):
                    pi = c3["b"] % 3
                    c3["b"] += 1
                    bg, bb = 2 * pi, 2 * pi + 1
                    for kc in range(8):
                        mm(bank(bg), wmg[:, kc, bi * D + oc * 128:bi * D + (oc + 1) * 128], h3[k][:, kc, :], kc == 0, kc == 7,
                           ["wmg%d" % bi, "h3_%d" % k], ["psb%d" % bg])
                    for kc in range(4):
                        mm(bank(bb), wbr[:, bi, kc * D + oc * 128:kc * D + (oc + 1) * 128], y3[k][:, bi * 4 + kc, :], kc == 0, kc == 3,
                           ["wbr%d" % bi, "y3_%d" % k], ["psb%d" % bb])
                    ti_ = c3["t"] % 3
                    c3["t"] += 1
                    act(tg[ti_], bank(bg), AF.Tanh, ["psb%d" % bg, "bmg"], ["tg%d" % ti_], scale=0.5,
                        bias=bmg[:, bi * 8 + oc:bi * 8 + oc + 1])
                    stt(tp[bi], tg[ti_], 1.0, bank(bb), ALU.add, ALU.mult, ["tg%d" % ti_, "psb%d" % bb], ["tp%d" % bi])
                tt("dve", tp[0], tp[0], tp[1], ALU.add, ["tp0", "tp1"], ["tp0"])
                tt("dve", mT[gi_ % 2][:, oc, :], tp[0], tp[2], ALU.add, ["tp0", "tp2"], ["mT%d" % (gi_ % 2)])

            def p3_post(gi_, sti):
                tok0_, is_p_ = groups[gi_]
                cs_ = 1 if is_p_ else 0
                km = gi_ % 2
                t0 = tok0_ + sti * 128
                kx = sti % 2
                for half in range(2):
                    for kc in range(8):
                        mm(bank(6 + half), mT[km][:, kc, sti * 128:(sti + 1) * 128], wo[:, kc, half * 512:(half + 1) * 512],
                           kc == 0, kc == 7, ["mT%d" % km, "wo"], ["psb%d" % (6 + half)])
                for half in range(2):
                    act(z3[kx][:, half * 512:(half + 1) * 512], bank(6 + half), AF.Copy, ["psb%d" % (6 + half)], ["z3_%d" % kx])
                tt("pool", z3[kx], z3[kx], grep[:, cs_, :], ALU.mult, ["z3_%d" % kx, "grep"], ["z3_%d" % kx])
                stt(z3[kx], x3[kx], ALPHA_FULL, z3[kx], ALU.mult, ALU.add, ["x3_%d" % kx, "z3_%d" % kx], ["z3_%d" % kx])
                zr = ["z3_%d" % kx]
                P.add("dve", lambda e, a=z3[kx], o=st6[:, 0, :]: e.bn_stats(out=o, in_=a[:, 0:512]), zr, ["st6"])
                P.add("dve", lambda e, a=z3[kx], o=st6[:, 1, :]: e.bn_stats(out=o, in_=a[:, 512:1024]), zr, ["st6"])
                P.add("dve", lambda e, o=mv[:, 0:2], i=st6[:].rearrange("p a b -> p (a b)"): e.bn_aggr(out=o, in_=i), ["st6"], ["mv01"])
                ts("dve", mv[:, 2:3], mv[:, 1:2], EPS, None, ALU.add, None, ["mv01"], ["mv2"])
                rstd_pow(mv[:, 3:4], mv[:, 2:3], ["mv2", "nhalf"], ["mv3"])
                ts("dve", z3[kx], z3[kx], mv[:, 0:1], mv[:, 3:4], ALU.subtract, ALU.mult, zr + ["mv01", "mv3"], zr)
                tt("pool", z3[kx], z3[kx], lng, ALU.mult, zr + ["lng"], zr)
                tt("pool", z3[kx], z3[kx], lnb, ALU.add, zr + ["lnb"], zr)
                if sti + 2 < 4:
                    p3_xload(gi_, sti + 2)
                dma("sp", x_dst[t0:t0 + 128, :], z3[kx], zr, ["x_d%d" % gi_], o3_s[kx])

            ngr = len(groups)
            p3_loads(0)
            for gi in range(ngr + 1):
                for oc in range(8):
                    if gi < ngr:
                        p3_gates(gi, oc)
                    if gi > 0 and oc % 2 == 1:
                        p3_post(gi - 1, oc // 2)
                if gi < ngr:
                    p3_xload(gi, 0)
                    p3_xload(gi, 1)
                    if gi + 1 < ngr:
                        p3_loads(gi + 1)

        print("arena P3 KB", A.off / 512.0)
        P.emit()
    return nc


def _const_tables(nrows):
    NT = nrows * GW
    t = np.arange(NT)
    half = 32
    inv = (1.0 / (10000.0 ** (np.arange(0, half, 2, dtype=np.float32) / np.float32(half)))).astype(np.float32)
    ang_r = (t // GW).astype(np.float32)[:, None] * inv
    ang_c = (t % GW).astype(np.float32)[:, None] * inv
    cos_r, sin_r, cos_c, sin_c = np.cos(ang_r), np.sin(ang_r), np.cos(ang_c), np.sin(ang_c)
    C = np.zeros((128, NT), np.float32)
    S = np.zeros((128, NT), np.float32)
    for p in range(128):
        d = p % 64
        ax, f = d // 32, d % 16
        C[p] = (cos_r if ax == 0 else cos_c)[:, f]
        S[p] = (sin_r if ax == 0 else sin_c)[:, f]
    npair = nrows // 2
    reps = [2, 0, 1, npair - 2, npair - 1]
    cpos = np.arange(GW)
    cstart = np.clip(cpos - 8, 0, GW - 16)
    colok = (cpos[None, :] >= cstart[:, None]) & (cpos[None, :] < cstart[:, None] + 16)
    mask = np.full((5, 128, 640), NEG, np.float32)
    for v, j in enumerate(reps):
        jb = min(max(j - 2, 0), npair - 5)
        for blk in range(5):
            for kr_ in range(2):
                krow = 2 * (jb + blk) + kr_
                for qr_ in range(2):
                    r = 2 * j + qr_
                    rs = min(max(r - 4, 0), nrows - 8)
                    if rs <= krow < rs + 8:
                        sub = np.where(colok.T, 0.0, NEG).astype(np.float32)
                        mask[v, kr_ * 64:(kr_ + 1) * 64, blk * 128 + qr_ * 64:blk * 128 + (qr_ + 1) * 64] = sub
    return C, S, mask


def _tb_gather(rel_bias):
    o = np.arange(-4, 5)
    kr = np.arange(128) // 64
    kc = np.arange(128) % 64
    dy = np.clip(2 * o[:, None, None] + kr[None, :, None] - kr[None, None, :] + 7, 0, 14)
    dx = np.clip(kc[:, None] - kc[None, :], -15, 15) + 15
    dx = np.broadcast_to(dx[None], dy.shape)
    return np.ascontiguousarray(rel_bias[:, :, dy, dx])


_PROG_CACHE = {}


def _run(nrows, depth, inputs):
    key = (nrows, depth)
    if key not in _PROG_CACHE:
        _PROG_CACHE[key] = build_program(nrows, depth)
    nc = _PROG_CACHE[key]
    f = lambda a: np.ascontiguousarray(np.asarray(a, dtype=np.float32))
    I = {k: f(v) for k, v in inputs.items()}
    ncores = I["x_sample"].shape[0]
    C, S, mask = _const_tables(nrows)
    tbg = _tb_gather(I["na_rel_bias"])
    lamv = np.ascontiguousarray(np.stack([I["lambda_q1"], I["lambda_k1"], I["lambda_q2"], I["lambda_k2"]], axis=1))
    shared = dict(w_ada=I["w_ada"], b_ada=I["b_ada"], w_in=I["w_in"], sg_norm_g=I["sg_norm_g"], sg_norm_b=I["sg_norm_b"],
                  w_spatial=I["w_spatial"], b_spatial=I["b_spatial"], lamv=lamv, diff_subln_g=I["diff_subln_g"],
                  tb=tbg, maskc=mask, ropec=C, ropes=S, w_br_a=I["w_br_a"], w_br_b=I["w_br_b"], w_br_c=I["w_br_c"],
                  w_mgate=I["w_mgate"], b_mgate=I["b_mgate"], w_out=I["w_out"], ln_g=I["ln_g"], ln_b=I["ln_b"])
    in_maps = []
    for b in range(ncores):
        m = dict(shared)
        m["xin"] = np.ascontiguousarray(np.concatenate(
            [I["x_sample"][b], I["x_prompt"][NPS * b:NPS * (b + 1)].reshape(NP, D)], axis=0))
        m["cvec"] = np.ascontiguousarray(np.stack([I["c"][b], I["c_ctx"]], axis=0))
        m["cdk"] = I["cache_diff_k"][b]
        m["cdv"] = I["cache_diff_v"][b]
        m["cnk"] = I["cache_na_k"][b]
        m["cnv"] = I["cache_na_v"][b]
        in_maps.append(m)
    res = run_bass_kernel_spmd(nc, in_maps, core_ids=list(range(ncores)))
    R = res.results
    NT = nrows * GW
    y_s = np.stack([R[b]["y"][:NT] for b in range(ncores)], axis=0)
    y_p = np.concatenate([R[b]["y"][NT:].reshape(NPS, SEQ, D) for b in range(ncores)], axis=0)
    cat = lambda n: np.concatenate([R[b][n] for b in range(ncores)], axis=0)
    return (y_p.astype(np.float32), y_s.astype(np.float32), cat("o_dk").astype(np.float32), cat("o_dv").astype(np.float32),
            cat("o_nk").astype(np.float32), cat("o_nv").astype(np.float32))


def kernel(**inputs):
    return _run(64, 4, inputs)
```

```python
import math
import contextlib
import numpy as np
import concourse.bass as bass
import concourse.mybir as mybir
from concourse.bass_utils import run_bass_kernel_spmd

F32 = mybir.dt.float32
BF16 = mybir.dt.bfloat16
AF = mybir.ActivationFunctionType
ALU = mybir.AluOpType
AX = mybir.AxisListType

D = 1024
DIN = 5632
GW = 64
SEQ = 256
NPS = 2
NP = NPS * SEQ
PAST = 512
EPS = 1e-6
NEG = -30000.0
OFF = dict(ua=0, va=512, ga=1024, qb=1536, kb=2048, vb=2560, gb=3072, qc=3584, kc=4096, vc=4608, gc=5120)
ALPHA_FULL = (2 * 4) ** 0.25
ARENA_KB = 170
N_DMA_SEMS = 72


class Sem:
    def __init__(self, handle):
        self.h = handle
        self.count = 0


class Op:
    __slots__ = ("eng", "fn", "deps", "signal", "sem", "val", "is_dma")

    def __init__(self, eng, fn, is_dma, sem):
        self.eng = eng
        self.fn = fn
        self.deps = []
        self.signal = is_dma
        self.sem = sem
        self.val = None
        self.is_dma = is_dma


class Prog:
    ENGS = ("pe", "act", "dve", "pool", "sp")

    def __init__(self, nc, stack):
        self.nc = nc
        self.ops = {e: [] for e in self.ENGS}
        self.eng_sem = {e: Sem(stack.enter_context(nc.semaphore("es_" + e))) for e in self.ENGS}
        self.dma_sems = [Sem(stack.enter_context(nc.semaphore("ds%d" % i))) for i in range(N_DMA_SEMS)]
        self.sem_i = 0
        self.pool_sems = [Sem(stack.enter_context(nc.semaphore("pq%d" % i))) for i in range(12)]
        self.pool_prev = [None] * 12
        self.pool_i = 0
        self.last_w = {}
        self.readers = {}
        self.last_dma = {}
        self.barrier_ops = {e: [] for e in self.ENGS}

    def reset_sems(self):
        self.sem_i = 0

    def new_sem(self):
        s = self.dma_sems[self.sem_i]
        self.sem_i += 1
        return s

    def barrier(self):
        ops = [self.ops[e][-1] for e in self.ENGS if self.ops[e]]
        ops += list(self.last_dma.values())
        for e in self.ENGS:
            self.barrier_ops[e] = list(ops)

    def add(self, eng, fn, reads=(), writes=(), dma_sem=None):
        is_dma = dma_sem is not None
        deps = {}
        if is_dma and eng == "pool":
            pi = self.pool_i % len(self.pool_sems)
            self.pool_i += 1
            dma_sem = self.pool_sems[pi]
            if self.pool_prev[pi] is not None:
                deps[id(self.pool_prev[pi])] = (self.pool_prev[pi], True)
        op = Op(eng, fn, is_dma, dma_sem if is_dma else self.eng_sem[eng])
        if is_dma and eng == "pool":
            self.pool_prev[pi] = op
        for r in reads:
            w = self.last_w.get(r)
            if w is not None:
                deps[id(w)] = (w, True)
        for wr in writes:
            w = self.last_w.get(wr)
            if w is not None and id(w) not in deps:
                deps[id(w)] = (w, False)
            for rd in self.readers.get(wr, ()):
                if id(rd) not in deps:
                    deps[id(rd)] = (rd, False)
        if self.barrier_ops[eng]:
            for x in self.barrier_ops[eng]:
                deps[id(x)] = (x, True)
            self.barrier_ops[eng] = []
        for d, raw in deps.values():
            if d is op:
                continue
            if d.eng == eng and not d.is_dma and not is_dma:
                if eng == "pe":
                    continue
            d.signal = True
            op.deps.append((d, d.sem.count if d.is_dma else None))
        for r in reads:
            self.readers.setdefault(r, []).append(op)
        for wr in writes:
            self.last_w[wr] = op
            self.readers[wr] = []
        if is_dma:
            self.last_dma[id(dma_sem)] = op
            dma_sem.count += 16
            op.val = dma_sem.count
        self.ops[eng].append(op)
        return op

    def emit(self):
        nc = self.nc
        for e in self.ENGS:
            for op in self.ops[e]:
                if op.is_dma:
                    pass
                elif op.signal:
                    op.sem.count += 1
                    op.val = op.sem.count
        all_sems = self.dma_sems + self.pool_sems

        def run(e, engine):
            waited = {}
            for op in self.ops[e]:
                for d, ov in op.deps:
                    k = id(d.sem)
                    v = d.val if ov is None else ov
                    if waited.get(k, 0) >= v:
                        continue
                    waited[k] = v
                    engine.wait_ge(d.sem.h, v)
                inst = op.fn(engine)
                if op.is_dma:
                    inst.then_inc(op.sem.h, 16)
                elif op.signal:
                    inst.then_inc(op.sem.h, 1)
            if e == "sp":
                for s in all_sems:
                    if s.count > 0 and waited.get(id(s), 0) < s.count:
                        engine.wait_ge(s.h, s.count)
                for e2 in ("pe", "act", "dve", "pool"):
                    s = self.eng_sem[e2]
                    if s.count > 0:
                        engine.wait_ge(s.h, s.count)

        with nc.Block() as block:
            @block.tensor
            def _(eng):
                run("pe", eng)

            @block.scalar
            def _(eng):
                run("act", eng)

            @block.vector
            def _(eng):
                run("dve", eng)

            @block.gpsimd
            def _(eng):
                run("pool", eng)

            @block.sync
            def _(eng):
                run("sp", eng)


class Arena:
    def __init__(self, big, nelem):
        self.big = big
        self.n = nelem
        self.off = 0

    def reset(self):
        self.off = 0

    def alloc(self, shape, dtype):
        nfree = 1
        for s in shape[1:]:
            nfree *= s
        n16 = nfree * (2 if dtype == F32 else 1)
        n16 = (n16 + 15) // 16 * 16
        assert self.off + n16 <= self.n, ("arena overflow", self.off, n16, self.n)
        ap = self.big[:, self.off:self.off + nfree * (2 if dtype == F32 else 1)]
        self.off += n16
        if dtype == F32:
            ap = ap.bitcast(F32)
        if len(shape) == 3:
            ap = ap.rearrange("p (a b) -> p a b", a=shape[1], b=shape[2])
        elif len(shape) == 4:
            ap = ap.rearrange("p (a b c) -> p a b c", a=shape[1], b=shape[2], c=shape[3])
        if shape[0] < 128:
            ap = ap[0:shape[0]]
        return ap


def build_program(nrows, depth, stop_after=99, stage=99):
    NT = nrows * GW
    NG = NT // 512
    NTOT = NT + NP
    NKS = NT // 128
    NKC = PAST // 128
    NKP = NP // 128
    NKB = NKS + NKC + NKP
    NPAIR = nrows // 2

    nc = bass.Bass("TRN2", target_bir_lowering=False)

    def din(name, shape):
        return nc.dram_tensor(name, list(shape), F32, kind="ExternalInput").ap()

    def dout(name, shape):
        return nc.dram_tensor(name, list(shape), F32, kind="ExternalOutput").ap()

    def dscr(name, shape, dt):
        return nc.dram_tensor(name, list(shape), dt).ap()

    xin = din("xin", [NTOT, D])
    cvec = din("cvec", [2, D])
    cdk = din("cdk", [depth, 2, 4, PAST, 64])
    cdv = din("cdv", [depth, 4, PAST, 128])
    cnk = din("cnk", [depth, 8, PAST, 64])
    cnv = din("cnv", [depth, 8, PAST, 64])
    w_ada = din("w_ada", [depth, D, 3 * D])
    b_ada = din("b_ada", [depth, 3 * D])
    w_in = din("w_in", [depth, D, DIN])
    sg_g = din("sg_norm_g", [depth, 512])
    sg_b = din("sg_norm_b", [depth, 512])
    w_sp = din("w_spatial", [depth, 4, 128, 128])
    b_sp = din("b_spatial", [depth, 4, 128])
    lamv = din("lamv", [depth, 4, 64])
    subg = din("diff_subln_g", [depth, 128])
    tb = din("tb", [depth, 8, 9, 128, 128])
    maskc = din("maskc", [5, 128, 640])
    ropec = din("ropec", [128, NT])
    ropes = din("ropes", [128, NT])
    w_bra = din("w_br_a", [depth, 512, D])
    w_brb = din("w_br_b", [depth, 512, D])
    w_brc = din("w_br_c", [depth, 512, D])
    w_mg = din("w_mgate", [depth, D, 3 * D])
    b_mg = din("b_mgate", [depth, 3 * D])
    w_o = din("w_out", [depth, D, D])
    ln_g = din("ln_g", [depth, D])
    ln_b = din("ln_b", [depth, D])

    y_out = dout("y", [NTOT, D])
    o_dk = dout("o_dk", [NPS, depth, 2, 4, SEQ, 64])
    o_dv = dout("o_dv", [NPS, depth, 4, SEQ, 128])
    o_nk = dout("o_nk", [NPS, depth, 8, SEQ, 64])
    o_nv = dout("o_nv", [NPS, depth, 8, SEQ, 64])

    xbuf = dscr("xbuf", [NTOT, D], F32)
    hT_d = dscr("hT_d", [D, NTOT], BF16)
    yaT_d = dscr("yaT_d", [512, NTOT], BF16)
    ybT_d = dscr("ybT_d", [512, NTOT], BF16)
    ycT_d = dscr("ycT_d", [512, NTOT], BF16)
    qbT_d = dscr("qbT_d", [128, 4, NTOT], BF16)
    kbT_d = dscr("kbT_d", [128, 4, NTOT], BF16)
    qcT_d = dscr("qcT_d", [128, 4, NTOT], BF16)
    kcT_d = dscr("kcT_d", [128, 4, NTOT], BF16)
    vb_d = dscr("vb_d", [NTOT, 520], BF16)
    gb_d = dscr("gb_d", [NTOT, 512], BF16)
    vc_d = dscr("vc_d", [NTOT, 528], BF16)
    gc_d = dscr("gc_d", [NTOT, 512], BF16)
    mod_d = dscr("mod_d", [depth, 2, 3 * D], F32)
    M_d = dscr("M_d", [128, 8 * 5 * 640], BF16)

    wperm_d = dscr("wperm_d", [depth, D, 1024], F32)
    groups = [(g * 512, False) for g in range(NG)] + [(NT, True)]

    with contextlib.ExitStack() as st:
        P = Prog(nc, st)
        big = st.enter_context(nc.sbuf_tensor("arena", [128, ARENA_KB * 512], BF16))
        A = Arena(big, ARENA_KB * 512)
        ps = st.enter_context(nc.psum_tensor("ps", [128, 4096], F32))
        identf = st.enter_context(nc.sbuf_tensor("identf", [128, 128], F32))
        ident = st.enter_context(nc.sbuf_tensor("ident", [128, 128], BF16))
        protf = st.enter_context(nc.sbuf_tensor("protf", [128, 128], F32))
        prot = st.enter_context(nc.sbuf_tensor("prot", [128, 128], BF16))
        nhalf = st.enter_context(nc.sbuf_tensor("nhalf", [128, 1], F32))
        ccol = st.enter_context(nc.sbuf_tensor("ccol", [128, 2, 8], F32))
        ctmp = st.enter_context(nc.sbuf_tensor("ctmp", [128, 2, 8], F32))
        sc_col = st.enter_context(nc.sbuf_tensor("sc_col", [128, 8, 2], BF16))
        wsT = st.enter_context(nc.sbuf_tensor("wsT", [128, 4, 128], BF16))
        neglam = st.enter_context(nc.sbuf_tensor("neglam", [128, 1], F32))
        small = st.enter_context(nc.sbuf_tensor("small", [128, 64], F32))

        def bank(i, n=512):
            return ps[:, i * 512:i * 512 + n]

        def bank16(i, n=1024):
            return ps[:, i * 512:(i + 1) * 512].bitcast(BF16)[:, 0:n]

        def mm(out, lhsT, rhs, start, stop, reads, writes):
            return P.add("pe", lambda e: e.matmul(out, lhsT=lhsT, rhs=rhs, start=start, stop=stop,
                                                   skip_group_check=True), reads, writes)

        def tr(out, in_, idn, reads, writes):
            return P.add("pe", lambda e: e.transpose(out=out, in_=in_, identity=idn), reads, writes)

        def act(out, in_, func, reads, writes, scale=1.0, bias=None, accum=None):
            def f(e):
                kw = dict(out=out, in_=in_, func=func, scale=scale)
                if bias is not None:
                    kw["bias"] = bias
                if accum is not None:
                    kw["accum_out"] = accum
                return e.activation(**kw)
            return P.add("act", f, reads, writes)

        def tt(eng, out, in0, in1, op, reads, writes):
            return P.add(eng, lambda e: e.tensor_tensor(out=out, in0=in0, in1=in1, op=op), reads, writes)

        def ts(eng, out, in0, s1, s2, op0, op1, reads, writes):
            if op1 is None:
                return P.add(eng, lambda e: e.tensor_scalar(out=out, in0=in0, scalar1=s1, scalar2=None, op0=op0),
                             reads, writes)
            return P.add(eng, lambda e: e.tensor_scalar(out=out, in0=in0, scalar1=s1, scalar2=s2, op0=op0, op1=op1),
                         reads, writes)

        def stt(out, in0, scalar, in1, op0, op1, reads, writes):
            return P.add("dve", lambda e: e.scalar_tensor_tensor(out=out, in0=in0, scalar=scalar, in1=in1,
                                                                 op0=op0, op1=op1), reads, writes)

        def cp(eng, out, in_, reads, writes):
            return P.add(eng, lambda e: e.tensor_copy(out=out, in_=in_), reads, writes)

        def dma(q, out, in_, reads, writes, sem, slow=False):
            if sem is None and q != "pool":
                sem = P.new_sem()
            elif sem is None:
                sem = P.dma_sems[0]
            if slow:
                return P.add(q, lambda e: e.dma_start(out=out, in_=in_, allow_slow_non_contiguous=True),
                             reads, writes, dma_sem=sem)
            return P.add(q, lambda e: e.dma_start(out=out, in_=in_), reads, writes, dma_sem=sem)

        def rstd_pow(out, in_, reads, writes):
            return P.add("pool", lambda e: e.tensor_tensor(out=out, in0=in_, in1=nhalf[:], op=ALU.pow),
                         reads, writes)

        P.add("pool", lambda e: e.memset(identf[:], 0.0), (), ["identf"])
        P.add("pool", lambda e: e.affine_select(out=identf[:], in_=identf[:], pattern=[[-1, 128]],
                                                compare_op=ALU.not_equal, fill=1.0, base=0,
                                                channel_multiplier=1), ["identf"], ["identf"])
        cp("dve", ident[:], identf[:], ["identf"], ["ident"])
        P.add("pool", lambda e: e.memset(protf[:], 0.0), (), ["protf"])
        for blk in range(4):
            lo = blk * 32
            sl = protf[:, lo:lo + 16]
            P.add("pool", lambda e, sl=sl, lo=lo: e.affine_select(
                out=sl, in_=sl, pattern=[[-1, 16]], compare_op=ALU.not_equal, fill=-1.0,
                base=-(lo + 16), channel_multiplier=1), ["protf"], ["protf"])
            sh = protf[:, lo + 16:lo + 32]
            P.add("pool", lambda e, sh=sh, lo=lo: e.affine_select(
                out=sh, in_=sh, pattern=[[-1, 16]], compare_op=ALU.not_equal, fill=1.0,
                base=-lo, channel_multiplier=1), ["protf"], ["protf"])
        cp("dve", prot[:], protf[:], ["protf"], ["prot"])
        P.add("pool", lambda e: e.memset(nhalf[:], -0.5), (), ["nhalf"])
        s0 = P.new_sem()
        dma("sp", ccol[:], cvec.rearrange("s (c p) -> p s c", p=128), (), ["ccol"], s0, slow=True)
        act(ctmp[:], ccol[:], AF.Tanh, ["ccol"], ["ctmp"], scale=0.5)
        stt(ctmp[:], ctmp[:], 1.0, ccol[:], ALU.add, ALU.mult, ["ctmp", "ccol"], ["ctmp"])
        ts("dve", sc_col[:].rearrange("p c s -> p s c"), ctmp[:], 0.5, None, ALU.mult, None, ["ctmp"], ["sc_col"])

        def issue_wperm(l_):
            for ri, off in enumerate((OFF["qb"], OFF["kb"])):
                for m in range(2):
                    dst = wperm_d[l_][:, ri * 512:(ri + 1) * 512].rearrange("k (h m d) -> k h m d", h=4, m=2, d=64)[:, :, m, :]
                    src = w_in[l_][:, off + m * 256:off + (m + 1) * 256].rearrange("k (h d) -> k h d", h=4)
                    dma("sp", dst, src, (), ["wperm%d" % l_], None)

        issue_wperm(0)

        for l in range(depth):
            lam_init = 0.8 - 0.6 * math.exp(-0.3 * l)
            x_src = xin if l == 0 else xbuf
            x_dst = y_out if l == depth - 1 else xbuf

            P.barrier()
            A.reset()
            P.reset_sems()
            wa = [A.alloc([128, 8, 512], BF16) for _ in range(3)]
            wa_s = [P.new_sem() for _ in range(3)]
            brow = A.alloc([2, 3 * D], F32)
            mrow = A.alloc([2, 3 * D], F32)
            s1 = P.new_sem()
            dma("sp", brow, b_ada[l:l + 1, :].partition_broadcast(2), (), ["brow"], s1)
            for cb in range(6):
                k = cb % 3
                dma("pool", wa[k], w_ada[l][:, cb * 512:(cb + 1) * 512].rearrange("(c p) n -> p c n", p=128),
                    (), ["wa%d" % k], wa_s[k])
                for kc in range(8):
                    mm(bank(cb % 2)[0:2, :], sc_col[:, kc, :], wa[k][:, kc, :], kc == 0, kc == 7,
                       ["wa%d" % k, "sc_col"], ["psb%d" % (cb % 2)])
                tt("dve", mrow[:, cb * 512:(cb + 1) * 512], bank(cb % 2)[0:2, :], brow[:, cb * 512:(cb + 1) * 512],
                   ALU.add, ["psb%d" % (cb % 2), "brow"], ["mrow"])
            s2 = P.new_sem()
            dma("sp", mod_d[l], mrow, ["mrow"], ["mod_d"], s2)
            lv = A.alloc([128, 4, 64], F32)
            s3 = P.new_sem()
            dma("sp", lv, lamv[l:l + 1].partition_broadcast(128), (), ["lv"], s3)
            pr = A.alloc([128, 2, 64], F32)
            tt("dve", pr[:, 0, :], lv[:, 0, :], lv[:, 1, :], ALU.mult, ["lv"], ["pr"])
            tt("dve", pr[:, 1, :], lv[:, 2, :], lv[:, 3, :], ALU.mult, ["lv"], ["pr"])
            P.add("dve", lambda e, pr=pr: e.reduce_sum(out=small[:, 0:1], in_=pr[:, 0, :], axis=AX.X), ["pr"], ["sm0"])
            P.add("dve", lambda e, pr=pr: e.reduce_sum(out=small[:, 1:2], in_=pr[:, 1, :], axis=AX.X), ["pr"], ["sm1"])
            act(small[:, 2:4], small[:, 0:2], AF.Exp, ["sm0", "sm1"], ["sm2"])
            tt("dve", small[:, 4:5], small[:, 3:4], small[:, 2:3], ALU.subtract, ["sm2"], ["sm4"])
            ts("dve", neglam[:], small[:, 4:5], -lam_init, None, ALU.add, None, ["sm4"], ["neglam"])
            wsn = A.alloc([128, 4, 128], BF16)
            s4 = P.new_sem()
            dma("pool", wsn, w_sp[l].rearrange("g p q -> p g q"), (), ["wsn"], s4)
            for g in range(4):
                tr(bank16(2)[:, g * 128:(g + 1) * 128], wsn[:, g, :], ident[:], ["wsn", "ident"], ["psb2"])
            cp("dve", wsT[:].rearrange("p g q -> p (g q)"), bank16(2)[:, 0:512], ["psb2"], ["wsT"])
            tbt = A.alloc([128, 8, 9 * 128], F32)
            mk = A.alloc([128, 5, 640], F32)
            mst = A.alloc([128, 8 * 5, 640], BF16)
            s5, s6, s7 = P.new_sem(), P.new_sem(), P.new_sem()
            for h in range(8):
                dma("sp", tbt[:, h, :].rearrange("p (o q) -> p o q", o=9), tb[l, h].rearrange("o k q -> k o q"),
                    (), ["tbt%d" % h], None)
            dma("sp", mk, maskc.rearrange("v k c -> k v c"), (), ["mk"], s6)
            OB = [0, 4, 3, 1, 0]
            for h in range(8):
                for v in range(5):
                    tt("pool" if (h * 5 + v) % 2 else "dve", mst[:, h * 5 + v, :], tbt[:, h, OB[v] * 128:OB[v] * 128 + 640],
                       mk[:, v, :], ALU.add, ["tbt%d" % h, "mk"], ["mst%d" % (h * 5 + v)])
            dma("sp", M_d, mst[:].rearrange("p a b -> p (a b)"), ["mst%d" % i_ for i_ in range(40)], ["M_d"], s7)

            if stop_after < 1:
                break
            P.barrier()
            A.reset()
            P.reset_sems()
            win = A.alloc([128, 8, DIN], BF16)
            wsrc = w_in[l]
            for (a, b, nm) in ((0, 1536, "win_a"), (2560, DIN, "win_b")):
                dma("pool", win[:, :, a:b], wsrc[:, a:b].rearrange("(c p) n -> p c n", p=128), (), [nm], None)
            dma("pool", win[:, :, 1536:2560], wperm_d[l].rearrange("(c p) n -> p c n", p=128), ["wperm%d" % l], ["win_q"], None)
            sgg = A.alloc([128, 512], F32)
            sgb = A.alloc([128, 512], F32)
            brep = A.alloc([128, 4, 128], F32)
            modc = A.alloc([128, 2, 2, 8], F32)
            dma("sp", sgg, sg_g[l:l + 1, :].partition_broadcast(128), (), ["sgg"], None)
            dma("sp", sgb, sg_b[l:l + 1, :].partition_broadcast(128), (), ["sgb"], None)
            dma("sp", brep, b_sp[l:l + 1].partition_broadcast(128), (), ["brep"], None)
            for cs in range(2):
                for k in range(2):
                    dma("sp", modc[:, cs, k, :], mod_d[l, cs, k * D:(k + 1) * D].rearrange("(c p) -> p c", p=128),
                        ["mod_d"], ["modc"], None, slow=True)
            ts("dve", modc[:, :, 1, :], modc[:, :, 1, :], 1.0, None, ALU.add, None, ["modc"], ["modc"])

            xs = [A.alloc([128, D], F32) for _ in range(2)]
            xs_s = [P.new_sem() for _ in range(2)]
            xn = [A.alloc([128, D], BF16) for _ in range(2)]
            hT = [A.alloc([128, 8, 512], BF16) for _ in range(2)]
            hT_s = [P.new_sem() for _ in range(2)]
            rc = A.alloc([128, 512], F32)
            rs = A.alloc([128, 512], F32)
            rope_s = P.new_sem()
            fm = {n: A.alloc([128, 4, 512], BF16) for n in ("qb", "kb", "ya")}
            fm_s = {n: P.new_sem() for n in fm}
            fm["qc"], fm["kc"] = fm["qb"], fm["kb"]
            fm_s["qc"], fm_s["kc"] = fm_s["qb"], fm_s["kb"]
            fmr = dict(qb="fm_qb", kb="fm_kb", qc="fm_qb", kc="fm_kb", ya="fm_ya")
            tm = [[A.alloc([128, w_], BF16) for w_ in (520, 512, 528, 512)] for _ in range(2)]
            for k_ in range(2):
                P.add("pool", lambda e, a=tm[k_][0]: e.memset(a, 1.0), (), ["tm%d" % k_])
                P.add("pool", lambda e, a=tm[k_][2]: e.memset(a, 1.0), (), ["tm%d" % k_])
            tm_s = [P.new_sem() for _ in range(2)]
            of32 = [A.alloc([128, 512], F32)] * 2
            of_s = [P.new_sem()] * 2
            vn = A.alloc([128, 4, 512], BF16)
            tA = [A.alloc([128, 512], F32) for _ in range(2)]
            tB = [A.alloc([128, 512], F32) for _ in range(2)]
            tC = [A.alloc([128, 512], F32) for _ in range(2)]
            qraw = [A.alloc([128, 512], BF16) for _ in range(2)]
            st6 = A.alloc([128, 2, 6], F32)
            mv = A.alloc([128, 8], F32)

            cnt = dict(x=0, bank=0, t=0, q=0, of=0, tm=0)

            def nbank():
                b = 2 + cnt["bank"] % 6
                cnt["bank"] += 1
                return b

            def layer_norm_stats(src_lo, src_hi, rd, tag):
                P.add("dve", lambda e, o=st6[:, 0, :]: e.bn_stats(out=o, in_=src_lo), rd, ["st6"])
                P.add("dve", lambda e, o=st6[:, 1, :]: e.bn_stats(out=o, in_=src_hi), rd, ["st6"])
                P.add("dve", lambda e, o=mv[:, 0:2], i=st6[:].rearrange("p a b -> p (a b)"): e.bn_aggr(out=o, in_=i),
                      ["st6"], ["mv01"])
                ts("dve", mv[:, 2:3], mv[:, 1:2], EPS, None, ALU.add, None, ["mv01"], ["mv2"])
                rstd_pow(mv[:, 3:4], mv[:, 2:3], ["mv2", "nhalf"], ["mv3"])

            flat = [(gi_, sti_) for gi_ in range(len(groups)) for sti_ in range(4)]

            def p1_xload(fi):
                gi_, sti_ = flat[fi]
                k_ = fi % 2
                t0_ = groups[gi_][0] + sti_ * 128
                dma("sp", xs[k_], x_src[t0_:t0_ + 128, :], ["x_d%d" % gi_], ["xs%d" % k_], xs_s[k_])

            def p1_ln(gi_):
                cs_ = 1 if groups[gi_][1] else 0
                tok0_ = groups[gi_][0]
                hk_ = gi_ % 2
                hres_ = "hT%d" % hk_
                for sti_ in range(4):
                    fi = gi_ * 4 + sti_
                    k = fi % 2
                    layer_norm_stats(xs[k][:, 0:512], xs[k][:, 512:1024], ["xs%d" % k], "x")
                    ts("dve", xn[k], xs[k], mv[:, 0:1], mv[:, 3:4], ALU.subtract, ALU.mult,
                       ["xs%d" % k, "mv01", "mv3"], ["xn%d" % k])
                    if fi + 2 < len(flat):
                        p1_xload(fi + 2)
                    tb_ = sti_ % 2
                    for c in range(8):
                        tr(bank16(tb_)[:, c * 128:(c + 1) * 128], xn[k][:, c * 128:(c + 1) * 128], ident[:],
                           ["xn%d" % k, "ident"], ["psb%d" % tb_])
                    for c in range(8):
                        act(hT[hk_][:, c, sti_ * 128:(sti_ + 1) * 128], bank16(tb_)[:, c * 128:(c + 1) * 128], AF.Identity,
                            ["psb%d" % tb_, "modc"], [hres_], scale=modc[:, cs_, 1, c:c + 1], bias=modc[:, cs_, 0, c:c + 1])
                dma("sp", hT_d[:, tok0_:tok0_ + 512].rearrange("(c p) t -> p c t", p=128), hT[hk_],
                    [hres_], ["hT_d%d" % gi_], hT_s[hk_])

            p1_xload(0)
            p1_xload(1)
            p1_ln(0)
            for gi, (tok0, is_p) in enumerate(groups):
                cs = 1 if is_p else 0
                hk = gi % 2
                hres = "hT%d" % hk
                if not is_p:
                    dma("sp", rc, ropec[:, tok0:tok0 + 512], (), ["rc"], rope_s)
                    dma("sp", rs, ropes[:, tok0:tok0 + 512], (), ["rs"], rope_s)

                def win_res(col0, kc):
                    if col0 < 1536:
                        return ["win_a"]
                    if col0 >= 2560:
                        return ["win_b"]
                    return ["win_q"]

                def proj_fm(col0, b):
                    for kc in range(8):
                        mm(bank(b), win[:, kc, col0:col0 + 128], hT[hk][:, kc, :], kc == 0, kc == 7,
                           win_res(col0, kc) + [hres], ["psb%d" % b])

                def proj_tm(col0, sti, b):
                    for kc in range(8):
                        mm(bank(b), hT[hk][:, kc, sti * 128:(sti + 1) * 128], win[:, kc, col0:col0 + 512],
                           kc == 0, kc == 7, win_res(col0, kc) + [hres], ["psb%d" % b])

                for sti in range(4):
                    b = nbank()
                    proj_tm(OFF["va"], sti, b)
                    pb = "psb%d" % b
                    layer_norm_stats(bank(b)[:, 0:256], bank(b)[:, 256:512], [pb], "v")
                    k = cnt["t"] % 2
                    cnt["t"] += 1
                    ts("dve", tA[k], bank(b), mv[:, 0:1], mv[:, 3:4], ALU.subtract, ALU.mult,
                       [pb, "mv01", "mv3"], ["tA%d" % k])
                    tt("pool", tA[k], tA[k], sgg, ALU.mult, ["tA%d" % k, "sgg"], ["tA%d" % k])
                    tt("pool", vn[:, sti, :], tA[k], sgb, ALU.add, ["tA%d" % k, "sgb"], ["vn"])
                for g in range(4):
                    k = cnt["t"] % 2
                    cnt["t"] += 1
                    bu = nbank()
                    proj_fm(OFF["ua"] + g * 128, bu)
                    act(tA[k], bank(bu), AF.Copy, ["psb%d" % bu], ["tA%d" % k], scale=0.5)
                    bg = nbank()
                    proj_fm(OFF["ga"] + g * 128, bg)
                    act(tB[k], bank(bg), AF.Tanh, ["psb%d" % bg], ["tB%d" % k], scale=0.5)
                    stt(tB[k], tB[k], 1.0, bank(bg), ALU.add, ALU.mult, ["tB%d" % k, "psb%d" % bg], ["tB%d" % k])
                    bs_ = nbank()
                    for n in range(4):
                        mm(bank(bs_)[:, n * 128:(n + 1) * 128], vn[:, n, g * 128:(g + 1) * 128], wsT[:, g, :], True, True,
                           ["vn", "wsT"], ["psb%d" % bs_])
                    for n in range(4):
                        tt("dve", tC[k][:, n * 128:(n + 1) * 128], bank(bs_)[:, n * 128:(n + 1) * 128], brep[:, g, :],
                           ALU.add, ["psb%d" % bs_, "brep"], ["tC%d" % k])
                    tt("pool", tA[k], tA[k], tB[k], ALU.mult, ["tA%d" % k, "tB%d" % k], ["tA%d" % k])
                    tt("dve", fm["ya"][:, g, :], tA[k], tC[k], ALU.mult, ["tA%d" % k, "tC%d" % k], ["fm_ya"])
                dma("sp", yaT_d[:, tok0:tok0 + 512].rearrange("(c p) t -> p c t", p=128), fm["ya"],
                    ["fm_ya"], ["yaT_d%d" % gi], fm_s["ya"])

                if gi + 1 < len(groups):
                    p1_ln(gi + 1)
                for name in ("qb", "kb"):
                    for h in range(4):
                        b = nbank()
                        proj_fm(OFF[name] + h * 128, b)
                        pb = "psb%d" % b
                        if is_p:
                            act(fm[name][:, h, :], bank(b), AF.Copy, [pb], [fmr[name]])
                        else:
                            k = cnt["q"] % 2
                            cnt["q"] += 1
                            act(qraw[k], bank(b), AF.Copy, [pb], ["qraw%d" % k])
                            b2 = nbank()
                            mm(bank(b2), prot[:], qraw[k], True, True, ["prot", "qraw%d" % k], ["psb%d" % b2])
                            act(tA[k], bank(b), AF.Copy, [pb], ["tA%d" % k])
                            act(tB[k], bank(b2), AF.Copy, ["psb%d" % b2], ["tB%d" % k])
                            tt("dve", tA[k], tA[k], rc, ALU.mult, ["tA%d" % k, "rc"], ["tA%d" % k])
                            tt("pool", tB[k], tB[k], rs, ALU.mult, ["tB%d" % k, "rs"], ["tB%d" % k])
                            tt("pool", fm[name][:, h, :], tA[k], tB[k], ALU.add, ["tA%d" % k, "tB%d" % k], [fmr[name]])
                    dst = (qbT_d if name == "qb" else kbT_d)[:, :, tok0:tok0 + 512]
                    dma("sp", dst, fm[name], [fmr[name]], ["%sT_d%d" % (name, gi)], fm_s[name])
                for name in ("qc", "kc"):
                    for hp in range(4):
                        b = nbank()
                        proj_fm(OFF[name] + hp * 128, b)
                        act(fm[name][:, hp, :], bank(b), AF.Copy, ["psb%d" % b], [fmr[name]])
                    dst = (qcT_d if name == "qc" else kcT_d)[:, :, tok0:tok0 + 512]
                    dma("sp", dst, fm[name], [fmr[name]], ["%sT_d%d" % (name, gi)], fm_s[name])

                for sti in range(4):
                    k = cnt["tm"] % 2
                    cnt["tm"] += 1
                    tmr = "tm%d" % k
                    t0 = tok0 + sti * 128
                    sq, tq = sti // 2, (sti % 2) * 128

                    def out_f32(b, dst_ap, src_view):
                        ko = 0
                        act(of32[ko], bank(b), AF.Copy, ["psb%d" % b], ["of%d" % ko])
                        dma("sp", dst_ap, src_view(of32[ko]), ["of%d" % ko], ["outs"], of_s[ko])

                    for ai, name in enumerate(("vb", "gb", "vc", "gc")):
                        b = nbank()
                        proj_tm(OFF[name], sti, b)
                        pb = "psb%d" % b
                        if name in ("vb", "vc"):
                            if name == "vb":
                                act(tm[k][0].rearrange("p (h v) -> p h v", h=4)[:, :, 0:128],
                                    bank(b).rearrange("p (h v) -> p h v", h=4), AF.Copy, [pb], [tmr])
                            else:
                                act(tm[k][2].rearrange("p (h v) -> p h v", h=8)[:, :, 0:64],
                                    bank(b).rearrange("p (h v) -> p h v", h=8), AF.Copy, [pb], [tmr])
                            if is_p:
                                if name == "vb":
                                    out_f32(b, o_dv[sq, l, :, tq:tq + 128, :].rearrange("h t v -> t h v"),
                                            lambda a: a.rearrange("p (h v) -> p h v", h=4))
                                else:
                                    out_f32(b, o_nv[sq, l, :, tq:tq + 128, :].rearrange("h t v -> t h v"),
                                            lambda a: a.rearrange("p (h v) -> p h v", h=8))
                        else:
                            kk = cnt["t"] % 2
                            cnt["t"] += 1
                            act(tC[kk], bank(b), AF.Tanh, [pb], ["tC%d" % kk], scale=0.5)
                            stt(tm[k][ai], tC[kk], 1.0, bank(b), ALU.add, ALU.mult, ["tC%d" % kk, pb], [tmr])
                    if is_p:
                        b = nbank()
                        proj_tm(OFF["kb"], sti, b)
                        act(of32[0], bank(b), AF.Copy, ["psb%d" % b], ["of0"])
                        for m_ in range(2):
                            dma("sp", o_dk[sq, l, m_, :, tq:tq + 128, :].rearrange("h t d -> t h d"),
                                of32[0].rearrange("p (h m d) -> p h m d", h=4, m=2)[:, :, m_, :], ["of0"], ["outs"], of_s[0])
                        b = nbank()
                        proj_tm(OFF["kc"], sti, b)
                        out_f32(b, o_nk[sq, l, :, tq:tq + 128, :].rearrange("h t v -> t h v"),
                                lambda a: a.rearrange("p (h v) -> p h v", h=8))
                    for ai, dd in enumerate((vb_d, gb_d, vc_d, gc_d)):
                        dma("sp", dd[t0:t0 + 128, :], tm[k][ai], [tmr], ["tmd%d_%d" % (ai, gi)], tm_s[k])

            if l == 0:
                print("arena P1 KB", A.off / 512.0)
            if stop_after < 2:
                break
            P.barrier()
            A.reset()
            P.reset_sems()
            KbT = A.alloc([128, 4, NKB * 128], BF16)
            Vb = A.alloc([128, NKB, 4 * 130], BF16)
            gsub = A.alloc([128, 128], F32)
            P.add("pool", lambda e, a=Vb[:, NKS:NKS + NKC, :]: e.memset(a, 1.0), (), ["Vb_c"])
            dma("sp", KbT[:, :, 0:NT], kbT_d[:, :, 0:NT], ["kbT_d%d" % g for g in range(NG)], ["KbT_s"], None)
            dma("sp", KbT[:, :, NT + PAST:NT + PAST + NP], kbT_d[:, :, NT:NTOT], ["kbT_d%d" % NG], ["KbT_p"], None)
            Vb4 = Vb.rearrange("p k (h v) -> p k h v", h=4)
            for k0 in range(0, NKS, 8):
                dma("sp", Vb[:, k0:k0 + 8, :], vb_d[k0 * 128:(k0 + 8) * 128, :].rearrange("(k p) f -> p k f", p=128),
                    ["tmd0_%d" % g for g in range(NG)], ["Vb_s%d" % (k0 // 8)], None)
            dma("sp", Vb[:, NKS + NKC:NKB, :], vb_d[NT:NTOT, :].rearrange("(k p) f -> p k f", p=128),
                ["tmd0_%d" % NG], ["Vb_p"], None)
            dma("sp", gsub, subg[l:l + 1, :].partition_broadcast(128), (), ["gsub"], None)
            ts("dve", gsub, gsub, (1.0 - lam_init) * 0.5, None, ALU.mult, None, ["gsub"], ["gsub"])
            for h_ in range(4):
                dma("pool", Vb4[:, NKS:NKS + NKC, h_, 0:128], cdv[l, h_].rearrange("(k p) v -> p k v", p=128),
                    (), ["Vb_c"], None)
            ckt = A.alloc([128, NKC, 4 * 128], BF16)
            ckt5 = ckt.rearrange("p k (h m d) -> p k h m d", h=4, m=2)
            for m in range(2):
                for h in range(4):
                    dma("pool", ckt5[:, :, h, m, :], cdk[l, m, h].rearrange("(k p) d -> p k d", p=128),
                        (), ["ckt"], None)
            for kb in range(NKC):
                for h in range(4):
                    tr(bank16(7)[:, h * 128:(h + 1) * 128], ckt[:, kb, h * 128:(h + 1) * 128], ident[:],
                       ["ckt", "ident"], ["psb7"])
                for h in range(4):
                    cp("dve", KbT[:, h, NT + kb * 128:NT + (kb + 1) * 128], bank16(7)[:, h * 128:(h + 1) * 128],
                       ["psb7"], ["KbT_c"])

            if l + 1 < depth:
                issue_wperm(l + 1)
            qt = [A.alloc([128, 4, 512], BF16) for _ in range(2)]
            qt_s = [P.new_sem() for _ in range(2)]
            gbt = [A.alloc([128, 4, 512], BF16) for _ in range(2)]
            pT2 = [A.alloc([128, 2, 512], BF16) for _ in range(3)]
            pT = [[pT2[i_][:, m_, :] for i_ in range(3)] for m_ in range(2)]
            ybt2 = [A.alloc([128, 4, 512], BF16) for _ in range(2)]
            ybT_st = A.alloc([128, 4, 512], BF16)
            ybT_s = P.new_sem()
            otmp = [A.alloc([128, 128], F32) for _ in range(4)]
            ofin = [A.alloc([128, 128], F32) for _ in range(4)]
            junk = A.alloc([128, 128], F32)
            sv_ = [A.alloc([128, 8], F32) for _ in range(4)]

            def kres(kb):
                return "KbT_s" if kb < NKS else ("KbT_c" if kb < NKS + NKC else "KbT_p")

            def vres(kb):
                return ("Vb_s%d" % (kb // 8)) if kb < NKS else ("Vb_c" if kb < NKS + NKC else "Vb_p")

            def acc_ap(i, n):
                return ps[:, (4 + i // 3) * 512 + (i % 3) * 130:(4 + i // 3) * 512 + (i % 3) * 130 + n]

            qtiles = [(g * 512, 512, list(range(NKS + NKC)), g) for g in range(NG)]
            for s in range(NPS):
                qtiles.append((NT + s * SEQ, SEQ, [NKS + NKC + 2 * s, NKS + NKC + 2 * s + 1], NG))
            cntp = dict(s=0, p=0, f=0)
            def p2a_loads(ti_):
                tok0_, N_, _, gi_ = qtiles[ti_]
                k_ = ti_ % 2
                dma("sp", qt[k_][:, :, 0:N_], qbT_d[:, :, tok0_:tok0_ + N_], ["qbT_d%d" % gi_], ["qt%d" % k_], qt_s[k_])
                dma("sp", gbt[k_][:, 0:N_ // 128, :], gb_d[tok0_:tok0_ + N_, :].rearrange("(s p) f -> p s f", p=128),
                    ["tmd1_%d" % gi_], ["gbt%d" % k_], qt_s[k_])

            def p2a_tail(ti_):
                tok0_, N_, _, gi_ = qtiles[ti_]
                yb_, ybr_ = ybt2[ti_ % 2], "ybt%d" % (ti_ % 2)
                for hc in range(4):
                    for qs in range(N_ // 128):
                        tr(bank16(7)[:, qs * 128:(qs + 1) * 128], yb_[:, qs, hc * 128:(hc + 1) * 128], ident[:],
                           [ybr_, "ident"], ["psb7"])
                    cp("dve", ybT_st[:, hc, 0:N_], bank16(7)[:, 0:N_], ["psb7"], ["ybT_st"])
                dma("sp", ybT_d[:, tok0_:tok0_ + N_].rearrange("(c p) t -> p c t", p=128), ybT_st[:, :, 0:N_],
                    ["ybT_st"], ["ybT_d%d" % gi_], ybT_s)

            p2a_loads(0)
            for ti, (tok0, N, kblocks, gi) in enumerate(qtiles):
                k = ti % 2
                nsub = N // 128
                ybt, ybr = ybt2[ti % 2], "ybt%d" % (ti % 2)
                if ti + 1 < len(qtiles):
                    p2a_loads(ti + 1)
                items = [(h, kbi, kb) for h in range(4) for kbi, kb in enumerate(kblocks)]
                nlast = len(kblocks) - 1
                started = set()

                def qk_exp(idx):
                    h, kbi, kb = items[idx]
                    sp_ = (cntp["s"] + idx) % 2
                    pk = (cntp["p"] + idx) % 3
                    for m in range(2):
                        r0 = m * 64
                        b = sp_ * 2 + m
                        mm(bank(b)[:, 0:N], KbT[r0:r0 + 64, h, kb * 128:(kb + 1) * 128], qt[k][r0:r0 + 64, h, 0:N],
                           True, True, [kres(kb), "qt%d" % k], ["psb%d" % b])
                    for m in range(2):
                        b = sp_ * 2 + m
                        act(pT[m][pk][:, 0:N], bank(b)[:, 0:N], AF.Exp, ["psb%d" % b], ["pT%d_%d" % (m, pk)], scale=0.125)

                def finalize(h):
                    for qs in range(nsub):
                        f = qs
                        i0, i1 = qs, 4 + qs
                        rd = ["psb%d" % (4 + i0 // 3), "psb%d" % (4 + i1 // 3)]
                        s = sv_[f]
                        sr = "sv%d" % f
                        P.add("dve", lambda e, s=s, a=acc_ap(i0, 130): e.reciprocal(out=s[:, 0:1], in_=a[:, 128:129]), rd, [sr + "a"])
                        P.add("dve", lambda e, s=s, a=acc_ap(i1, 130): e.reciprocal(out=s[:, 1:2], in_=a[:, 128:129]), rd, [sr + "b"])
                        tt("dve", s[:, 2:3], s[:, 1:2], neglam[:], ALU.mult, [sr + "b", "neglam"], [sr + "c"])
                        ts("dve", otmp[f], acc_ap(i1, 128), s[:, 2:3], None, ALU.mult, None, rd + [sr + "c"], ["otmp%d" % f])
                        stt(ofin[f], acc_ap(i0, 128), s[:, 0:1], otmp[f], ALU.mult, ALU.add, rd + [sr + "a", "otmp%d" % f],
                            ["ofin%d" % f])
                    for qs in range(nsub):
                        f = qs
                        s = sv_[f]
                        sr = "sv%d" % f
                        act(junk, ofin[f], AF.Square, ["ofin%d" % f], ["junk", sr + "d"], accum=s[:, 3:4])
                        ts("dve", s[:, 4:5], s[:, 3:4], 1.0 / 128.0, EPS, ALU.mult, ALU.add, [sr + "d"], [sr + "e"])
                        rstd_pow(s[:, 5:6], s[:, 4:5], [sr + "e", "nhalf"], [sr + "f"])
                        stt(ofin[f], ofin[f], s[:, 5:6], gsub, ALU.mult, ALU.mult, ["ofin%d" % f, sr + "f", "gsub"], ["ofin%d" % f])
                        tt("pool", ybt[:, qs, h * 128:(h + 1) * 128], ofin[f], gbt[k][:, qs, h * 128:(h + 1) * 128], ALU.mult,
                           ["ofin%d" % f, "gbt%d" % k], [ybr])

                def pv(idx):
                    h, kbi, kb = items[idx]
                    pk = (cntp["p"] + idx) % 3
                    if kbi == 0:
                        started.clear()
                    for m in range(2):
                        for qs in range(nsub):
                            i = m * 4 + qs
                            bk = 4 + i // 3
                            first = bk not in started
                            started.add(bk)
                            mm(acc_ap(i, 130), pT[m][pk][:, qs * 128:(qs + 1) * 128], Vb[:, kb, h * 130:(h + 1) * 130],
                               first, kbi == nlast, ["pT%d_%d" % (m, pk), vres(kb)], ["psb%d" % bk])
                    if kbi == nlast:
                        finalize(h)

                qk_exp(0)
                for idx in range(len(items)):
                    if idx + 1 < len(items):
                        qk_exp(idx + 1)
                    pv(idx)
                    if idx == min(5, len(items) - 1) and ti > 0:
                        p2a_tail(ti - 1)
                cntp["s"] += len(items)
                cntp["p"] += len(items)
            p2a_tail(len(qtiles) - 1)

            if l == 0:
                print("arena P2a KB", A.off / 512.0)
            if stop_after < 3:
                break
            P.barrier()
            A.reset()
            P.reset_sems()
            Mt = A.alloc([128, 40, 640], BF16)
            dma("sp", Mt[:].rearrange("p a b -> p (a b)"), M_d, ["M_d"], ["Mt"], None)
            kctx = A.alloc([128, 4, PAST + NP], BF16)
            vctx = A.alloc([128, NKC + NKP, 8 * 66], BF16)
            vctx4 = vctx.rearrange("p k (h v) -> p k h v", h=8)
            P.add("pool", lambda e, a=vctx: e.memset(a, 1.0), (), ["vctx"])
            dma("sp", kctx[:, :, PAST:PAST + NP], kcT_d[:, :, NT:NTOT], ["kcT_d%d" % NG], ["kctx"], None)
            dma("sp", vctx[:, NKC:NKC + NKP, :], vc_d[NT:NTOT, :].rearrange("(k p) f -> p k f", p=128),
                ["tmd2_%d" % NG], ["vctx"], None)
            for h_ in range(8):
                dma("pool", vctx4[:, 0:NKC, h_, 0:64], cnv[l, h_].rearrange("(k p) v -> p k v", p=128), (), ["vctx"], None)
            cnt_ = A.alloc([128, NKC, 8 * 64], BF16)
            for h_ in range(8):
                dma("pool", cnt_.rearrange("p k (h d) -> p k h d", h=8)[:, :, h_, :], cnk[l, h_].rearrange("(k p) d -> p k d", p=128),
                    (), ["cnt"], None)
            for kb in range(NKC):
                for hp in range(4):
                    tr(bank16(7)[:, hp * 128:(hp + 1) * 128], cnt_[:, kb, hp * 128:(hp + 1) * 128], ident[:],
                       ["cnt", "ident"], ["psb7"])
                for hp in range(4):
                    cp("dve", kctx[:, hp, kb * 128:(kb + 1) * 128], bank16(7)[:, hp * 128:(hp + 1) * 128],
                       ["psb7"], ["kctx"])
            kw = [A.alloc([128, 4, 1024], BF16) for _ in range(2)]
            vw = [A.alloc([128, 8, 8 * 66], BF16) for _ in range(2)]
            qct = [A.alloc([128, 4, 512], BF16) for _ in range(2)]
            gct = [A.alloc([128, 4, 512], BF16) for _ in range(2)]
            w_s = [P.new_sem() for _ in range(2)]
            pc = [A.alloc([128, 512], BF16) for _ in range(4)]
            pl = [A.alloc([128, 640], BF16) for _ in range(3)]
            slf = [A.alloc([128, 640], F32) for _ in range(3)]
            yct2 = [A.alloc([128, 4, 512], BF16) for _ in range(2)]
            ycT_st = A.alloc([128, 4, 512], BF16)
            ycT_s = P.new_sem()
            rv = [A.alloc([128, 4], F32) for _ in range(2)]
            cq = dict(s=0, p=0, l=0, f=0)

            def accc(hh, qs, n):
                return ps[:, (4 + hh) * 512 + qs * 66:(4 + hh) * 512 + qs * 66 + n]

            ntiles = NG + NPS

            def p2b_geom(ti_):
                if ti_ < NG:
                    g_ = ti_
                    wb0_ = max(0, 4 * g_ - 2)
                    wb1_ = min(NKS - 1, 4 * g_ + 5)
                    return g_ * 512, 512, g_, wb0_, wb1_, list(range(NKC))
                s_ = ti_ - NG
                return NT + s_ * SEQ, SEQ, NG, 0, 0, [NKC + 2 * s_, NKC + 2 * s_ + 1]

            def p2b_loads(ti_):
                tok0_, N_, gi_, wb0_, wb1_, _ = p2b_geom(ti_)
                k_ = ti_ % 2
                if ti_ < NG:
                    nwb_ = wb1_ - wb0_ + 1
                    dma("sp", kw[k_][:, :, 0:nwb_ * 128], kcT_d[:, :, wb0_ * 128:(wb1_ + 1) * 128],
                        ["kcT_d%d" % x for x in range(NG)], ["kw%d" % k_], w_s[k_])
                    dma("sp", vw[k_][:, 0:nwb_, :],
                        vc_d[wb0_ * 128:(wb1_ + 1) * 128, :].rearrange("(k p) f -> p k f", p=128),
                        ["tmd2_%d" % x for x in range(NG)], ["vw%d" % k_], w_s[k_])
                dma("sp", qct[k_][:, :, 0:N_], qcT_d[:, :, tok0_:tok0_ + N_], ["qcT_d%d" % gi_], ["qct%d" % k_], w_s[k_])
                dma("sp", gct[k_][:, 0:N_ // 128, :], gc_d[tok0_:tok0_ + N_, :].rearrange("(s p) f -> p s f", p=128),
                    ["tmd3_%d" % gi_], ["gct%d" % k_], w_s[k_])

            def p2b_tail(ti_):
                tok0_, N_, gi_, _, _, _ = p2b_geom(ti_)
                yc_, ycr_ = yct2[ti_ % 2], "yct%d" % (ti_ % 2)
                for hc in range(4):
                    for qs in range(N_ // 128):
                        tr(bank16(7)[:, qs * 128:(qs + 1) * 128], yc_[:, qs, hc * 128:(hc + 1) * 128], ident[:],
                           [ycr_, "ident"], ["psb7"])
                    cp("dve", ycT_st[:, hc, 0:N_], bank16(7)[:, 0:N_], ["psb7"], ["ycT_st"])
                dma("sp", ycT_d[:, tok0_:tok0_ + N_].rearrange("(c p) t -> p c t", p=128), ycT_st[:, :, 0:N_],
                    ["ycT_st"], ["ycT_d%d" % gi_], ycT_s)

            p2b_loads(0)
            for ti in range(ntiles):
                k = ti % 2
                is_p = ti >= NG
                g = ti
                tok0, N, gi, wb0, wb1, cblocks = p2b_geom(ti)
                nsub = N // 128
                yct, ycr = yct2[ti % 2], "yct%d" % (ti % 2)
                if ti + 1 < ntiles:
                    p2b_loads(ti + 1)
                items = []
                for hp in range(4):
                    for hh in range(2):
                        hitems = [("c", hp, hh, ci, cb) for ci, cb in enumerate(cblocks)]
                        if not is_p:
                            hitems += [("l", hp, hh, qs, None) for qs in range(4)]
                        for ii, it in enumerate(hitems):
                            items.append(it + (ii == 0, ii == len(hitems) - 1))

                def local_geom(qs):
                    j = 4 * g + qs
                    jb = min(max(j - 2, 0), NPAIR - 5)
                    if j == 0:
                        v = 1
                    elif j == 1:
                        v = 2
                    elif j == NPAIR - 2:
                        v = 3
                    elif j == NPAIR - 1:
                        v = 4
                    else:
                        v = 0
                    return jb, v

                def front(idx):
                    kind, hp, hh, a1, a2, isf, isl = items[idx]
                    head = 2 * hp + hh
                    r0 = 64 * hh
                    slot = (cq["s"] + idx) % 3
                    b0 = (0, 2, 6)[slot]
                    if kind == "c":
                        cb = a2
                        pk = (cq["p"] + idx) % 4
                        mm(bank(b0)[:, 0:N], kctx[r0:r0 + 64, hp, cb * 128:(cb + 1) * 128], qct[k][r0:r0 + 64, hp, 0:N],
                           True, True, ["kctx", "qct%d" % k], ["psb%d" % b0])
                        act(pc[pk][:, 0:N], bank(b0)[:, 0:N], AF.Exp, ["psb%d" % b0], ["pc%d" % pk], scale=0.125)
                    else:
                        qs = a1
                        jb, v = local_geom(qs)
                        lk = (cq["l"] + idx) % 3
                        for blk in range(5):
                            wi = jb + blk - wb0
                            mm(ps[:, b0 * 512 + blk * 128:b0 * 512 + (blk + 1) * 128],
                               kw[k][r0:r0 + 64, hp, wi * 128:(wi + 1) * 128], qct[k][r0:r0 + 64, hp, qs * 128:(qs + 1) * 128],
                               True, True, ["kw%d" % k, "qct%d" % k], ["psb%d" % b0, "psb%d" % (b0 + 1)])
                        stt(slf[lk], ps[:, b0 * 512:b0 * 512 + 640], 0.125, Mt[:, head * 5 + v, :], ALU.mult, ALU.add,
                            ["psb%d" % b0, "psb%d" % (b0 + 1), "Mt"], ["slf%d" % lk])
                        act(pl[lk], slf[lk], AF.Exp, ["slf%d" % lk], ["pl%d" % lk])

                def back(idx):
                    kind, hp, hh, a1, a2, isf, isl = items[idx]
                    head = 2 * hp + hh
                    accr = "psb%d" % (4 + hh)
                    if kind == "c":
                        cb = a2
                        pk = (cq["p"] + idx) % 4
                        for qs in range(nsub):
                            mm(accc(hh, qs, 66), pc[pk][:, qs * 128:(qs + 1) * 128], vctx[:, cb, head * 66:(head + 1) * 66],
                               isf and qs == 0, isl, ["pc%d" % pk, "vctx"], [accr])
                    else:
                        qs = a1
                        jb, v = local_geom(qs)
                        lk = (cq["l"] + idx) % 3
                        for blk in range(5):
                            wi = jb + blk - wb0
                            mm(accc(hh, qs, 66), pl[lk][:, blk * 128:(blk + 1) * 128], vw[k][:, wi, head * 66:(head + 1) * 66],
                               False, blk == 4, ["pl%d" % lk, "vw%d" % k], [accr])
                    if isl:
                        for qs in range(nsub):
                            f = cq["f"] % 2
                            cq["f"] += 1
                            ts("dve", rv[f][:, 0:1], accc(hh, qs, 66)[:, 64:65], 2.0, None, ALU.mult, None, [accr], ["rv%da" % f])
                            P.add("dve", lambda e, r=rv[f]: e.reciprocal(out=r[:, 1:2], in_=r[:, 0:1]), ["rv%da" % f], ["rv%db" % f])
                            stt(yct[:, qs, head * 64:(head + 1) * 64], accc(hh, qs, 64), rv[f][:, 1:2],
                                gct[k][:, qs, head * 64:(head + 1) * 64], ALU.mult, ALU.mult, [accr, "rv%db" % f, "gct%d" % k], [ycr])

                front(0)
                if len(items) > 1:
                    front(1)
                for idx in range(len(items)):
                    if idx + 2 < len(items):
                        front(idx + 2)
                    back(idx)
                    if idx == min(5, len(items) - 1) and ti > 0:
                        p2b_tail(ti - 1)
                cq["s"] += len(items)
                cq["p"] += len(items)
                cq["l"] += len(items)
            p2b_tail(ntiles - 1)

            if l == 0:
                print("arena P2b KB", A.off / 512.0)
            if stop_after < 4:
                break
            P.barrier()
            A.reset()
            P.reset_sems()
            wmg = A.alloc([128, 8, 3 * D], BF16)
            wbr = A.alloc([128, 3, 4 * D], BF16)
            wo = A.alloc([128, 8, D], BF16)
            for c3 in range(3):
                dma("pool", wmg[:, :, c3 * D:(c3 + 1) * D], w_mg[l][:, c3 * D:(c3 + 1) * D].rearrange("(c p) n -> p c n", p=128),
                    (), ["wmg%d" % c3], None)
            for bi, wsrc_ in enumerate((w_bra, w_brb, w_brc)):
                dma("pool", wbr[:, bi, :].rearrange("p (c n) -> p c n", c=4), wsrc_[l].rearrange("(c p) n -> p c n", p=128),
                    (), ["wbr%d" % bi], None)
            dma("pool", wo, w_o[l].rearrange("(c p) n -> p c n", p=128), (), ["wo"], None)
            grep = A.alloc([128, 2, D], F32)
            lng = A.alloc([128, D], F32)
            lnb = A.alloc([128, D], F32)
            bmg = A.alloc([128, 24], F32)
            for cs in range(2):
                dma("sp", grep[:, cs, :], mod_d[l, cs:cs + 1, 2 * D:3 * D].partition_broadcast(128), ["mod_d"], ["grep"], None)
            ts("dve", grep, grep, 0.5, None, ALU.mult, None, ["grep"], ["grep"])
            dma("sp", lng, ln_g[l:l + 1, :].partition_broadcast(128), (), ["lng"], None)
            dma("sp", lnb, ln_b[l:l + 1, :].partition_broadcast(128), (), ["lnb"], None)
            dma("sp", bmg, b_mg[l].rearrange("(c p) -> p c", p=128), (), ["bmg"], None, slow=True)
            ts("dve", bmg, bmg, 0.5, None, ALU.mult, None, ["bmg"], ["bmg"])
            h3 = [A.alloc([128, 8, 512], BF16)] * 2
            y3 = [A.alloc([128, 12, 512], BF16)] * 2
            in_s = [P.new_sem()] * 2
            x3 = [A.alloc([128, D], F32) for _ in range(2)]
            x3_s = [P.new_sem() for _ in range(2)]
            tg = [A.alloc([128, 512], F32) for _ in range(3)]
            tp = [A.alloc([128, 512], F32) for _ in range(3)]
            mT = [A.alloc([128, 8, 512], BF16)] * 2
            z3 = [A.alloc([128, D], F32) for _ in range(2)]
            o3 = [A.alloc([128, D], F32) for _ in range(2)]
            o3_s = [P.new_sem() for _ in range(2)]
            st6 = A.alloc([128, 2, 6], F32)
            mv = A.alloc([128, 8], F32)
            c3 = dict(b=0, t=0, x=0)
            def p3_loads(gi_):
                tok0_ = groups[gi_][0]
                k_ = 0
                dma("sp", h3[k_], hT_d[:, tok0_:tok0_ + 512].rearrange("(c p) t -> p c t", p=128), ["hT_d%d" % gi_],
                    ["h3_%d" % k_], in_s[k_])
                for bi_, (dd, nm) in enumerate(((yaT_d, "yaT_d"), (ybT_d, "ybT_d"), (ycT_d, "ycT_d"))):
                    dma("sp", y3[k_][:, bi_ * 4:(bi_ + 1) * 4, :], dd[:, tok0_:tok0_ + 512].rearrange("(c p) t -> p c t", p=128),
                        ["%s%d" % (nm, gi_)], ["y3_%d" % k_], in_s[k_])

            def p3_xload(gi_, sti_):
                t0_ = groups[gi_][0] + sti_ * 128
                kx_ = sti_ % 2
                dma("sp", x3[kx_], x_src[t0_:t0_ + 128, :], ["x_d%d" % gi_], ["x3_%d" % kx_], x3_s[kx_])

            p3_loads(0)
            for gi, (tok0, is_p) in enumerate(groups):
                cs = 1 if is_p else 0
                k = 0
                for oc in range(8):
                    for bi in range(3):
                        pi = c3["b"] % 3
                        c3["b"] += 1
                        bg, bb = 2 * pi, 2 * pi + 1
                        for kc in range(8):
                            mm(bank(bg), wmg[:, kc, bi * D + oc * 128:bi * D + (oc + 1) * 128], h3[k][:, kc, :], kc == 0, kc == 7,
                               ["wmg%d" % bi, "h3_%d" % k], ["psb%d" % bg])
                        for kc in range(4):
                            mm(bank(bb), wbr[:, bi, kc * D + oc * 128:kc * D + (oc + 1) * 128], y3[k][:, bi * 4 + kc, :], kc == 0, kc == 3,
                               ["wbr%d" % bi, "y3_%d" % k], ["psb%d" % bb])
                        ti_ = c3["t"] % 3
                        c3["t"] += 1
                        act(tg[ti_], bank(bg), AF.Tanh, ["psb%d" % bg, "bmg"], ["tg%d" % ti_], scale=0.5,
                            bias=bmg[:, bi * 8 + oc:bi * 8 + oc + 1])
                        stt(tp[bi], tg[ti_], 1.0, bank(bb), ALU.add, ALU.mult, ["tg%d" % ti_, "psb%d" % bb], ["tp%d" % bi])
                    tt("dve", tp[0], tp[0], tp[1], ALU.add, ["tp0", "tp1"], ["tp0"])
                    tt("dve", mT[k][:, oc, :], tp[0], tp[2], ALU.add, ["tp0", "tp2"], ["mT%d" % k])
                p3_xload(gi, 0)
                p3_xload(gi, 1)
                if gi + 1 < len(groups):
                    p3_loads(gi + 1)
                for sti in range(4):
                    t0 = tok0 + sti * 128
                    kx = sti % 2
                    for half in range(2):
                        for kc in range(8):
                            mm(bank(6 + half), mT[k][:, kc, sti * 128:(sti + 1) * 128], wo[:, kc, half * 512:(half + 1) * 512],
                               kc == 0, kc == 7, ["mT%d" % k, "wo"], ["psb%d" % (6 + half)])
                    for half in range(2):
                        act(z3[kx][:, half * 512:(half + 1) * 512], bank(6 + half), AF.Copy, ["psb%d" % (6 + half)], ["z3_%d" % kx])
                    tt("pool", z3[kx], z3[kx], grep[:, cs, :], ALU.mult, ["z3_%d" % kx, "grep"], ["z3_%d" % kx])
                    stt(z3[kx], x3[kx], ALPHA_FULL, z3[kx], ALU.mult, ALU.add, ["x3_%d" % kx, "z3_%d" % kx], ["z3_%d" % kx])
                    zr = ["z3_%d" % kx]
                    P.add("dve", lambda e, a=z3[kx], o=st6[:, 0, :]: e.bn_stats(out=o, in_=a[:, 0:512]), zr, ["st6"])
                    P.add("dve", lambda e, a=z3[kx], o=st6[:, 1, :]: e.bn_stats(out=o, in_=a[:, 512:1024]), zr, ["st6"])
                    P.add("dve", lambda e, o=mv[:, 0:2], i=st6[:].rearrange("p a b -> p (a b)"): e.bn_aggr(out=o, in_=i), ["st6"], ["mv01"])
                    ts("dve", mv[:, 2:3], mv[:, 1:2], EPS, None, ALU.add, None, ["mv01"], ["mv2"])
                    rstd_pow(mv[:, 3:4], mv[:, 2:3], ["mv2", "nhalf"], ["mv3"])
                    ts("dve", z3[kx], z3[kx], mv[:, 0:1], mv[:, 3:4], ALU.subtract, ALU.mult, zr + ["mv01", "mv3"], zr)
                    tt("pool", z3[kx], z3[kx], lng, ALU.mult, zr + ["lng"], zr)
                    tt("pool", o3[kx], z3[kx], lnb, ALU.add, zr + ["lnb"], ["o3_%d" % kx])
                    if sti + 2 < 4:
                        p3_xload(gi, sti + 2)
                    dma("sp", x_dst[t0:t0 + 128, :], o3[kx], ["o3_%d" % kx], ["x_d%d" % gi], o3_s[kx])

        print("arena P3 KB", A.off / 512.0)
        P.emit()
    return nc


def _const_tables(nrows):
    NT = nrows * GW
    t = np.arange(NT)
    half = 32
    inv = (1.0 / (10000.0 ** (np.arange(0, half, 2, dtype=np.float32) / np.float32(half)))).astype(np.float32)
    ang_r = (t // GW).astype(np.float32)[:, None] * inv
    ang_c = (t % GW).astype(np.float32)[:, None] * inv
    cos_r, sin_r, cos_c, sin_c = np.cos(ang_r), np.sin(ang_r), np.cos(ang_c), np.sin(ang_c)
    C = np.zeros((128, NT), np.float32)
    S = np.zeros((128, NT), np.float32)
    for p in range(128):
        d = p % 64
        ax, f = d // 32, d % 16
        C[p] = (cos_r if ax == 0 else cos_c)[:, f]
        S[p] = (sin_r if ax == 0 else sin_c)[:, f]
    npair = nrows // 2
    reps = [2, 0, 1, npair - 2, npair - 1]
    cpos = np.arange(GW)
    cstart = np.clip(cpos - 8, 0, GW - 16)
    colok = (cpos[None, :] >= cstart[:, None]) & (cpos[None, :] < cstart[:, None] + 16)
    mask = np.full((5, 128, 640), NEG, np.float32)
    for v, j in enumerate(reps):
        jb = min(max(j - 2, 0), npair - 5)
        for blk in range(5):
            for kr_ in range(2):
                krow = 2 * (jb + blk) + kr_
                for qr_ in range(2):
                    r = 2 * j + qr_
                    rs = min(max(r - 4, 0), nrows - 8)
                    if rs <= krow < rs + 8:
                        sub = np.where(colok.T, 0.0, NEG).astype(np.float32)
                        mask[v, kr_ * 64:(kr_ + 1) * 64, blk * 128 + qr_ * 64:blk * 128 + (qr_ + 1) * 64] = sub
    return C, S, mask


def _tb_gather(rel_bias):
    o = np.arange(-4, 5)
    kr = np.arange(128) // 64
    kc = np.arange(128) % 64
    dy = np.clip(2 * o[:, None, None] + kr[None, :, None] - kr[None, None, :] + 7, 0, 14)
    dx = np.clip(kc[:, None] - kc[None, :], -15, 15) + 15
    dx = np.broadcast_to(dx[None], dy.shape)
    return np.ascontiguousarray(rel_bias[:, :, dy, dx])


_PROG_CACHE = {}


def _run(nrows, depth, inputs):
    key = (nrows, depth)
    if key not in _PROG_CACHE:
        _PROG_CACHE[key] = build_program(nrows, depth)
    nc = _PROG_CACHE[key]
    f = lambda a: np.ascontiguousarray(np.asarray(a, dtype=np.float32))
    I = {k: f(v) for k, v in inputs.items()}
    ncores = I["x_sample"].shape[0]
    C, S, mask = _const_tables(nrows)
    tbg = _tb_gather(I["na_rel_bias"])
    lamv = np.ascontiguousarray(np.stack([I["lambda_q1"], I["lambda_k1"], I["lambda_q2"], I["lambda_k2"]], axis=1))
    shared = dict(w_ada=I["w_ada"], b_ada=I["b_ada"], w_in=I["w_in"], sg_norm_g=I["sg_norm_g"], sg_norm_b=I["sg_norm_b"],
                  w_spatial=I["w_spatial"], b_spatial=I["b_spatial"], lamv=lamv, diff_subln_g=I["diff_subln_g"],
                  tb=tbg, maskc=mask, ropec=C, ropes=S, w_br_a=I["w_br_a"], w_br_b=I["w_br_b"], w_br_c=I["w_br_c"],
                  w_mgate=I["w_mgate"], b_mgate=I["b_mgate"], w_out=I["w_out"], ln_g=I["ln_g"], ln_b=I["ln_b"])
    in_maps = []
    for b in range(ncores):
        m = dict(shared)
        m["xin"] = np.ascontiguousarray(np.concatenate(
            [I["x_sample"][b], I["x_prompt"][NPS * b:NPS * (b + 1)].reshape(NP, D)], axis=0))
        m["cvec"] = np.ascontiguousarray(np.stack([I["c"][b], I["c_ctx"]], axis=0))
        m["cdk"] = I["cache_diff_k"][b]
        m["cdv"] = I["cache_diff_v"][b]
        m["cnk"] = I["cache_na_k"][b]
        m["cnv"] = I["cache_na_v"][b]
        in_maps.append(m)
    res = run_bass_kernel_spmd(nc, in_maps, core_ids=list(range(ncores)))
    R = res.results
    NT = nrows * GW
    y_s = np.stack([R[b]["y"][:NT] for b in range(ncores)], axis=0)
    y_p = np.concatenate([R[b]["y"][NT:].reshape(NPS, SEQ, D) for b in range(ncores)], axis=0)
    cat = lambda n: np.concatenate([R[b][n] for b in range(ncores)], axis=0)
    return (y_p.astype(np.float32), y_s.astype(np.float32), cat("o_dk").astype(np.float32), cat("o_dv").astype(np.float32),
            cat("o_nk").astype(np.float32), cat("o_nv").astype(np.float32))


def kernel(**inputs):
    return _run(64, 4, inputs)
```

```python
import math
import contextlib
import numpy as np
import concourse.bass as bass
import concourse.mybir as mybir
from concourse.bass_utils import run_bass_kernel_spmd

F32 = mybir.dt.float32
BF16 = mybir.dt.bfloat16
AF = mybir.ActivationFunctionType
ALU = mybir.AluOpType
AX = mybir.AxisListType

D = 1024
DIN = 5632
GW = 64
SEQ = 256
NPS = 2
NP = NPS * SEQ
PAST = 512
EPS = 1e-6
NEG = -30000.0
OFF = dict(ua=0, va=512, ga=1024, qb=1536, kb=2048, vb=2560, gb=3072, qc=3584, kc=4096, vc=4608, gc=5120)
ALPHA_FULL = (2 * 4) ** 0.25
ARENA_KB = 170
N_DMA_SEMS = 72


class Sem:
    def __init__(self, handle):
        self.h = handle
        self.count = 0


class Op:
    __slots__ = ("eng", "fn", "deps", "signal", "sem", "val", "is_dma")

    def __init__(self, eng, fn, is_dma, sem):
        self.eng = eng
        self.fn = fn
        self.deps = []
        self.signal = is_dma
        self.sem = sem
        self.val = None
        self.is_dma = is_dma


class Prog:
    ENGS = ("pe", "act", "dve", "pool", "sp")

    def __init__(self, nc, stack):
        self.nc = nc
        self.ops = {e: [] for e in self.ENGS}
        self.eng_sem = {e: Sem(stack.enter_context(nc.semaphore("es_" + e))) for e in self.ENGS}
        self.dma_sems = [Sem(stack.enter_context(nc.semaphore("ds%d" % i))) for i in range(N_DMA_SEMS)]
        self.sem_i = 0
        self.pool_sems = [Sem(stack.enter_context(nc.semaphore("pq%d" % i))) for i in range(12)]
        self.pool_prev = [None] * 12
        self.pool_i = 0
        self.last_w = {}
        self.readers = {}
        self.last_dma = {}
        self.barrier_ops = {e: [] for e in self.ENGS}

    def reset_sems(self):
        self.sem_i = 0

    def new_sem(self):
        s = self.dma_sems[self.sem_i]
        self.sem_i += 1
        return s

    def barrier(self):
        ops = [self.ops[e][-1] for e in self.ENGS if self.ops[e]]
        ops += list(self.last_dma.values())
        for e in self.ENGS:
            self.barrier_ops[e] = list(ops)

    def add(self, eng, fn, reads=(), writes=(), dma_sem=None):
        is_dma = dma_sem is not None
        deps = {}
        if is_dma and eng == "pool":
            pi = self.pool_i % len(self.pool_sems)
            self.pool_i += 1
            dma_sem = self.pool_sems[pi]
            if self.pool_prev[pi] is not None:
                deps[id(self.pool_prev[pi])] = (self.pool_prev[pi], True)
        op = Op(eng, fn, is_dma, dma_sem if is_dma else self.eng_sem[eng])
        if is_dma and eng == "pool":
            self.pool_prev[pi] = op
        for r in reads:
            w = self.last_w.get(r)
            if w is not None:
                deps[id(w)] = (w, True)
        for wr in writes:
            w = self.last_w.get(wr)
            if w is not None and id(w) not in deps:
                deps[id(w)] = (w, False)
            for rd in self.readers.get(wr, ()):
                if id(rd) not in deps:
                    deps[id(rd)] = (rd, False)
        if self.barrier_ops[eng]:
            for x in self.barrier_ops[eng]:
                deps[id(x)] = (x, True)
            self.barrier_ops[eng] = []
        for d, raw in deps.values():
            if d is op:
                continue
            if d.eng == eng and not d.is_dma and not is_dma:
                if eng == "pe":
                    continue
            d.signal = True
            op.deps.append((d, d.sem.count if d.is_dma else None))
        for r in reads:
            self.readers.setdefault(r, []).append(op)
        for wr in writes:
            self.last_w[wr] = op
            self.readers[wr] = []
        if is_dma:
            self.last_dma[id(dma_sem)] = op
            dma_sem.count += 16
            op.val = dma_sem.count
        self.ops[eng].append(op)
        return op

    def emit(self):
        nc = self.nc
        for e in self.ENGS:
            for op in self.ops[e]:
                if op.is_dma:
                    pass
                elif op.signal:
                    op.sem.count += 1
                    op.val = op.sem.count
        all_sems = self.dma_sems + self.pool_sems

        def run(e, engine):
            waited = {}
            for op in self.ops[e]:
                for d, ov in op.deps:
                    k = id(d.sem)
                    v = d.val if ov is None else ov
                    if waited.get(k, 0) >= v:
                        continue
                    waited[k] = v
                    engine.wait_ge(d.sem.h, v)
                inst = op.fn(engine)
                if op.is_dma:
                    inst.then_inc(op.sem.h, 16)
                elif op.signal:
                    inst.then_inc(op.sem.h, 1)
            if e == "sp":
                for s in all_sems:
                    if s.count > 0 and waited.get(id(s), 0) < s.count:
                        engine.wait_ge(s.h, s.count)
                for e2 in ("pe", "act", "dve", "pool"):
                    s = self.eng_sem[e2]
                    if s.count > 0:
                        engine.wait_ge(s.h, s.count)

        with nc.Block() as block:
            @block.tensor
            def _(eng):
                run("pe", eng)

            @block.scalar
            def _(eng):
                run("act", eng)

            @block.vector
            def _(eng):
                run("dve", eng)

            @block.gpsimd
            def _(eng):
                run("pool", eng)

            @block.sync
            def _(eng):
                run("sp", eng)


class Arena:
    def __init__(self, big, nelem):
        self.big = big
        self.n = nelem
        self.off = 0

    def reset(self):
        self.off = 0

    def alloc(self, shape, dtype):
        nfree = 1
        for s in shape[1:]:
            nfree *= s
        n16 = nfree * (2 if dtype == F32 else 1)
        n16 = (n16 + 15) // 16 * 16
        assert self.off + n16 <= self.n, ("arena overflow", self.off, n16, self.n)
        ap = self.big[:, self.off:self.off + nfree * (2 if dtype == F32 else 1)]
        self.off += n16
        if dtype == F32:
            ap = ap.bitcast(F32)
        if len(shape) == 3:
            ap = ap.rearrange("p (a b) -> p a b", a=shape[1], b=shape[2])
        elif len(shape) == 4:
            ap = ap.rearrange("p (a b c) -> p a b c", a=shape[1], b=shape[2], c=shape[3])
        if shape[0] < 128:
            ap = ap[0:shape[0]]
        return ap


def build_program(nrows, depth, stop_after=99, stage=99):
    NT = nrows * GW
    NG = NT // 512
    NTOT = NT + NP
    NKS = NT // 128
    NKC = PAST // 128
    NKP = NP // 128
    NKB = NKS + NKC + NKP
    NPAIR = nrows // 2

    nc = bass.Bass("TRN2", target_bir_lowering=False)

    def din(name, shape):
        return nc.dram_tensor(name, list(shape), F32, kind="ExternalInput").ap()

    def dout(name, shape):
        return nc.dram_tensor(name, list(shape), F32, kind="ExternalOutput").ap()

    def dscr(name, shape, dt):
        return nc.dram_tensor(name, list(shape), dt).ap()

    xin = din("xin", [NTOT, D])
    cvec = din("cvec", [2, D])
    cdk = din("cdk", [depth, 2, 4, PAST, 64])
    cdv = din("cdv", [depth, 4, PAST, 128])
    cnk = din("cnk", [depth, 8, PAST, 64])
    cnv = din("cnv", [depth, 8, PAST, 64])
    w_ada = din("w_ada", [depth, D, 3 * D])
    b_ada = din("b_ada", [depth, 3 * D])
    w_in = din("w_in", [depth, D, DIN])
    sg_g = din("sg_norm_g", [depth, 512])
    sg_b = din("sg_norm_b", [depth, 512])
    w_sp = din("w_spatial", [depth, 4, 128, 128])
    b_sp = din("b_spatial", [depth, 4, 128])
    lamv = din("lamv", [depth, 4, 64])
    subg = din("diff_subln_g", [depth, 128])
    tb = din("tb", [depth, 8, 9, 128, 128])
    maskc = din("maskc", [5, 128, 640])
    ropec = din("ropec", [128, NT])
    ropes = din("ropes", [128, NT])
    w_bra = din("w_br_a", [depth, 512, D])
    w_brb = din("w_br_b", [depth, 512, D])
    w_brc = din("w_br_c", [depth, 512, D])
    w_mg = din("w_mgate", [depth, D, 3 * D])
    b_mg = din("b_mgate", [depth, 3 * D])
    w_o = din("w_out", [depth, D, D])
    ln_g = din("ln_g", [depth, D])
    ln_b = din("ln_b", [depth, D])

    y_out = dout("y", [NTOT, D])
    o_dk = dout("o_dk", [NPS, depth, 2, 4, SEQ, 64])
    o_dv = dout("o_dv", [NPS, depth, 4, SEQ, 128])
    o_nk = dout("o_nk", [NPS, depth, 8, SEQ, 64])
    o_nv = dout("o_nv", [NPS, depth, 8, SEQ, 64])

    xbuf = dscr("xbuf", [NTOT, D], F32)
    hT_d = dscr("hT_d", [D, NTOT], BF16)
    yaT_d = dscr("yaT_d", [512, NTOT], BF16)
    ybT_d = dscr("ybT_d", [512, NTOT], BF16)
    ycT_d = dscr("ycT_d", [512, NTOT], BF16)
    qbT_d = dscr("qbT_d", [128, 4, NTOT], BF16)
    kbT_d = dscr("kbT_d", [128, 4, NTOT], BF16)
    qcT_d = dscr("qcT_d", [128, 4, NTOT], BF16)
    kcT_d = dscr("kcT_d", [128, 4, NTOT], BF16)
    vb_d = dscr("vb_d", [NTOT, 520], BF16)
    gb_d = dscr("gb_d", [NTOT, 512], BF16)
    vc_d = dscr("vc_d", [NTOT, 528], BF16)
    gc_d = dscr("gc_d", [NTOT, 512], BF16)
    mod_d = dscr("mod_d", [depth, 2, 3 * D], F32)
    M_d = dscr("M_d", [128, 8 * 5 * 640], BF16)

    wperm_d = dscr("wperm_d", [depth, D, 1024], F32)
    groups = [(g * 512, False) for g in range(NG)] + [(NT, True)]

    with contextlib.ExitStack() as st:
        P = Prog(nc, st)
        big = st.enter_context(nc.sbuf_tensor("arena", [128, ARENA_KB * 512], BF16))
        A = Arena(big, ARENA_KB * 512)
        ps = st.enter_context(nc.psum_tensor("ps", [128, 4096], F32))
        identf = st.enter_context(nc.sbuf_tensor("identf", [128, 128], F32))
        ident = st.enter_context(nc.sbuf_tensor("ident", [128, 128], BF16))
        protf = st.enter_context(nc.sbuf_tensor("protf", [128, 128], F32))
        prot = st.enter_context(nc.sbuf_tensor("prot", [128, 128], BF16))
        nhalf = st.enter_context(nc.sbuf_tensor("nhalf", [128, 1], F32))
        ccol = st.enter_context(nc.sbuf_tensor("ccol", [128, 2, 8], F32))
        ctmp = st.enter_context(nc.sbuf_tensor("ctmp", [128, 2, 8], F32))
        sc_col = st.enter_context(nc.sbuf_tensor("sc_col", [128, 8, 2], BF16))
        wsT = st.enter_context(nc.sbuf_tensor("wsT", [128, 4, 128], BF16))
        neglam = st.enter_context(nc.sbuf_tensor("neglam", [128, 1], F32))
        small = st.enter_context(nc.sbuf_tensor("small", [128, 64], F32))

        def bank(i, n=512):
            return ps[:, i * 512:i * 512 + n]

        def bank16(i, n=1024):
            return ps[:, i * 512:(i + 1) * 512].bitcast(BF16)[:, 0:n]

        def mm(out, lhsT, rhs, start, stop, reads, writes):
            return P.add("pe", lambda e: e.matmul(out, lhsT=lhsT, rhs=rhs, start=start, stop=stop,
                                                   skip_group_check=True), reads, writes)

        def tr(out, in_, idn, reads, writes):
            return P.add("pe", lambda e: e.transpose(out=out, in_=in_, identity=idn), reads, writes)

        def act(out, in_, func, reads, writes, scale=1.0, bias=None, accum=None):
            def f(e):
                kw = dict(out=out, in_=in_, func=func, scale=scale)
                if bias is not None:
                    kw["bias"] = bias
                if accum is not None:
                    kw["accum_out"] = accum
                return e.activation(**kw)
            return P.add("act", f, reads, writes)

        def tt(eng, out, in0, in1, op, reads, writes):
            return P.add(eng, lambda e: e.tensor_tensor(out=out, in0=in0, in1=in1, op=op), reads, writes)

        def ts(eng, out, in0, s1, s2, op0, op1, reads, writes):
            if op1 is None:
                return P.add(eng, lambda e: e.tensor_scalar(out=out, in0=in0, scalar1=s1, scalar2=None, op0=op0),
                             reads, writes)
            return P.add(eng, lambda e: e.tensor_scalar(out=out, in0=in0, scalar1=s1, scalar2=s2, op0=op0, op1=op1),
                         reads, writes)

        def stt(out, in0, scalar, in1, op0, op1, reads, writes):
            return P.add("dve", lambda e: e.scalar_tensor_tensor(out=out, in0=in0, scalar=scalar, in1=in1,
                                                                 op0=op0, op1=op1), reads, writes)

        def cp(eng, out, in_, reads, writes):
            return P.add(eng, lambda e: e.tensor_copy(out=out, in_=in_), reads, writes)

        def dma(q, out, in_, reads, writes, sem, slow=False):
            if sem is None and q != "pool":
                sem = P.new_sem()
            elif sem is None:
                sem = P.dma_sems[0]
            if slow:
                return P.add(q, lambda e: e.dma_start(out=out, in_=in_, allow_slow_non_contiguous=True),
                             reads, writes, dma_sem=sem)
            return P.add(q, lambda e: e.dma_start(out=out, in_=in_), reads, writes, dma_sem=sem)

        def rstd_pow(out, in_, reads, writes):
            return P.add("pool", lambda e: e.tensor_tensor(out=out, in0=in_, in1=nhalf[:], op=ALU.pow),
                         reads, writes)

        P.add("pool", lambda e: e.memset(identf[:], 0.0), (), ["identf"])
        P.add("pool", lambda e: e.affine_select(out=identf[:], in_=identf[:], pattern=[[-1, 128]],
                                                compare_op=ALU.not_equal, fill=1.0, base=0,
                                                channel_multiplier=1), ["identf"], ["identf"])
        cp("dve", ident[:], identf[:], ["identf"], ["ident"])
        P.add("pool", lambda e: e.memset(protf[:], 0.0), (), ["protf"])
        for blk in range(4):
            lo = blk * 32
            sl = protf[:, lo:lo + 16]
            P.add("pool", lambda e, sl=sl, lo=lo: e.affine_select(
                out=sl, in_=sl, pattern=[[-1, 16]], compare_op=ALU.not_equal, fill=-1.0,
                base=-(lo + 16), channel_multiplier=1), ["protf"], ["protf"])
            sh = protf[:, lo + 16:lo + 32]
            P.add("pool", lambda e, sh=sh, lo=lo: e.affine_select(
                out=sh, in_=sh, pattern=[[-1, 16]], compare_op=ALU.not_equal, fill=1.0,
                base=-lo, channel_multiplier=1), ["protf"], ["protf"])
        cp("dve", prot[:], protf[:], ["protf"], ["prot"])
        P.add("pool", lambda e: e.memset(nhalf[:], -0.5), (), ["nhalf"])
        s0 = P.new_sem()
        dma("sp", ccol[:], cvec.rearrange("s (c p) -> p s c", p=128), (), ["ccol"], s0, slow=True)
        act(ctmp[:], ccol[:], AF.Tanh, ["ccol"], ["ctmp"], scale=0.5)
        stt(ctmp[:], ctmp[:], 1.0, ccol[:], ALU.add, ALU.mult, ["ctmp", "ccol"], ["ctmp"])
        ts("dve", sc_col[:].rearrange("p c s -> p s c"), ctmp[:], 0.5, None, ALU.mult, None, ["ctmp"], ["sc_col"])

        def issue_wperm(l_):
            for ri, off in enumerate((OFF["qb"], OFF["kb"])):
                for m in range(2):
                    dst = wperm_d[l_][:, ri * 512:(ri + 1) * 512].rearrange("k (h m d) -> k h m d", h=4, m=2, d=64)[:, :, m, :]
                    src = w_in[l_][:, off + m * 256:off + (m + 1) * 256].rearrange("k (h d) -> k h d", h=4)
                    dma("sp", dst, src, (), ["wperm%d" % l_], None)

        issue_wperm(0)

        for l in range(depth):
            lam_init = 0.8 - 0.6 * math.exp(-0.3 * l)
            x_src = xin if l == 0 else xbuf
            x_dst = y_out if l == depth - 1 else xbuf

            P.barrier()
            A.reset()
            P.reset_sems()
            wa = [A.alloc([128, 8, 512], BF16) for _ in range(3)]
            wa_s = [P.new_sem() for _ in range(3)]
            brow = A.alloc([2, 3 * D], F32)
            mrow = A.alloc([2, 3 * D], F32)
            s1 = P.new_sem()
            dma("sp", brow, b_ada[l:l + 1, :].partition_broadcast(2), (), ["brow"], s1)
            for cb in range(6):
                k = cb % 3
                dma("pool", wa[k], w_ada[l][:, cb * 512:(cb + 1) * 512].rearrange("(c p) n -> p c n", p=128),
                    (), ["wa%d" % k], wa_s[k])
                for kc in range(8):
                    mm(bank(cb % 2)[0:2, :], sc_col[:, kc, :], wa[k][:, kc, :], kc == 0, kc == 7,
                       ["wa%d" % k, "sc_col"], ["psb%d" % (cb % 2)])
                tt("dve", mrow[:, cb * 512:(cb + 1) * 512], bank(cb % 2)[0:2, :], brow[:, cb * 512:(cb + 1) * 512],
                   ALU.add, ["psb%d" % (cb % 2), "brow"], ["mrow"])
            s2 = P.new_sem()
            dma("sp", mod_d[l], mrow, ["mrow"], ["mod_d"], s2)
            lv = A.alloc([128, 4, 64], F32)
            s3 = P.new_sem()
            dma("sp", lv, lamv[l:l + 1].partition_broadcast(128), (), ["lv"], s3)
            pr = A.alloc([128, 2, 64], F32)
            tt("dve", pr[:, 0, :], lv[:, 0, :], lv[:, 1, :], ALU.mult, ["lv"], ["pr"])
            tt("dve", pr[:, 1, :], lv[:, 2, :], lv[:, 3, :], ALU.mult, ["lv"], ["pr"])
            P.add("dve", lambda e, pr=pr: e.reduce_sum(out=small[:, 0:1], in_=pr[:, 0, :], axis=AX.X), ["pr"], ["sm0"])
            P.add("dve", lambda e, pr=pr: e.reduce_sum(out=small[:, 1:2], in_=pr[:, 1, :], axis=AX.X), ["pr"], ["sm1"])
            act(small[:, 2:4], small[:, 0:2], AF.Exp, ["sm0", "sm1"], ["sm2"])
            tt("dve", small[:, 4:5], small[:, 3:4], small[:, 2:3], ALU.subtract, ["sm2"], ["sm4"])
            ts("dve", neglam[:], small[:, 4:5], -lam_init, None, ALU.add, None, ["sm4"], ["neglam"])
            wsn = A.alloc([128, 4, 128], BF16)
            s4 = P.new_sem()
            dma("pool", wsn, w_sp[l].rearrange("g p q -> p g q"), (), ["wsn"], s4)
            for g in range(4):
                tr(bank16(2)[:, g * 128:(g + 1) * 128], wsn[:, g, :], ident[:], ["wsn", "ident"], ["psb2"])
            cp("dve", wsT[:].rearrange("p g q -> p (g q)"), bank16(2)[:, 0:512], ["psb2"], ["wsT"])
            tbt = A.alloc([128, 8, 9 * 128], F32)
            mk = A.alloc([128, 5, 640], F32)
            mst = A.alloc([128, 8 * 5, 640], BF16)
            s5, s6, s7 = P.new_sem(), P.new_sem(), P.new_sem()
            for h in range(8):
                dma("sp", tbt[:, h, :].rearrange("p (o q) -> p o q", o=9), tb[l, h].rearrange("o k q -> k o q"),
                    (), ["tbt%d" % h], None)
            dma("sp", mk, maskc.rearrange("v k c -> k v c"), (), ["mk"], s6)
            OB = [0, 4, 3, 1, 0]
            for h in range(8):
                for v in range(5):
                    tt("pool" if (h * 5 + v) % 2 else "dve", mst[:, h * 5 + v, :], tbt[:, h, OB[v] * 128:OB[v] * 128 + 640],
                       mk[:, v, :], ALU.add, ["tbt%d" % h, "mk"], ["mst%d" % (h * 5 + v)])
            dma("sp", M_d, mst[:].rearrange("p a b -> p (a b)"), ["mst%d" % i_ for i_ in range(40)], ["M_d"], s7)

            if stop_after < 1:
                break
            P.barrier()
            A.reset()
            P.reset_sems()
            win = A.alloc([128, 8, DIN], BF16)
            wsrc = w_in[l]
            for (a, b, nm) in ((0, 1536, "win_a"), (2560, DIN, "win_b")):
                dma("pool", win[:, :, a:b], wsrc[:, a:b].rearrange("(c p) n -> p c n", p=128), (), [nm], None)
            dma("pool", win[:, :, 1536:2560], wperm_d[l].rearrange("(c p) n -> p c n", p=128), ["wperm%d" % l], ["win_q"], None)
            sgg = A.alloc([128, 512], F32)
            sgb = A.alloc([128, 512], F32)
            brep = A.alloc([128, 4, 128], F32)
            modc = A.alloc([128, 2, 2, 8], F32)
            dma("sp", sgg, sg_g[l:l + 1, :].partition_broadcast(128), (), ["sgg"], None)
            dma("sp", sgb, sg_b[l:l + 1, :].partition_broadcast(128), (), ["sgb"], None)
            dma("sp", brep, b_sp[l:l + 1].partition_broadcast(128), (), ["brep"], None)
            for cs in range(2):
                for k in range(2):
                    dma("sp", modc[:, cs, k, :], mod_d[l, cs, k * D:(k + 1) * D].rearrange("(c p) -> p c", p=128),
                        ["mod_d"], ["modc"], None, slow=True)
            ts("dve", modc[:, :, 1, :], modc[:, :, 1, :], 1.0, None, ALU.add, None, ["modc"], ["modc"])

            xs = [A.alloc([128, D], F32) for _ in range(2)]
            xs_s = [P.new_sem() for _ in range(2)]
            xn = [A.alloc([128, D], BF16) for _ in range(2)]
            hT = [A.alloc([128, 8, 512], BF16) for _ in range(2)]
            hT_s = [P.new_sem() for _ in range(2)]
            rc = A.alloc([128, 512], F32)
            rs = A.alloc([128, 512], F32)
            rope_s = P.new_sem()
            fm = {n: A.alloc([128, 4, 512], BF16) for n in ("qb", "kb", "ya")}
            fm_s = {n: P.new_sem() for n in fm}
            fm["qc"], fm["kc"] = fm["qb"], fm["kb"]
            fm_s["qc"], fm_s["kc"] = fm_s["qb"], fm_s["kb"]
            fmr = dict(qb="fm_qb", kb="fm_kb", qc="fm_qb", kc="fm_kb", ya="fm_ya")
            tm = [[A.alloc([128, w_], BF16) for w_ in (520, 512, 528, 512)] for _ in range(2)]
            for k_ in range(2):
                P.add("pool", lambda e, a=tm[k_][0]: e.memset(a, 1.0), (), ["tm%d" % k_])
                P.add("pool", lambda e, a=tm[k_][2]: e.memset(a, 1.0), (), ["tm%d" % k_])
            tm_s = [P.new_sem() for _ in range(2)]
            of32 = [A.alloc([128, 512], F32)] * 2
            of_s = [P.new_sem()] * 2
            vn = A.alloc([128, 4, 512], BF16)
            tA = [A.alloc([128, 512], F32) for _ in range(2)]
            tB = [A.alloc([128, 512], F32) for _ in range(2)]
            tC = [A.alloc([128, 512], F32) for _ in range(2)]
            qraw = [A.alloc([128, 512], BF16) for _ in range(2)]
            ofb = [(of32[0], "of0", of_s[0]), (tA[0], "tA0", P.new_sem()), (tA[1], "tA1", P.new_sem()),
                   (tB[0], "tB0", P.new_sem()), (tB[1], "tB1", P.new_sem())]
            st6 = A.alloc([128, 2, 6], F32)
            mv = A.alloc([128, 8], F32)

            cnt = dict(x=0, bank=0, t=0, q=0, of=0, tm=0)

            def nbank():
                b = 2 + cnt["bank"] % 6
                cnt["bank"] += 1
                return b

            def layer_norm_stats(src_lo, src_hi, rd, tag):
                P.add("dve", lambda e, o=st6[:, 0, :]: e.bn_stats(out=o, in_=src_lo), rd, ["st6"])
                P.add("dve", lambda e, o=st6[:, 1, :]: e.bn_stats(out=o, in_=src_hi), rd, ["st6"])
                P.add("dve", lambda e, o=mv[:, 0:2], i=st6[:].rearrange("p a b -> p (a b)"): e.bn_aggr(out=o, in_=i),
                      ["st6"], ["mv01"])
                ts("dve", mv[:, 2:3], mv[:, 1:2], EPS, None, ALU.add, None, ["mv01"], ["mv2"])
                rstd_pow(mv[:, 3:4], mv[:, 2:3], ["mv2", "nhalf"], ["mv3"])

            flat = [(gi_, sti_) for gi_ in range(len(groups)) for sti_ in range(4)]

            def p1_xload(fi):
                gi_, sti_ = flat[fi]
                k_ = fi % 2
                t0_ = groups[gi_][0] + sti_ * 128
                dma("sp", xs[k_], x_src[t0_:t0_ + 128, :], ["x_d%d" % gi_], ["xs%d" % k_], xs_s[k_])

            def p1_ln(gi_):
                cs_ = 1 if groups[gi_][1] else 0
                tok0_ = groups[gi_][0]
                hk_ = gi_ % 2
                hres_ = "hT%d" % hk_
                for sti_ in range(4):
                    fi = gi_ * 4 + sti_
                    k = fi % 2
                    layer_norm_stats(xs[k][:, 0:512], xs[k][:, 512:1024], ["xs%d" % k], "x")
                    ts("dve", xn[k], xs[k], mv[:, 0:1], mv[:, 3:4], ALU.subtract, ALU.mult,
                       ["xs%d" % k, "mv01", "mv3"], ["xn%d" % k])
                    if fi + 2 < len(flat):
                        p1_xload(fi + 2)
                    tb_ = sti_ % 2
                    for c in range(8):
                        tr(bank16(tb_)[:, c * 128:(c + 1) * 128], xn[k][:, c * 128:(c + 1) * 128], ident[:],
                           ["xn%d" % k, "ident"], ["psb%d" % tb_])
                    for c in range(8):
                        act(hT[hk_][:, c, sti_ * 128:(sti_ + 1) * 128], bank16(tb_)[:, c * 128:(c + 1) * 128], AF.Identity,
                            ["psb%d" % tb_, "modc"], [hres_], scale=modc[:, cs_, 1, c:c + 1], bias=modc[:, cs_, 0, c:c + 1])
                dma("sp", hT_d[:, tok0_:tok0_ + 512].rearrange("(c p) t -> p c t", p=128), hT[hk_],
                    [hres_], ["hT_d%d" % gi_], hT_s[hk_])

            p1_xload(0)
            p1_xload(1)
            p1_ln(0)
            for gi, (tok0, is_p) in enumerate(groups):
                cs = 1 if is_p else 0
                hk = gi % 2
                hres = "hT%d" % hk
                if not is_p:
                    dma("sp", rc, ropec[:, tok0:tok0 + 512], (), ["rc"], rope_s)
                    dma("sp", rs, ropes[:, tok0:tok0 + 512], (), ["rs"], rope_s)

                def win_res(col0, kc):
                    if col0 < 1536:
                        return ["win_a"]
                    if col0 >= 2560:
                        return ["win_b"]
                    return ["win_q"]

                def proj_fm(col0, b):
                    for kc in range(8):
                        mm(bank(b), win[:, kc, col0:col0 + 128], hT[hk][:, kc, :], kc == 0, kc == 7,
                           win_res(col0, kc) + [hres], ["psb%d" % b])

                def proj_tm(col0, sti, b):
                    for kc in range(8):
                        mm(bank(b), hT[hk][:, kc, sti * 128:(sti + 1) * 128], win[:, kc, col0:col0 + 512],
                           kc == 0, kc == 7, win_res(col0, kc) + [hres], ["psb%d" % b])

                for sti in range(4):
                    b = nbank()
                    proj_tm(OFF["va"], sti, b)
                    pb = "psb%d" % b
                    layer_norm_stats(bank(b)[:, 0:256], bank(b)[:, 256:512], [pb], "v")
                    k = cnt["t"] % 2
                    cnt["t"] += 1
                    ts("dve", tA[k], bank(b), mv[:, 0:1], mv[:, 3:4], ALU.subtract, ALU.mult,
                       [pb, "mv01", "mv3"], ["tA%d" % k])
                    tt("pool", tA[k], tA[k], sgg, ALU.mult, ["tA%d" % k, "sgg"], ["tA%d" % k])
                    tt("pool", vn[:, sti, :], tA[k], sgb, ALU.add, ["tA%d" % k, "sgb"], ["vn"])
                for g in range(4):
                    k = cnt["t"] % 2
                    cnt["t"] += 1
                    bu = nbank()
                    proj_fm(OFF["ua"] + g * 128, bu)
                    act(tA[k], bank(bu), AF.Copy, ["psb%d" % bu], ["tA%d" % k], scale=0.5)
                    bg = nbank()
                    proj_fm(OFF["ga"] + g * 128, bg)
                    act(tB[k], bank(bg), AF.Tanh, ["psb%d" % bg], ["tB%d" % k], scale=0.5)
                    stt(tB[k], tB[k], 1.0, bank(bg), ALU.add, ALU.mult, ["tB%d" % k, "psb%d" % bg], ["tB%d" % k])
                    bs_ = nbank()
                    for n in range(4):
                        mm(bank(bs_)[:, n * 128:(n + 1) * 128], vn[:, n, g * 128:(g + 1) * 128], wsT[:, g, :], True, True,
                           ["vn", "wsT"], ["psb%d" % bs_])
                    for n in range(4):
                        tt("dve", tC[k][:, n * 128:(n + 1) * 128], bank(bs_)[:, n * 128:(n + 1) * 128], brep[:, g, :],
                           ALU.add, ["psb%d" % bs_, "brep"], ["tC%d" % k])
                    tt("pool", tA[k], tA[k], tB[k], ALU.mult, ["tA%d" % k, "tB%d" % k], ["tA%d" % k])
                    tt("dve", fm["ya"][:, g, :], tA[k], tC[k], ALU.mult, ["tA%d" % k, "tC%d" % k], ["fm_ya"])
                dma("sp", yaT_d[:, tok0:tok0 + 512].rearrange("(c p) t -> p c t", p=128), fm["ya"],
                    ["fm_ya"], ["yaT_d%d" % gi], fm_s["ya"])

                if gi + 1 < len(groups):
                    p1_ln(gi + 1)
                for name in ("qb", "kb"):
                    for h in range(4):
                        b = nbank()
                        proj_fm(OFF[name] + h * 128, b)
                        pb = "psb%d" % b
                        if is_p:
                            act(fm[name][:, h, :], bank(b), AF.Copy, [pb], [fmr[name]])
                        else:
                            k = cnt["q"] % 2
                            cnt["q"] += 1
                            act(qraw[k], bank(b), AF.Copy, [pb], ["qraw%d" % k])
                            b2 = nbank()
                            mm(bank(b2), prot[:], qraw[k], True, True, ["prot", "qraw%d" % k], ["psb%d" % b2])
                            act(tA[k], bank(b), AF.Copy, [pb], ["tA%d" % k])
                            act(tB[k], bank(b2), AF.Copy, ["psb%d" % b2], ["tB%d" % k])
                            tt("dve", tA[k], tA[k], rc, ALU.mult, ["tA%d" % k, "rc"], ["tA%d" % k])
                            tt("pool", tB[k], tB[k], rs, ALU.mult, ["tB%d" % k, "rs"], ["tB%d" % k])
                            tt("pool", fm[name][:, h, :], tA[k], tB[k], ALU.add, ["tA%d" % k, "tB%d" % k], [fmr[name]])
                    dst = (qbT_d if name == "qb" else kbT_d)[:, :, tok0:tok0 + 512]
                    dma("sp", dst, fm[name], [fmr[name]], ["%sT_d%d" % (name, gi)], fm_s[name])
                for name in ("qc", "kc"):
                    for hp in range(4):
                        b = nbank()
                        proj_fm(OFF[name] + hp * 128, b)
                        act(fm[name][:, hp, :], bank(b), AF.Copy, ["psb%d" % b], [fmr[name]])
                    dst = (qcT_d if name == "qc" else kcT_d)[:, :, tok0:tok0 + 512]
                    dma("sp", dst, fm[name], [fmr[name]], ["%sT_d%d" % (name, gi)], fm_s[name])

                for sti in range(4):
                    k = cnt["tm"] % 2
                    cnt["tm"] += 1
                    tmr = "tm%d" % k
                    t0 = tok0 + sti * 128
                    sq, tq = sti // 2, (sti % 2) * 128

                    def out_f32(b, pairs):
                        ko = cnt["of"] % 5
                        cnt["of"] += 1
                        buf, res, sem_ = ofb[ko]
                        act(buf, bank(b), AF.Copy, ["psb%d" % b], [res])
                        for dst_ap, src_view in pairs:
                            dma("sp", dst_ap, src_view(buf), [res], ["outs"], sem_)

                    for ai, name in enumerate(("vb", "gb", "vc", "gc")):
                        b = nbank()
                        proj_tm(OFF[name], sti, b)
                        pb = "psb%d" % b
                        if name in ("vb", "vc"):
                            if name == "vb":
                                act(tm[k][0].rearrange("p (h v) -> p h v", h=4)[:, :, 0:128],
                                    bank(b).rearrange("p (h v) -> p h v", h=4), AF.Copy, [pb], [tmr])
                            else:
                                act(tm[k][2].rearrange("p (h v) -> p h v", h=8)[:, :, 0:64],
                                    bank(b).rearrange("p (h v) -> p h v", h=8), AF.Copy, [pb], [tmr])
                            if is_p:
                                if name == "vb":
                                    out_f32(b, [(o_dv[sq, l, :, tq:tq + 128, :].rearrange("h t v -> t h v"),
                                                 lambda a: a.rearrange("p (h v) -> p h v", h=4))])
                                else:
                                    out_f32(b, [(o_nv[sq, l, :, tq:tq + 128, :].rearrange("h t v -> t h v"),
                                                 lambda a: a.rearrange("p (h v) -> p h v", h=8))])
                        else:
                            kk = cnt["t"] % 2
                            cnt["t"] += 1
                            act(tC[kk], bank(b), AF.Tanh, [pb], ["tC%d" % kk], scale=0.5)
                            stt(tm[k][ai], tC[kk], 1.0, bank(b), ALU.add, ALU.mult, ["tC%d" % kk, pb], [tmr])
                    if is_p:
                        b = nbank()
                        proj_tm(OFF["kb"], sti, b)
                        out_f32(b, [(o_dk[sq, l, m_, :, tq:tq + 128, :].rearrange("h t d -> t h d"),
                                     (lambda a, m_=m_: a.rearrange("p (h m d) -> p h m d", h=4, m=2)[:, :, m_, :]))
                                    for m_ in range(2)])
                        b = nbank()
                        proj_tm(OFF["kc"], sti, b)
                        out_f32(b, [(o_nk[sq, l, :, tq:tq + 128, :].rearrange("h t v -> t h v"),
                                     lambda a: a.rearrange("p (h v) -> p h v", h=8))])
                    for ai, dd in enumerate((vb_d, gb_d, vc_d, gc_d)):
                        dma("sp", dd[t0:t0 + 128, :], tm[k][ai], [tmr], ["tmd%d_%d" % (ai, gi)], tm_s[k])

            if l == 0:
                print("arena P1 KB", A.off / 512.0)
            if stop_after < 2:
                break
            P.barrier()
            A.reset()
            P.reset_sems()
            KbT = A.alloc([128, 4, NKB * 128], BF16)
            Vb = A.alloc([128, NKB, 4 * 130], BF16)
            gsub = A.alloc([128, 128], F32)
            P.add("pool", lambda e, a=Vb[:, NKS:NKS + NKC, :]: e.memset(a, 1.0), (), ["Vb_c"])
            dma("sp", KbT[:, :, 0:NT], kbT_d[:, :, 0:NT], ["kbT_d%d" % g for g in range(NG)], ["KbT_s"], None)
            dma("sp", KbT[:, :, NT + PAST:NT + PAST + NP], kbT_d[:, :, NT:NTOT], ["kbT_d%d" % NG], ["KbT_p"], None)
            Vb4 = Vb.rearrange("p k (h v) -> p k h v", h=4)
            for k0 in range(0, NKS, 8):
                dma("sp", Vb[:, k0:k0 + 8, :], vb_d[k0 * 128:(k0 + 8) * 128, :].rearrange("(k p) f -> p k f", p=128),
                    ["tmd0_%d" % g for g in range(NG)], ["Vb_s%d" % (k0 // 8)], None)
            dma("sp", Vb[:, NKS + NKC:NKB, :], vb_d[NT:NTOT, :].rearrange("(k p) f -> p k f", p=128),
                ["tmd0_%d" % NG], ["Vb_p"], None)
            dma("sp", gsub, subg[l:l + 1, :].partition_broadcast(128), (), ["gsub"], None)
            ts("dve", gsub, gsub, (1.0 - lam_init) * 0.5, None, ALU.mult, None, ["gsub"], ["gsub"])
            for h_ in range(4):
                dma("pool", Vb4[:, NKS:NKS + NKC, h_, 0:128], cdv[l, h_].rearrange("(k p) v -> p k v", p=128),
                    (), ["Vb_c"], None)
            ckt = A.alloc([128, NKC, 4 * 128], BF16)
            ckt5 = ckt.rearrange("p k (h m d) -> p k h m d", h=4, m=2)
            for m in range(2):
                for h in range(4):
                    dma("pool", ckt5[:, :, h, m, :], cdk[l, m, h].rearrange("(k p) d -> p k d", p=128),
                        (), ["ckt"], None)
            def p2a_cache_k():
                for kb in range(NKC):
                    for h in range(4):
                        tr(bank16(7)[:, h * 128:(h + 1) * 128], ckt[:, kb, h * 128:(h + 1) * 128], ident[:],
                           ["ckt", "ident"], ["psb7"])
                    for h in range(4):
                        cp("dve", KbT[:, h, NT + kb * 128:NT + (kb + 1) * 128], bank16(7)[:, h * 128:(h + 1) * 128],
                           ["psb7"], ["KbT_c"])

            if l + 1 < depth:
                issue_wperm(l + 1)
            qt = [A.alloc([128, 4, 512], BF16) for _ in range(2)]
            qt_s = [P.new_sem() for _ in range(2)]
            gbt = [A.alloc([128, 4, 512], BF16) for _ in range(2)]
            pT2 = [A.alloc([128, 2, 512], BF16) for _ in range(3)]
            pT = [[pT2[i_][:, m_, :] for i_ in range(3)] for m_ in range(2)]
            ybt2 = [A.alloc([128, 4, 512], BF16) for _ in range(2)]
            ybT_st = A.alloc([128, 4, 512], BF16)
            ybT_s = P.new_sem()
            otmp = [A.alloc([128, 128], F32) for _ in range(8)]
            ofin = [A.alloc([128, 128], F32) for _ in range(8)]
            junk = A.alloc([128, 128], F32)
            sv_ = [A.alloc([128, 8], F32) for _ in range(8)]

            def kres(kb):
                return "KbT_s" if kb < NKS else ("KbT_c" if kb < NKS + NKC else "KbT_p")

            def vres(kb):
                return ("Vb_s%d" % (kb // 8)) if kb < NKS else ("Vb_c" if kb < NKS + NKC else "Vb_p")

            def acc_ap(i, n):
                return ps[:, (4 + i // 3) * 512 + (i % 3) * 130:(4 + i // 3) * 512 + (i % 3) * 130 + n]

            qtiles = [(g * 512, 512, list(range(NKS + NKC)), g) for g in range(NG)]
            for s in range(NPS):
                qtiles.append((NT + s * SEQ, SEQ, [NKS + NKC + 2 * s, NKS + NKC + 2 * s + 1], NG))
            cntp = dict(s=0, p=0, f=0)
            def p2a_loads(ti_):
                tok0_, N_, _, gi_ = qtiles[ti_]
                k_ = ti_ % 2
                dma("sp", qt[k_][:, :, 0:N_], qbT_d[:, :, tok0_:tok0_ + N_], ["qbT_d%d" % gi_], ["qt%d" % k_], qt_s[k_])
                dma("sp", gbt[k_][:, 0:N_ // 128, :], gb_d[tok0_:tok0_ + N_, :].rearrange("(s p) f -> p s f", p=128),
                    ["tmd1_%d" % gi_], ["gbt%d" % k_], qt_s[k_])

            def p2a_tail(ti_):
                tok0_, N_, _, gi_ = qtiles[ti_]
                yb_, ybr_ = ybt2[ti_ % 2], "ybt%d" % (ti_ % 2)
                for hc in range(4):
                    for qs in range(N_ // 128):
                        tr(bank16(7)[:, qs * 128:(qs + 1) * 128], yb_[:, qs, hc * 128:(hc + 1) * 128], ident[:],
                           [ybr_, "ident"], ["psb7"])
                    cp("dve", ybT_st[:, hc, 0:N_], bank16(7)[:, 0:N_], ["psb7"], ["ybT_st"])
                dma("sp", ybT_d[:, tok0_:tok0_ + N_].rearrange("(c p) t -> p c t", p=128), ybT_st[:, :, 0:N_],
                    ["ybT_st"], ["ybT_d%d" % gi_], ybT_s)

            p2a_loads(0)
            for ti, (tok0, N, kblocks, gi) in enumerate(qtiles):
                k = ti % 2
                nsub = N // 128
                ybt, ybr = ybt2[ti % 2], "ybt%d" % (ti % 2)
                if ti + 1 < len(qtiles):
                    p2a_loads(ti + 1)
                items = [(h, kbi, kb) for h in range(4) for kbi, kb in enumerate(kblocks)]
                nlast = len(kblocks) - 1
                started = set()

                def qk_exp(idx):
                    h, kbi, kb = items[idx]
                    sp_ = (cntp["s"] + idx) % 2
                    pk = (cntp["p"] + idx) % 3
                    for m in range(2):
                        r0 = m * 64
                        b = sp_ * 2 + m
                        mm(bank(b)[:, 0:N], KbT[r0:r0 + 64, h, kb * 128:(kb + 1) * 128], qt[k][r0:r0 + 64, h, 0:N],
                           True, True, [kres(kb), "qt%d" % k], ["psb%d" % b])
                    for m in range(2):
                        b = sp_ * 2 + m
                        act(pT[m][pk][:, 0:N], bank(b)[:, 0:N], AF.Exp, ["psb%d" % b], ["pT%d_%d" % (m, pk)], scale=0.125)

                def finalize(h):
                    for qs in range(nsub):
                        f = qs + 4 * (h % 2)
                        i0, i1 = qs, 4 + qs
                        rd = ["psb%d" % (4 + i0 // 3), "psb%d" % (4 + i1 // 3)]
                        s = sv_[f]
                        sr = "sv%d" % f
                        P.add("dve", lambda e, s=s, a=acc_ap(i0, 130): e.reciprocal(out=s[:, 0:1], in_=a[:, 128:129]), rd, [sr + "a"])
                        P.add("dve", lambda e, s=s, a=acc_ap(i1, 130): e.reciprocal(out=s[:, 1:2], in_=a[:, 128:129]), rd, [sr + "b"])
                        tt("dve", s[:, 2:3], s[:, 1:2], neglam[:], ALU.mult, [sr + "b", "neglam"], [sr + "c"])
                        ts("dve", otmp[f], acc_ap(i1, 128), s[:, 2:3], None, ALU.mult, None, rd + [sr + "c"], ["otmp%d" % f])
                        stt(ofin[f], acc_ap(i0, 128), s[:, 0:1], otmp[f], ALU.mult, ALU.add, rd + [sr + "a", "otmp%d" % f],
                            ["ofin%d" % f])
                    for qs in range(nsub):
                        f = qs + 4 * (h % 2)
                        s = sv_[f]
                        sr = "sv%d" % f
                        act(junk, ofin[f], AF.Square, ["ofin%d" % f], ["junk", sr + "d"], accum=s[:, 3:4])
                        ts("dve", s[:, 4:5], s[:, 3:4], 1.0 / 128.0, EPS, ALU.mult, ALU.add, [sr + "d"], [sr + "e"])
                        rstd_pow(s[:, 5:6], s[:, 4:5], [sr + "e", "nhalf"], [sr + "f"])
                        stt(ofin[f], ofin[f], s[:, 5:6], gsub, ALU.mult, ALU.mult, ["ofin%d" % f, sr + "f", "gsub"], ["ofin%d" % f])
                        tt("pool", ybt[:, qs, h * 128:(h + 1) * 128], ofin[f], gbt[k][:, qs, h * 128:(h + 1) * 128], ALU.mult,
                           ["ofin%d" % f, "gbt%d" % k], [ybr])

                def pv(idx):
                    h, kbi, kb = items[idx]
                    pk = (cntp["p"] + idx) % 3
                    if kbi == 0:
                        started.clear()
                    for m in range(2):
                        for qs in range(nsub):
                            i = m * 4 + qs
                            bk = 4 + i // 3
                            first = bk not in started
                            started.add(bk)
                            mm(acc_ap(i, 130), pT[m][pk][:, qs * 128:(qs + 1) * 128], Vb[:, kb, h * 130:(h + 1) * 130],
                               first, kbi == nlast, ["pT%d_%d" % (m, pk), vres(kb)], ["psb%d" % bk])
                    if kbi == nlast:
                        finalize(h)

                qk_exp(0)
                for idx in range(len(items)):
                    if idx + 1 < len(items):
                        qk_exp(idx + 1)
                    pv(idx)
                    if idx == min(5, len(items) - 1) and ti > 0:
                        p2a_tail(ti - 1)
                    if idx == 3 and ti == 0:
                        p2a_cache_k()
                cntp["s"] += len(items)
                cntp["p"] += len(items)
            p2a_tail(len(qtiles) - 1)

            if l == 0:
                print("arena P2a KB", A.off / 512.0)
            if stop_after < 3:
                break
            P.barrier()
            A.reset()
            P.reset_sems()
            Mt = A.alloc([128, 40, 640], BF16)
            dma("sp", Mt[:].rearrange("p a b -> p (a b)"), M_d, ["M_d"], ["Mt"], None)
            kctx = A.alloc([128, 4, PAST + NP], BF16)
            vctx = A.alloc([128, NKC + NKP, 8 * 66], BF16)
            vctx4 = vctx.rearrange("p k (h v) -> p k h v", h=8)
            P.add("pool", lambda e, a=vctx: e.memset(a, 1.0), (), ["vctx"])
            dma("sp", kctx[:, :, PAST:PAST + NP], kcT_d[:, :, NT:NTOT], ["kcT_d%d" % NG], ["kctx"], None)
            dma("sp", vctx[:, NKC:NKC + NKP, :], vc_d[NT:NTOT, :].rearrange("(k p) f -> p k f", p=128),
                ["tmd2_%d" % NG], ["vctx"], None)
            for h_ in range(8):
                dma("pool", vctx4[:, 0:NKC, h_, 0:64], cnv[l, h_].rearrange("(k p) v -> p k v", p=128), (), ["vctx"], None)
            cnt_ = A.alloc([128, NKC, 8 * 64], BF16)
            for h_ in range(8):
                dma("pool", cnt_.rearrange("p k (h d) -> p k h d", h=8)[:, :, h_, :], cnk[l, h_].rearrange("(k p) d -> p k d", p=128),
                    (), ["cnt"], None)
            for kb in range(NKC):
                for hp in range(4):
                    tr(bank16(7)[:, hp * 128:(hp + 1) * 128], cnt_[:, kb, hp * 128:(hp + 1) * 128], ident[:],
                       ["cnt", "ident"], ["psb7"])
                for hp in range(4):
                    cp("dve", kctx[:, hp, kb * 128:(kb + 1) * 128], bank16(7)[:, hp * 128:(hp + 1) * 128],
                       ["psb7"], ["kctx"])
            kw = [A.alloc([128, 4, 1024], BF16) for _ in range(2)]
            vw = [A.alloc([128, 8, 8 * 66], BF16) for _ in range(2)]
            qct = [A.alloc([128, 4, 512], BF16) for _ in range(2)]
            gct = [A.alloc([128, 4, 512], BF16) for _ in range(2)]
            w_s = [P.new_sem() for _ in range(2)]
            pc = [A.alloc([128, 512], BF16) for _ in range(4)]
            pl = [A.alloc([128, 640], BF16) for _ in range(3)]
            slf = [A.alloc([128, 640], F32) for _ in range(3)]
            yct2 = [A.alloc([128, 4, 512], BF16) for _ in range(2)]
            ycT_st = A.alloc([128, 4, 512], BF16)
            ycT_s = P.new_sem()
            rv = [A.alloc([128, 4], F32) for _ in range(2)]
            cq = dict(s=0, p=0, l=0, f=0)

            def accc(hh, qs, n):
                return ps[:, (4 + hh) * 512 + qs * 66:(4 + hh) * 512 + qs * 66 + n]

            ntiles = NG + NPS

            def p2b_geom(ti_):
                if ti_ < NG:
                    g_ = ti_
                    wb0_ = max(0, 4 * g_ - 2)
                    wb1_ = min(NKS - 1, 4 * g_ + 5)
                    return g_ * 512, 512, g_, wb0_, wb1_, list(range(NKC))
                s_ = ti_ - NG
                return NT + s_ * SEQ, SEQ, NG, 0, 0, [NKC + 2 * s_, NKC + 2 * s_ + 1]

            def p2b_loads(ti_):
                tok0_, N_, gi_, wb0_, wb1_, _ = p2b_geom(ti_)
                k_ = ti_ % 2
                if ti_ < NG:
                    nwb_ = wb1_ - wb0_ + 1
                    dma("sp", kw[k_][:, :, 0:nwb_ * 128], kcT_d[:, :, wb0_ * 128:(wb1_ + 1) * 128],
                        ["kcT_d%d" % x for x in range(NG)], ["kw%d" % k_], w_s[k_])
                    dma("sp", vw[k_][:, 0:nwb_, :],
                        vc_d[wb0_ * 128:(wb1_ + 1) * 128, :].rearrange("(k p) f -> p k f", p=128),
                        ["tmd2_%d" % x for x in range(NG)], ["vw%d" % k_], w_s[k_])
                dma("sp", qct[k_][:, :, 0:N_], qcT_d[:, :, tok0_:tok0_ + N_], ["qcT_d%d" % gi_], ["qct%d" % k_], w_s[k_])
                dma("sp", gct[k_][:, 0:N_ // 128, :], gc_d[tok0_:tok0_ + N_, :].rearrange("(s p) f -> p s f", p=128),
                    ["tmd3_%d" % gi_], ["gct%d" % k_], w_s[k_])

            def p2b_tail(ti_):
                tok0_, N_, gi_, _, _, _ = p2b_geom(ti_)
                yc_, ycr_ = yct2[ti_ % 2], "yct%d" % (ti_ % 2)
                for hc in range(4):
                    for qs in range(N_ // 128):
                        tr(bank16(7)[:, qs * 128:(qs + 1) * 128], yc_[:, qs, hc * 128:(hc + 1) * 128], ident[:],
                           [ycr_, "ident"], ["psb7"])
                    cp("dve", ycT_st[:, hc, 0:N_], bank16(7)[:, 0:N_], ["psb7"], ["ycT_st"])
                dma("sp", ycT_d[:, tok0_:tok0_ + N_].rearrange("(c p) t -> p c t", p=128), ycT_st[:, :, 0:N_],
                    ["ycT_st"], ["ycT_d%d" % gi_], ycT_s)

            p2b_loads(0)
            for ti in range(ntiles):
                k = ti % 2
                is_p = ti >= NG
                g = ti
                tok0, N, gi, wb0, wb1, cblocks = p2b_geom(ti)
                nsub = N // 128
                yct, ycr = yct2[ti % 2], "yct%d" % (ti % 2)
                if ti + 1 < ntiles:
                    p2b_loads(ti + 1)
                items = []
                for hp in range(4):
                    for hh in range(2):
                        hitems = [("c", hp, hh, ci, cb) for ci, cb in enumerate(cblocks)]
                        if not is_p:
                            hitems += [("l", hp, hh, qs, None) for qs in range(4)]
                        for ii, it in enumerate(hitems):
                            items.append(it + (ii == 0, ii == len(hitems) - 1))

                def local_geom(qs):
                    j = 4 * g + qs
                    jb = min(max(j - 2, 0), NPAIR - 5)
                    if j == 0:
                        v = 1
                    elif j == 1:
                        v = 2
                    elif j == NPAIR - 2:
                        v = 3
                    elif j == NPAIR - 1:
                        v = 4
                    else:
                        v = 0
                    return jb, v

                def front(idx):
                    kind, hp, hh, a1, a2, isf, isl = items[idx]
                    head = 2 * hp + hh
                    r0 = 64 * hh
                    slot = (cq["s"] + idx) % 3
                    b0 = (0, 2, 6)[slot]
                    if kind == "c":
                        cb = a2
                        pk = (cq["p"] + idx) % 4
                        mm(bank(b0)[:, 0:N], kctx[r0:r0 + 64, hp, cb * 128:(cb + 1) * 128], qct[k][r0:r0 + 64, hp, 0:N],
                           True, True, ["kctx", "qct%d" % k], ["psb%d" % b0])
                        act(pc[pk][:, 0:N], bank(b0)[:, 0:N], AF.Exp, ["psb%d" % b0], ["pc%d" % pk], scale=0.125)
                    else:
                        qs = a1
                        jb, v = local_geom(qs)
                        lk = (cq["l"] + idx) % 3
                        for blk in range(5):
                            wi = jb + blk - wb0
                            mm(ps[:, b0 * 512 + blk * 128:b0 * 512 + (blk + 1) * 128],
                               kw[k][r0:r0 + 64, hp, wi * 128:(wi + 1) * 128], qct[k][r0:r0 + 64, hp, qs * 128:(qs + 1) * 128],
                               True, True, ["kw%d" % k, "qct%d" % k], ["psb%d" % b0, "psb%d" % (b0 + 1)])
                        stt(slf[lk], ps[:, b0 * 512:b0 * 512 + 640], 0.125, Mt[:, head * 5 + v, :], ALU.mult, ALU.add,
                            ["psb%d" % b0, "psb%d" % (b0 + 1), "Mt"], ["slf%d" % lk])
                        act(pl[lk], slf[lk], AF.Exp, ["slf%d" % lk], ["pl%d" % lk])

                def back(idx):
                    kind, hp, hh, a1, a2, isf, isl = items[idx]
                    head = 2 * hp + hh
                    accr = "psb%d" % (4 + hh)
                    if kind == "c":
                        cb = a2
                        pk = (cq["p"] + idx) % 4
                        for qs in range(nsub):
                            mm(accc(hh, qs, 66), pc[pk][:, qs * 128:(qs + 1) * 128], vctx[:, cb, head * 66:(head + 1) * 66],
                               isf and qs == 0, isl, ["pc%d" % pk, "vctx"], [accr])
                    else:
                        qs = a1
                        jb, v = local_geom(qs)
                        lk = (cq["l"] + idx) % 3
                        for blk in range(5):
                            wi = jb + blk - wb0
                            mm(accc(hh, qs, 66), pl[lk][:, blk * 128:(blk + 1) * 128], vw[k][:, wi, head * 66:(head + 1) * 66],
                               False, blk == 4, ["pl%d" % lk, "vw%d" % k], [accr])
                    if isl:
                        for qs in range(nsub):
                            f = cq["f"] % 2
                            cq["f"] += 1
                            ts("dve", rv[f][:, 0:1], accc(hh, qs, 66)[:, 64:65], 2.0, None, ALU.mult, None, [accr], ["rv%da" % f])
                            P.add("dve", lambda e, r=rv[f]: e.reciprocal(out=r[:, 1:2], in_=r[:, 0:1]), ["rv%da" % f], ["rv%db" % f])
                            stt(yct[:, qs, head * 64:(head + 1) * 64], accc(hh, qs, 64), rv[f][:, 1:2],
                                gct[k][:, qs, head * 64:(head + 1) * 64], ALU.mult, ALU.mult, [accr, "rv%db" % f, "gct%d" % k], [ycr])

                front(0)
                if len(items) > 1:
                    front(1)
                for idx in range(len(items)):
                    if idx + 2 < len(items):
                        front(idx + 2)
                    back(idx)
                    if idx == min(5, len(items) - 1) and ti > 0:
                        p2b_tail(ti - 1)
                cq["s"] += len(items)
                cq["p"] += len(items)
                cq["l"] += len(items)
            p2b_tail(ntiles - 1)

            if l == 0:
                print("arena P2b KB", A.off / 512.0)
            if stop_after < 4:
                break
            P.barrier()
            A.reset()
            P.reset_sems()
            wmg = A.alloc([128, 8, 3 * D], BF16)
            wbr = A.alloc([128, 3, 4 * D], BF16)
            wo = A.alloc([128, 8, D], BF16)
            for c3 in range(3):
                dma("pool", wmg[:, :, c3 * D:(c3 + 1) * D], w_mg[l][:, c3 * D:(c3 + 1) * D].rearrange("(c p) n -> p c n", p=128),
                    (), ["wmg%d" % c3], None)
            for bi, wsrc_ in enumerate((w_bra, w_brb, w_brc)):
                dma("pool", wbr[:, bi, :].rearrange("p (c n) -> p c n", c=4), wsrc_[l].rearrange("(c p) n -> p c n", p=128),
                    (), ["wbr%d" % bi], None)
            dma("pool", wo, w_o[l].rearrange("(c p) n -> p c n", p=128), (), ["wo"], None)
            grep = A.alloc([128, 2, D], F32)
            lng = A.alloc([128, D], F32)
            lnb = A.alloc([128, D], F32)
            bmg = A.alloc([128, 24], F32)
            for cs in range(2):
                dma("sp", grep[:, cs, :], mod_d[l, cs:cs + 1, 2 * D:3 * D].partition_broadcast(128), ["mod_d"], ["grep"], None)
            ts("dve", grep, grep, 0.5, None, ALU.mult, None, ["grep"], ["grep"])
            dma("sp", lng, ln_g[l:l + 1, :].partition_broadcast(128), (), ["lng"], None)
            dma("sp", lnb, ln_b[l:l + 1, :].partition_broadcast(128), (), ["lnb"], None)
            dma("sp", bmg, b_mg[l].rearrange("(c p) -> p c", p=128), (), ["bmg"], None, slow=True)
            ts("dve", bmg, bmg, 0.5, None, ALU.mult, None, ["bmg"], ["bmg"])
            h3 = [A.alloc([128, 8, 512], BF16)] * 2
            y3 = [A.alloc([128, 12, 512], BF16)] * 2
            in_s = [P.new_sem()] * 2
            x3 = [A.alloc([128, D], F32) for _ in range(2)]
            x3_s = [P.new_sem() for _ in range(2)]
            tg = [A.alloc([128, 512], F32) for _ in range(3)]
            tp = [A.alloc([128, 512], F32) for _ in range(3)]
            mT = [A.alloc([128, 8, 512], BF16)] * 2
            z3 = [A.alloc([128, D], F32) for _ in range(2)]
            o3 = [A.alloc([128, D], F32) for _ in range(2)]
            o3_s = [P.new_sem() for _ in range(2)]
            st6 = A.alloc([128, 2, 6], F32)
            mv = A.alloc([128, 8], F32)
            c3 = dict(b=0, t=0, x=0)
            def p3_loads(gi_):
                tok0_ = groups[gi_][0]
                k_ = 0
                dma("sp", h3[k_], hT_d[:, tok0_:tok0_ + 512].rearrange("(c p) t -> p c t", p=128), ["hT_d%d" % gi_],
                    ["h3_%d" % k_], in_s[k_])
                for bi_, (dd, nm) in enumerate(((yaT_d, "yaT_d"), (ybT_d, "ybT_d"), (ycT_d, "ycT_d"))):
                    dma("sp", y3[k_][:, bi_ * 4:(bi_ + 1) * 4, :], dd[:, tok0_:tok0_ + 512].rearrange("(c p) t -> p c t", p=128),
                        ["%s%d" % (nm, gi_)], ["y3_%d" % k_], in_s[k_])

            def p3_xload(gi_, sti_):
                t0_ = groups[gi_][0] + sti_ * 128
                kx_ = sti_ % 2
                dma("sp", x3[kx_], x_src[t0_:t0_ + 128, :], ["x_d%d" % gi_], ["x3_%d" % kx_], x3_s[kx_])

            p3_loads(0)
            for gi, (tok0, is_p) in enumerate(groups):
                cs = 1 if is_p else 0
                k = 0
                for oc in range(8):
                    for bi in range(3):
                        pi = c3["b"] % 3
                        c3["b"] += 1
                        bg, bb = 2 * pi, 2 * pi + 1
                        for kc in range(8):
                            mm(bank(bg), wmg[:, kc, bi * D + oc * 128:bi * D + (oc + 1) * 128], h3[k][:, kc, :], kc == 0, kc == 7,
                               ["wmg%d" % bi, "h3_%d" % k], ["psb%d" % bg])
                        for kc in range(4):
                            mm(bank(bb), wbr[:, bi, kc * D + oc * 128:kc * D + (oc + 1) * 128], y3[k][:, bi * 4 + kc, :], kc == 0, kc == 3,
                               ["wbr%d" % bi, "y3_%d" % k], ["psb%d" % bb])
                        ti_ = c3["t"] % 3
                        c3["t"] += 1
                        act(tg[ti_], bank(bg), AF.Tanh, ["psb%d" % bg, "bmg"], ["tg%d" % ti_], scale=0.5,
                            bias=bmg[:, bi * 8 + oc:bi * 8 + oc + 1])
                        stt(tp[bi], tg[ti_], 1.0, bank(bb), ALU.add, ALU.mult, ["tg%d" % ti_, "psb%d" % bb], ["tp%d" % bi])
                    tt("dve", tp[0], tp[0], tp[1], ALU.add, ["tp0", "tp1"], ["tp0"])
                    tt("dve", mT[k][:, oc, :], tp[0], tp[2], ALU.add, ["tp0", "tp2"], ["mT%d" % k])
                p3_xload(gi, 0)
                p3_xload(gi, 1)
                if gi + 1 < len(groups):
                    p3_loads(gi + 1)
                for sti in range(4):
                    t0 = tok0 + sti * 128
                    kx = sti % 2
                    for half in range(2):
                        for kc in range(8):
                            mm(bank(6 + half), mT[k][:, kc, sti * 128:(sti + 1) * 128], wo[:, kc, half * 512:(half + 1) * 512],
                               kc == 0, kc == 7, ["mT%d" % k, "wo"], ["psb%d" % (6 + half)])
                    for half in range(2):
                        act(z3[kx][:, half * 512:(half + 1) * 512], bank(6 + half), AF.Copy, ["psb%d" % (6 + half)], ["z3_%d" % kx])
                    tt("pool", z3[kx], z3[kx], grep[:, cs, :], ALU.mult, ["z3_%d" % kx, "grep"], ["z3_%d" % kx])
                    stt(z3[kx], x3[kx], ALPHA_FULL, z3[kx], ALU.mult, ALU.add, ["x3_%d" % kx, "z3_%d" % kx], ["z3_%d" % kx])
                    zr = ["z3_%d" % kx]
                    P.add("dve", lambda e, a=z3[kx], o=st6[:, 0, :]: e.bn_stats(out=o, in_=a[:, 0:512]), zr, ["st6"])
                    P.add("dve", lambda e, a=z3[kx], o=st6[:, 1, :]: e.bn_stats(out=o, in_=a[:, 512:1024]), zr, ["st6"])
                    P.add("dve", lambda e, o=mv[:, 0:2], i=st6[:].rearrange("p a b -> p (a b)"): e.bn_aggr(out=o, in_=i), ["st6"], ["mv01"])
                    ts("dve", mv[:, 2:3], mv[:, 1:2], EPS, None, ALU.add, None, ["mv01"], ["mv2"])
                    rstd_pow(mv[:, 3:4], mv[:, 2:3], ["mv2", "nhalf"], ["mv3"])
                    ts("dve", z3[kx], z3[kx], mv[:, 0:1], mv[:, 3:4], ALU.subtract, ALU.mult, zr + ["mv01", "mv3"], zr)
                    tt("pool", z3[kx], z3[kx], lng, ALU.mult, zr + ["lng"], zr)
                    tt("pool", o3[kx], z3[kx], lnb, ALU.add, zr + ["lnb"], ["o3_%d" % kx])
                    if sti + 2 < 4:
                        p3_xload(gi, sti + 2)
                    dma("sp", x_dst[t0:t0 + 128, :], o3[kx], ["o3_%d" % kx], ["x_d%d" % gi], o3_s[kx])

        print("arena P3 KB", A.off / 512.0)
        P.emit()
    return nc


def _const_tables(nrows):
    NT = nrows * GW
    t = np.arange(NT)
    half = 32
    inv = (1.0 / (10000.0 ** (np.arange(0, half, 2, dtype=np.float32) / np.float32(half)))).astype(np.float32)
    ang_r = (t // GW).astype(np.float32)[:, None] * inv
    ang_c = (t % GW).astype(np.float32)[:, None] * inv
    cos_r, sin_r, cos_c, sin_c = np.cos(ang_r), np.sin(ang_r), np.cos(ang_c), np.sin(ang_c)
    C = np.zeros((128, NT), np.float32)
    S = np.zeros((128, NT), np.float32)
    for p in range(128):
        d = p % 64
        ax, f = d // 32, d % 16
        C[p] = (cos_r if ax == 0 else cos_c)[:, f]
        S[p] = (sin_r if ax == 0 else sin_c)[:, f]
    npair = nrows // 2
    reps = [2, 0, 1, npair - 2, npair - 1]
    cpos = np.arange(GW)
    cstart = np.clip(cpos - 8, 0, GW - 16)
    colok = (cpos[None, :] >= cstart[:, None]) & (cpos[None, :] < cstart[:, None] + 16)
    mask = np.full((5, 128, 640), NEG, np.float32)
    for v, j in enumerate(reps):
        jb = min(max(j - 2, 0), npair - 5)
        for blk in range(5):
            for kr_ in range(2):
                krow = 2 * (jb + blk) + kr_
                for qr_ in range(2):
                    r = 2 * j + qr_
                    rs = min(max(r - 4, 0), nrows - 8)
                    if rs <= krow < rs + 8:
                        sub = np.where(colok.T, 0.0, NEG).astype(np.float32)
                        mask[v, kr_ * 64:(kr_ + 1) * 64, blk * 128 + qr_ * 64:blk * 128 + (qr_ + 1) * 64] = sub
    return C, S, mask


def _tb_gather(rel_bias):
    o = np.arange(-4, 5)
    kr = np.arange(128) // 64
    kc = np.arange(128) % 64
    dy = np.clip(2 * o[:, None, None] + kr[None, :, None] - kr[None, None, :] + 7, 0, 14)
    dx = np.clip(kc[:, None] - kc[None, :], -15, 15) + 15
    dx = np.broadcast_to(dx[None], dy.shape)
    return np.ascontiguousarray(rel_bias[:, :, dy, dx])


_PROG_CACHE = {}


def _run(nrows, depth, inputs):
    key = (nrows, depth)
    if key not in _PROG_CACHE:
        _PROG_CACHE[key] = build_program(nrows, depth)
    nc = _PROG_CACHE[key]
    f = lambda a: np.ascontiguousarray(np.asarray(a, dtype=np.float32))
    I = {k: f(v) for k, v in inputs.items()}
    ncores = I["x_sample"].shape[0]
    C, S, mask = _const_tables(nrows)
    tbg = _tb_gather(I["na_rel_bias"])
    lamv = np.ascontiguousarray(np.stack([I["lambda_q1"], I["lambda_k1"], I["lambda_q2"], I["lambda_k2"]], axis=1))
    shared = dict(w_ada=I["w_ada"], b_ada=I["b_ada"], w_in=I["w_in"], sg_norm_g=I["sg_norm_g"], sg_norm_b=I["sg_norm_b"],
                  w_spatial=I["w_spatial"], b_spatial=I["b_spatial"], lamv=lamv, diff_subln_g=I["diff_subln_g"],
                  tb=tbg, maskc=mask, ropec=C, ropes=S, w_br_a=I["w_br_a"], w_br_b=I["w_br_b"], w_br_c=I["w_br_c"],
                  w_mgate=I["w_mgate"], b_mgate=I["b_mgate"], w_out=I["w_out"], ln_g=I["ln_g"], ln_b=I["ln_b"])
    in_maps = []
    for b in range(ncores):
        m = dict(shared)
        m["xin"] = np.ascontiguousarray(np.concatenate(
            [I["x_sample"][b], I["x_prompt"][NPS * b:NPS * (b + 1)].reshape(NP, D)], axis=0))
        m["cvec"] = np.ascontiguousarray(np.stack([I["c"][b], I["c_ctx"]], axis=0))
        m["cdk"] = I["cache_diff_k"][b]
        m["cdv"] = I["cache_diff_v"][b]
        m["cnk"] = I["cache_na_k"][b]
        m["cnv"] = I["cache_na_v"][b]
        in_maps.append(m)
    res = run_bass_kernel_spmd(nc, in_maps, core_ids=list(range(ncores)))
    R = res.results
    NT = nrows * GW
    y_s = np.stack([R[b]["y"][:NT] for b in range(ncores)], axis=0)
    y_p = np.concatenate([R[b]["y"][NT:].reshape(NPS, SEQ, D) for b in range(ncores)], axis=0)
    cat = lambda n: np.concatenate([R[b][n] for b in range(ncores)], axis=0)
    return (y_p.astype(np.float32), y_s.astype(np.float32), cat("o_dk").astype(np.float32), cat("o_dv").astype(np.float32),
            cat("o_nk").astype(np.float32), cat("o_nv").astype(np.float32))


def kernel(**inputs):
    return _run(64, 4, inputs)
```

```python
import math
import contextlib
import numpy as np
import concourse.bass as bass
import concourse.mybir as mybir
from concourse.bass_utils import run_bass_kernel_spmd

F32 = mybir.dt.float32
BF16 = mybir.dt.bfloat16
AF = mybir.ActivationFunctionType
ALU = mybir.AluOpType
AX = mybir.AxisListType

D = 1024
DIN = 5632
GW = 64
SEQ = 256
NPS = 2
NP = NPS * SEQ
PAST = 512
EPS = 1e-6
NEG = -30000.0
OFF = dict(ua=0, va=512, ga=1024, qb=1536, kb=2048, vb=2560, gb=3072, qc=3584, kc=4096, vc=4608, gc=5120)
ALPHA_FULL = (2 * 4) ** 0.25
ARENA_KB = 170
N_DMA_SEMS = 72


class Sem:
    def __init__(self, handle):
        self.h = handle
        self.count = 0


class Op:
    __slots__ = ("eng", "fn", "deps", "signal", "sem", "val", "is_dma")

    def __init__(self, eng, fn, is_dma, sem):
        self.eng = eng
        self.fn = fn
        self.deps = []
        self.signal = is_dma
        self.sem = sem
        self.val = None
        self.is_dma = is_dma


class Prog:
    ENGS = ("pe", "act", "dve", "pool", "sp")

    def __init__(self, nc, stack):
        self.nc = nc
        self.ops = {e: [] for e in self.ENGS}
        self.eng_sem = {e: Sem(stack.enter_context(nc.semaphore("es_" + e))) for e in self.ENGS}
        self.dma_sems = [Sem(stack.enter_context(nc.semaphore("ds%d" % i))) for i in range(N_DMA_SEMS)]
        self.sem_i = 0
        self.pool_sems = [Sem(stack.enter_context(nc.semaphore("pq%d" % i))) for i in range(12)]
        self.pool_prev = [None] * 12
        self.pool_i = 0
        self.last_w = {}
        self.readers = {}
        self.last_dma = {}
        self.barrier_ops = {e: [] for e in self.ENGS}

    def reset_sems(self):
        self.sem_i = 0

    def new_sem(self):
        s = self.dma_sems[self.sem_i]
        self.sem_i += 1
        return s

    def barrier(self):
        ops = [self.ops[e][-1] for e in self.ENGS if self.ops[e]]
        ops += list(self.last_dma.values())
        for e in self.ENGS:
            self.barrier_ops[e] = list(ops)

    def add(self, eng, fn, reads=(), writes=(), dma_sem=None):
        is_dma = dma_sem is not None
        deps = {}
        if is_dma and eng == "pool":
            pi = self.pool_i % len(self.pool_sems)
            self.pool_i += 1
            dma_sem = self.pool_sems[pi]
            if self.pool_prev[pi] is not None:
                deps[id(self.pool_prev[pi])] = (self.pool_prev[pi], True)
        op = Op(eng, fn, is_dma, dma_sem if is_dma else self.eng_sem[eng])
        if is_dma and eng == "pool":
            self.pool_prev[pi] = op
        for r in reads:
            w = self.last_w.get(r)
            if w is not None:
                deps[id(w)] = (w, True)
        for wr in writes:
            w = self.last_w.get(wr)
            if w is not None and id(w) not in deps:
                deps[id(w)] = (w, False)
            for rd in self.readers.get(wr, ()):
                if id(rd) not in deps:
                    deps[id(rd)] = (rd, False)
        if self.barrier_ops[eng]:
            for x in self.barrier_ops[eng]:
                deps[id(x)] = (x, True)
            self.barrier_ops[eng] = []
        for d, raw in deps.values():
            if d is op:
                continue
            if d.eng == eng and not d.is_dma and not is_dma:
                if eng == "pe":
                    continue
            d.signal = True
            op.deps.append((d, d.sem.count if d.is_dma else None))
        for r in reads:
            self.readers.setdefault(r, []).append(op)
        for wr in writes:
            self.last_w[wr] = op
            self.readers[wr] = []
        if is_dma:
            self.last_dma[id(dma_sem)] = op
            dma_sem.count += 16
            op.val = dma_sem.count
        self.ops[eng].append(op)
        return op

    def emit(self):
        nc = self.nc
        for e in self.ENGS:
            for op in self.ops[e]:
                if op.is_dma:
                    pass
                elif op.signal:
                    op.sem.count += 1
                    op.val = op.sem.count
        all_sems = self.dma_sems + self.pool_sems

        def run(e, engine):
            waited = {}
            for op in self.ops[e]:
                for d, ov in op.deps:
                    k = id(d.sem)
                    v = d.val if ov is None else ov
                    if waited.get(k, 0) >= v:
                        continue
                    waited[k] = v
                    engine.wait_ge(d.sem.h, v)
                inst = op.fn(engine)
                if op.is_dma:
                    inst.then_inc(op.sem.h, 16)
                elif op.signal:
                    inst.then_inc(op.sem.h, 1)
            if e == "sp":
                for s in all_sems:
                    if s.count > 0 and waited.get(id(s), 0) < s.count:
                        engine.wait_ge(s.h, s.count)
                for e2 in ("pe", "act", "dve", "pool"):
                    s = self.eng_sem[e2]
                    if s.count > 0:
                        engine.wait_ge(s.h, s.count)

        with nc.Block() as block:
            @block.tensor
            def _(eng):
                run("pe", eng)

            @block.scalar
            def _(eng):
                run("act", eng)

            @block.vector
            def _(eng):
                run("dve", eng)

            @block.gpsimd
            def _(eng):
                run("pool", eng)

            @block.sync
            def _(eng):
                run("sp", eng)


class Arena:
    def __init__(self, big, nelem):
        self.big = big
        self.n = nelem
        self.off = 0

    def reset(self):
        self.off = 0

    def alloc(self, shape, dtype):
        nfree = 1
        for s in shape[1:]:
            nfree *= s
        n16 = nfree * (2 if dtype == F32 else 1)
        n16 = (n16 + 15) // 16 * 16
        assert self.off + n16 <= self.n, ("arena overflow", self.off, n16, self.n)
        ap = self.big[:, self.off:self.off + nfree * (2 if dtype == F32 else 1)]
        self.off += n16
        if dtype == F32:
            ap = ap.bitcast(F32)
        if len(shape) == 3:
            ap = ap.rearrange("p (a b) -> p a b", a=shape[1], b=shape[2])
        elif len(shape) == 4:
            ap = ap.rearrange("p (a b c) -> p a b c", a=shape[1], b=shape[2], c=shape[3])
        if shape[0] < 128:
            ap = ap[0:shape[0]]
        return ap


def build_program(nrows, depth, stop_after=99, stage=99):
    NT = nrows * GW
    NG = NT // 512
    NTOT = NT + NP
    NKS = NT // 128
    NKC = PAST // 128
    NKP = NP // 128
    NKB = NKS + NKC + NKP
    NPAIR = nrows // 2

    nc = bass.Bass("TRN2", target_bir_lowering=False)

    def din(name, shape):
        return nc.dram_tensor(name, list(shape), F32, kind="ExternalInput").ap()

    def dout(name, shape):
        return nc.dram_tensor(name, list(shape), F32, kind="ExternalOutput").ap()

    def dscr(name, shape, dt):
        return nc.dram_tensor(name, list(shape), dt).ap()

    xin = din("xin", [NTOT, D])
    cvec = din("cvec", [2, D])
    cdk = din("cdk", [depth, 2, 4, PAST, 64])
    cdv = din("cdv", [depth, 4, PAST, 128])
    cnk = din("cnk", [depth, 8, PAST, 64])
    cnv = din("cnv", [depth, 8, PAST, 64])
    w_ada = din("w_ada", [depth, D, 3 * D])
    b_ada = din("b_ada", [depth, 3 * D])
    w_in = din("w_in", [depth, D, DIN])
    sg_g = din("sg_norm_g", [depth, 512])
    sg_b = din("sg_norm_b", [depth, 512])
    w_sp = din("w_spatial", [depth, 4, 128, 128])
    b_sp = din("b_spatial", [depth, 4, 128])
    lamv = din("lamv", [depth, 4, 64])
    subg = din("diff_subln_g", [depth, 128])
    tb = din("tb", [depth, 8, 9, 128, 128])
    maskc = din("maskc", [5, 128, 640])
    ropec = din("ropec", [128, NT])
    ropes = din("ropes", [128, NT])
    w_bra = din("w_br_a", [depth, 512, D])
    w_brb = din("w_br_b", [depth, 512, D])
    w_brc = din("w_br_c", [depth, 512, D])
    w_mg = din("w_mgate", [depth, D, 3 * D])
    b_mg = din("b_mgate", [depth, 3 * D])
    w_o = din("w_out", [depth, D, D])
    ln_g = din("ln_g", [depth, D])
    ln_b = din("ln_b", [depth, D])

    y_out = dout("y", [NTOT, D])
    o_dk = dout("o_dk", [NPS, depth, 2, 4, SEQ, 64])
    o_dv = dout("o_dv", [NPS, depth, 4, SEQ, 128])
    o_nk = dout("o_nk", [NPS, depth, 8, SEQ, 64])
    o_nv = dout("o_nv", [NPS, depth, 8, SEQ, 64])

    xbuf = dscr("xbuf", [NTOT, D], F32)
    hT_d = dscr("hT_d", [D, NTOT], BF16)
    yaT_d = dscr("yaT_d", [512, NTOT], BF16)
    ybT_d = dscr("ybT_d", [512, NTOT], BF16)
    ycT_d = dscr("ycT_d", [512, NTOT], BF16)
    qbT_d = dscr("qbT_d", [128, 4, NTOT], BF16)
    kbT_d = dscr("kbT_d", [128, 4, NTOT], BF16)
    qcT_d = dscr("qcT_d", [128, 4, NTOT], BF16)
    kcT_d = dscr("kcT_d", [128, 4, NTOT], BF16)
    vb_d = dscr("vb_d", [NTOT, 520], BF16)
    gb_d = dscr("gb_d", [NTOT, 512], BF16)
    vc_d = dscr("vc_d", [NTOT, 528], BF16)
    gc_d = dscr("gc_d", [NTOT, 512], BF16)
    mod_d = dscr("mod_d", [depth, 2, 3 * D], F32)
    M_d = dscr("M_d", [128, 8 * 5 * 640], BF16)

    wperm_d = dscr("wperm_d", [depth, D, 1024], F32)
    groups = [(g * 512, False) for g in range(NG)] + [(NT, True)]

    with contextlib.ExitStack() as st:
        P = Prog(nc, st)
        big = st.enter_context(nc.sbuf_tensor("arena", [128, ARENA_KB * 512], BF16))
        A = Arena(big, ARENA_KB * 512)
        ps = st.enter_context(nc.psum_tensor("ps", [128, 4096], F32))
        identf = st.enter_context(nc.sbuf_tensor("identf", [128, 128], F32))
        ident = st.enter_context(nc.sbuf_tensor("ident", [128, 128], BF16))
        protf = st.enter_context(nc.sbuf_tensor("protf", [128, 128], F32))
        prot = st.enter_context(nc.sbuf_tensor("prot", [128, 128], BF16))
        nhalf = st.enter_context(nc.sbuf_tensor("nhalf", [128, 1], F32))
        ccol = st.enter_context(nc.sbuf_tensor("ccol", [128, 2, 8], F32))
        ctmp = st.enter_context(nc.sbuf_tensor("ctmp", [128, 2, 8], F32))
        sc_col = st.enter_context(nc.sbuf_tensor("sc_col", [128, 8, 2], BF16))
        wsT = st.enter_context(nc.sbuf_tensor("wsT", [128, 4, 128], BF16))
        neglam = st.enter_context(nc.sbuf_tensor("neglam", [128, 1], F32))
        small = st.enter_context(nc.sbuf_tensor("small", [128, 64], F32))

        def bank(i, n=512):
            return ps[:, i * 512:i * 512 + n]

        def bank16(i, n=1024):
            return ps[:, i * 512:(i + 1) * 512].bitcast(BF16)[:, 0:n]

        def mm(out, lhsT, rhs, start, stop, reads, writes):
            return P.add("pe", lambda e: e.matmul(out, lhsT=lhsT, rhs=rhs, start=start, stop=stop,
                                                   skip_group_check=True), reads, writes)

        def tr(out, in_, idn, reads, writes):
            return P.add("pe", lambda e: e.transpose(out=out, in_=in_, identity=idn), reads, writes)

        def act(out, in_, func, reads, writes, scale=1.0, bias=None, accum=None):
            def f(e):
                kw = dict(out=out, in_=in_, func=func, scale=scale)
                if bias is not None:
                    kw["bias"] = bias
                if accum is not None:
                    kw["accum_out"] = accum
                return e.activation(**kw)
            return P.add("act", f, reads, writes)

        def tt(eng, out, in0, in1, op, reads, writes):
            return P.add(eng, lambda e: e.tensor_tensor(out=out, in0=in0, in1=in1, op=op), reads, writes)

        def ts(eng, out, in0, s1, s2, op0, op1, reads, writes):
            if op1 is None:
                return P.add(eng, lambda e: e.tensor_scalar(out=out, in0=in0, scalar1=s1, scalar2=None, op0=op0),
                             reads, writes)
            return P.add(eng, lambda e: e.tensor_scalar(out=out, in0=in0, scalar1=s1, scalar2=s2, op0=op0, op1=op1),
                         reads, writes)

        def stt(out, in0, scalar, in1, op0, op1, reads, writes):
            return P.add("dve", lambda e: e.scalar_tensor_tensor(out=out, in0=in0, scalar=scalar, in1=in1,
                                                                 op0=op0, op1=op1), reads, writes)

        def cp(eng, out, in_, reads, writes):
            return P.add(eng, lambda e: e.tensor_copy(out=out, in_=in_), reads, writes)

        def dma(q, out, in_, reads, writes, sem, slow=False):
            if sem is None and q != "pool":
                sem = P.new_sem()
            elif sem is None:
                sem = P.dma_sems[0]
            if slow:
                return P.add(q, lambda e: e.dma_start(out=out, in_=in_, allow_slow_non_contiguous=True),
                             reads, writes, dma_sem=sem)
            return P.add(q, lambda e: e.dma_start(out=out, in_=in_), reads, writes, dma_sem=sem)

        def rstd_pow(out, in_, reads, writes):
            return P.add("pool", lambda e: e.tensor_tensor(out=out, in0=in_, in1=nhalf[:], op=ALU.pow),
                         reads, writes)

        P.add("pool", lambda e: e.memset(identf[:], 0.0), (), ["identf"])
        P.add("pool", lambda e: e.affine_select(out=identf[:], in_=identf[:], pattern=[[-1, 128]],
                                                compare_op=ALU.not_equal, fill=1.0, base=0,
                                                channel_multiplier=1), ["identf"], ["identf"])
        cp("dve", ident[:], identf[:], ["identf"], ["ident"])
        P.add("pool", lambda e: e.memset(protf[:], 0.0), (), ["protf"])
        for blk in range(4):
            lo = blk * 32
            sl = protf[:, lo:lo + 16]
            P.add("pool", lambda e, sl=sl, lo=lo: e.affine_select(
                out=sl, in_=sl, pattern=[[-1, 16]], compare_op=ALU.not_equal, fill=-1.0,
                base=-(lo + 16), channel_multiplier=1), ["protf"], ["protf"])
            sh = protf[:, lo + 16:lo + 32]
            P.add("pool", lambda e, sh=sh, lo=lo: e.affine_select(
                out=sh, in_=sh, pattern=[[-1, 16]], compare_op=ALU.not_equal, fill=1.0,
                base=-lo, channel_multiplier=1), ["protf"], ["protf"])
        cp("dve", prot[:], protf[:], ["protf"], ["prot"])
        P.add("pool", lambda e: e.memset(nhalf[:], -0.5), (), ["nhalf"])
        s0 = P.new_sem()
        dma("sp", ccol[:], cvec.rearrange("s (c p) -> p s c", p=128), (), ["ccol"], s0, slow=True)
        act(ctmp[:], ccol[:], AF.Tanh, ["ccol"], ["ctmp"], scale=0.5)
        stt(ctmp[:], ctmp[:], 1.0, ccol[:], ALU.add, ALU.mult, ["ctmp", "ccol"], ["ctmp"])
        ts("dve", sc_col[:].rearrange("p c s -> p s c"), ctmp[:], 0.5, None, ALU.mult, None, ["ctmp"], ["sc_col"])

        def issue_wperm(l_):
            for ri, off in enumerate((OFF["qb"], OFF["kb"])):
                for m in range(2):
                    dst = wperm_d[l_][:, ri * 512:(ri + 1) * 512].rearrange("k (h m d) -> k h m d", h=4, m=2, d=64)[:, :, m, :]
                    src = w_in[l_][:, off + m * 256:off + (m + 1) * 256].rearrange("k (h d) -> k h d", h=4)
                    dma("sp", dst, src, (), ["wperm%d" % l_], None)

        issue_wperm(0)

        for l in range(depth):
            lam_init = 0.8 - 0.6 * math.exp(-0.3 * l)
            x_src = xin if l == 0 else xbuf
            x_dst = y_out if l == depth - 1 else xbuf

            P.barrier()
            A.reset()
            P.reset_sems()
            wa = [A.alloc([128, 8, 512], BF16) for _ in range(3)]
            wa_s = [P.new_sem() for _ in range(3)]
            brow = A.alloc([2, 3 * D], F32)
            mrow = A.alloc([2, 3 * D], F32)
            s1 = P.new_sem()
            dma("sp", brow, b_ada[l:l + 1, :].partition_broadcast(2), (), ["brow"], s1)
            for cb in range(6):
                k = cb % 3
                dma("pool", wa[k], w_ada[l][:, cb * 512:(cb + 1) * 512].rearrange("(c p) n -> p c n", p=128),
                    (), ["wa%d" % k], wa_s[k])
                for kc in range(8):
                    mm(bank(cb % 2)[0:2, :], sc_col[:, kc, :], wa[k][:, kc, :], kc == 0, kc == 7,
                       ["wa%d" % k, "sc_col"], ["psb%d" % (cb % 2)])
                tt("dve", mrow[:, cb * 512:(cb + 1) * 512], bank(cb % 2)[0:2, :], brow[:, cb * 512:(cb + 1) * 512],
                   ALU.add, ["psb%d" % (cb % 2), "brow"], ["mrow"])
            s2 = P.new_sem()
            dma("sp", mod_d[l], mrow, ["mrow"], ["mod_d"], s2)
            lv = A.alloc([128, 4, 64], F32)
            s3 = P.new_sem()
            dma("sp", lv, lamv[l:l + 1].partition_broadcast(128), (), ["lv"], s3)
            pr = A.alloc([128, 2, 64], F32)
            tt("dve", pr[:, 0, :], lv[:, 0, :], lv[:, 1, :], ALU.mult, ["lv"], ["pr"])
            tt("dve", pr[:, 1, :], lv[:, 2, :], lv[:, 3, :], ALU.mult, ["lv"], ["pr"])
            P.add("dve", lambda e, pr=pr: e.reduce_sum(out=small[:, 0:1], in_=pr[:, 0, :], axis=AX.X), ["pr"], ["sm0"])
            P.add("dve", lambda e, pr=pr: e.reduce_sum(out=small[:, 1:2], in_=pr[:, 1, :], axis=AX.X), ["pr"], ["sm1"])
            act(small[:, 2:4], small[:, 0:2], AF.Exp, ["sm0", "sm1"], ["sm2"])
            tt("dve", small[:, 4:5], small[:, 3:4], small[:, 2:3], ALU.subtract, ["sm2"], ["sm4"])
            ts("dve", neglam[:], small[:, 4:5], -lam_init, None, ALU.add, None, ["sm4"], ["neglam"])
            wsn = A.alloc([128, 4, 128], BF16)
            s4 = P.new_sem()
            dma("pool", wsn, w_sp[l].rearrange("g p q -> p g q"), (), ["wsn"], s4)
            for g in range(4):
                tr(bank16(2)[:, g * 128:(g + 1) * 128], wsn[:, g, :], ident[:], ["wsn", "ident"], ["psb2"])
            cp("dve", wsT[:].rearrange("p g q -> p (g q)"), bank16(2)[:, 0:512], ["psb2"], ["wsT"])
            tbt = A.alloc([128, 8, 9 * 128], F32)
            mk = A.alloc([128, 5, 640], F32)
            mst = A.alloc([128, 8 * 5, 640], BF16)
            s5, s6, s7 = P.new_sem(), P.new_sem(), P.new_sem()
            for h in range(8):
                dma("sp", tbt[:, h, :].rearrange("p (o q) -> p o q", o=9), tb[l, h].rearrange("o k q -> k o q"),
                    (), ["tbt%d" % h], None)
            dma("sp", mk, maskc.rearrange("v k c -> k v c"), (), ["mk"], s6)
            OB = [0, 4, 3, 1, 0]
            for h in range(8):
                for v in range(5):
                    tt("pool" if (h * 5 + v) % 2 else "dve", mst[:, h * 5 + v, :], tbt[:, h, OB[v] * 128:OB[v] * 128 + 640],
                       mk[:, v, :], ALU.add, ["tbt%d" % h, "mk"], ["mst%d" % (h * 5 + v)])
            dma("sp", M_d, mst[:].rearrange("p a b -> p (a b)"), ["mst%d" % i_ for i_ in range(40)], ["M_d"], s7)

            if stop_after < 1:
                break
            P.barrier()
            A.reset()
            P.reset_sems()
            win = A.alloc([128, 8, DIN], BF16)
            wsrc = w_in[l]
            for (a, b, nm) in ((0, 1536, "win_a"), (2560, DIN, "win_b")):
                dma("pool", win[:, :, a:b], wsrc[:, a:b].rearrange("(c p) n -> p c n", p=128), (), [nm], None)
            dma("pool", win[:, :, 1536:2560], wperm_d[l].rearrange("(c p) n -> p c n", p=128), ["wperm%d" % l], ["win_q"], None)
            sgg = A.alloc([128, 512], F32)
            sgb = A.alloc([128, 512], F32)
            brep = A.alloc([128, 4, 128], F32)
            modc = A.alloc([128, 2, 2, 8], F32)
            dma("sp", sgg, sg_g[l:l + 1, :].partition_broadcast(128), (), ["sgg"], None)
            dma("sp", sgb, sg_b[l:l + 1, :].partition_broadcast(128), (), ["sgb"], None)
            dma("sp", brep, b_sp[l:l + 1].partition_broadcast(128), (), ["brep"], None)
            for cs in range(2):
                for k in range(2):
                    dma("sp", modc[:, cs, k, :], mod_d[l, cs, k * D:(k + 1) * D].rearrange("(c p) -> p c", p=128),
                        ["mod_d"], ["modc"], None, slow=True)
            ts("dve", modc[:, :, 1, :], modc[:, :, 1, :], 1.0, None, ALU.add, None, ["modc"], ["modc"])

            xs = [A.alloc([128, D], F32) for _ in range(2)]
            xs_s = [P.new_sem() for _ in range(2)]
            xn = [A.alloc([128, D], BF16) for _ in range(2)]
            hT = [A.alloc([128, 8, 512], BF16) for _ in range(2)]
            hT_s = [P.new_sem() for _ in range(2)]
            rc = A.alloc([128, 512], F32)
            rs = A.alloc([128, 512], F32)
            rope_s = P.new_sem()
            fm = {n: A.alloc([128, 4, 512], BF16) for n in ("qb", "kb", "ya")}
            fm_s = {n: P.new_sem() for n in fm}
            fm["qc"], fm["kc"] = fm["qb"], fm["kb"]
            fm_s["qc"], fm_s["kc"] = fm_s["qb"], fm_s["kb"]
            fmr = dict(qb="fm_qb", kb="fm_kb", qc="fm_qb", kc="fm_kb", ya="fm_ya")
            tm = [[A.alloc([128, w_], BF16) for w_ in (520, 512, 528, 512)] for _ in range(2)]
            for k_ in range(2):
                P.add("pool", lambda e, a=tm[k_][0]: e.memset(a, 1.0), (), ["tm%d" % k_])
                P.add("pool", lambda e, a=tm[k_][2]: e.memset(a, 1.0), (), ["tm%d" % k_])
            tm_s = [P.new_sem() for _ in range(2)]
            of32 = [A.alloc([128, 512], F32)] * 2
            of_s = [P.new_sem()] * 2
            vn = A.alloc([128, 4, 512], BF16)
            tA = [A.alloc([128, 512], F32) for _ in range(2)]
            tB = [A.alloc([128, 512], F32) for _ in range(2)]
            tC = [A.alloc([128, 512], F32) for _ in range(2)]
            qraw = [A.alloc([128, 512], BF16) for _ in range(2)]
            ofb = [(of32[0], "of0", of_s[0]), (tA[0], "tA0", P.new_sem()), (tA[1], "tA1", P.new_sem()),
                   (tB[0], "tB0", P.new_sem()), (tB[1], "tB1", P.new_sem())]
            st6 = A.alloc([128, 2, 6], F32)
            mv = A.alloc([128, 8], F32)

            cnt = dict(x=0, bank=0, t=0, q=0, of=0, tm=0)

            def nbank():
                b = 2 + cnt["bank"] % 6
                cnt["bank"] += 1
                return b

            def layer_norm_stats(src_lo, src_hi, rd, tag):
                P.add("dve", lambda e, o=st6[:, 0, :]: e.bn_stats(out=o, in_=src_lo), rd, ["st6"])
                P.add("dve", lambda e, o=st6[:, 1, :]: e.bn_stats(out=o, in_=src_hi), rd, ["st6"])
                P.add("dve", lambda e, o=mv[:, 0:2], i=st6[:].rearrange("p a b -> p (a b)"): e.bn_aggr(out=o, in_=i),
                      ["st6"], ["mv01"])
                ts("dve", mv[:, 2:3], mv[:, 1:2], EPS, None, ALU.add, None, ["mv01"], ["mv2"])
                rstd_pow(mv[:, 3:4], mv[:, 2:3], ["mv2", "nhalf"], ["mv3"])

            flat = [(gi_, sti_) for gi_ in range(len(groups)) for sti_ in range(4)]

            def p1_xload(fi):
                gi_, sti_ = flat[fi]
                k_ = fi % 2
                t0_ = groups[gi_][0] + sti_ * 128
                dma("sp", xs[k_], x_src[t0_:t0_ + 128, :], ["x_d%d" % gi_], ["xs%d" % k_], xs_s[k_])

            def p1_ln(gi_):
                cs_ = 1 if groups[gi_][1] else 0
                tok0_ = groups[gi_][0]
                hk_ = gi_ % 2
                hres_ = "hT%d" % hk_
                for sti_ in range(4):
                    fi = gi_ * 4 + sti_
                    k = fi % 2
                    layer_norm_stats(xs[k][:, 0:512], xs[k][:, 512:1024], ["xs%d" % k], "x")
                    ts("dve", xn[k], xs[k], mv[:, 0:1], mv[:, 3:4], ALU.subtract, ALU.mult,
                       ["xs%d" % k, "mv01", "mv3"], ["xn%d" % k])
                    if fi + 2 < len(flat):
                        p1_xload(fi + 2)
                    tb_ = sti_ % 2
                    for c in range(8):
                        tr(bank16(tb_)[:, c * 128:(c + 1) * 128], xn[k][:, c * 128:(c + 1) * 128], ident[:],
                           ["xn%d" % k, "ident"], ["psb%d" % tb_])
                    for c in range(8):
                        act(hT[hk_][:, c, sti_ * 128:(sti_ + 1) * 128], bank16(tb_)[:, c * 128:(c + 1) * 128], AF.Identity,
                            ["psb%d" % tb_, "modc"], [hres_], scale=modc[:, cs_, 1, c:c + 1], bias=modc[:, cs_, 0, c:c + 1])
                dma("sp", hT_d[:, tok0_:tok0_ + 512].rearrange("(c p) t -> p c t", p=128), hT[hk_],
                    [hres_], ["hT_d%d" % gi_], hT_s[hk_])

            p1_xload(0)
            p1_xload(1)
            p1_ln(0)
            for gi, (tok0, is_p) in enumerate(groups):
                cs = 1 if is_p else 0
                hk = gi % 2
                hres = "hT%d" % hk
                if not is_p:
                    dma("sp", rc, ropec[:, tok0:tok0 + 512], (), ["rc"], rope_s)
                    dma("sp", rs, ropes[:, tok0:tok0 + 512], (), ["rs"], rope_s)

                def win_res(col0, kc):
                    if col0 < 1536:
                        return ["win_a"]
                    if col0 >= 2560:
                        return ["win_b"]
                    return ["win_q"]

                def proj_fm(col0, b):
                    for kc in range(8):
                        mm(bank(b), win[:, kc, col0:col0 + 128], hT[hk][:, kc, :], kc == 0, kc == 7,
                           win_res(col0, kc) + [hres], ["psb%d" % b])

                def proj_tm(col0, sti, b):
                    for kc in range(8):
                        mm(bank(b), hT[hk][:, kc, sti * 128:(sti + 1) * 128], win[:, kc, col0:col0 + 512],
                           kc == 0, kc == 7, win_res(col0, kc) + [hres], ["psb%d" % b])

                for sti in range(4):
                    b = nbank()
                    proj_tm(OFF["va"], sti, b)
                    pb = "psb%d" % b
                    layer_norm_stats(bank(b)[:, 0:256], bank(b)[:, 256:512], [pb], "v")
                    k = cnt["t"] % 2
                    cnt["t"] += 1
                    ts("dve", tA[k], bank(b), mv[:, 0:1], mv[:, 3:4], ALU.subtract, ALU.mult,
                       [pb, "mv01", "mv3"], ["tA%d" % k])
                    tt("pool", tA[k], tA[k], sgg, ALU.mult, ["tA%d" % k, "sgg"], ["tA%d" % k])
                    tt("pool", vn[:, sti, :], tA[k], sgb, ALU.add, ["tA%d" % k, "sgb"], ["vn"])
                for g in range(4):
                    k = cnt["t"] % 2
                    cnt["t"] += 1
                    bu = nbank()
                    proj_fm(OFF["ua"] + g * 128, bu)
                    act(tA[k], bank(bu), AF.Copy, ["psb%d" % bu], ["tA%d" % k], scale=0.5)
                    bg = nbank()
                    proj_fm(OFF["ga"] + g * 128, bg)
                    act(tB[k], bank(bg), AF.Tanh, ["psb%d" % bg], ["tB%d" % k], scale=0.5)
                    stt(tB[k], tB[k], 1.0, bank(bg), ALU.add, ALU.mult, ["tB%d" % k, "psb%d" % bg], ["tB%d" % k])
                    bs_ = nbank()
                    for n in range(4):
                        mm(bank(bs_)[:, n * 128:(n + 1) * 128], vn[:, n, g * 128:(g + 1) * 128], wsT[:, g, :], True, True,
                           ["vn", "wsT"], ["psb%d" % bs_])
                    for n in range(4):
                        tt("dve", tC[k][:, n * 128:(n + 1) * 128], bank(bs_)[:, n * 128:(n + 1) * 128], brep[:, g, :],
                           ALU.add, ["psb%d" % bs_, "brep"], ["tC%d" % k])
                    tt("pool", tA[k], tA[k], tB[k], ALU.mult, ["tA%d" % k, "tB%d" % k], ["tA%d" % k])
                    tt("dve", fm["ya"][:, g, :], tA[k], tC[k], ALU.mult, ["tA%d" % k, "tC%d" % k], ["fm_ya"])
                dma("sp", yaT_d[:, tok0:tok0 + 512].rearrange("(c p) t -> p c t", p=128), fm["ya"],
                    ["fm_ya"], ["yaT_d%d" % gi], fm_s["ya"])

                if gi + 1 < len(groups):
                    p1_ln(gi + 1)
                for name in ("qb", "kb"):
                    for h in range(4):
                        b = nbank()
                        proj_fm(OFF[name] + h * 128, b)
                        pb = "psb%d" % b
                        if is_p:
                            act(fm[name][:, h, :], bank(b), AF.Copy, [pb], [fmr[name]])
                        else:
                            k = cnt["q"] % 2
                            cnt["q"] += 1
                            act(qraw[k], bank(b), AF.Copy, [pb], ["qraw%d" % k])
                            b2 = nbank()
                            mm(bank(b2), prot[:], qraw[k], True, True, ["prot", "qraw%d" % k], ["psb%d" % b2])
                            act(tA[k], bank(b), AF.Copy, [pb], ["tA%d" % k])
                            act(tB[k], bank(b2), AF.Copy, ["psb%d" % b2], ["tB%d" % k])
                            tt("dve", tA[k], tA[k], rc, ALU.mult, ["tA%d" % k, "rc"], ["tA%d" % k])
                            tt("pool", tB[k], tB[k], rs, ALU.mult, ["tB%d" % k, "rs"], ["tB%d" % k])
                            tt("pool", fm[name][:, h, :], tA[k], tB[k], ALU.add, ["tA%d" % k, "tB%d" % k], [fmr[name]])
                    dst = (qbT_d if name == "qb" else kbT_d)[:, :, tok0:tok0 + 512]
                    dma("sp", dst, fm[name], [fmr[name]], ["%sT_d%d" % (name, gi)], fm_s[name])
                for name in ("qc", "kc"):
                    for hp in range(4):
                        b = nbank()
                        proj_fm(OFF[name] + hp * 128, b)
                        act(fm[name][:, hp, :], bank(b), AF.Copy, ["psb%d" % b], [fmr[name]])
                    dst = (qcT_d if name == "qc" else kcT_d)[:, :, tok0:tok0 + 512]
                    dma("sp", dst, fm[name], [fmr[name]], ["%sT_d%d" % (name, gi)], fm_s[name])

                for sti in range(4):
                    k = cnt["tm"] % 2
                    cnt["tm"] += 1
                    tmr = "tm%d" % k
                    t0 = tok0 + sti * 128
                    sq, tq = sti // 2, (sti % 2) * 128

                    def out_f32(b, pairs):
                        ko = cnt["of"] % 5
                        cnt["of"] += 1
                        buf, res, sem_ = ofb[ko]
                        act(buf, bank(b), AF.Copy, ["psb%d" % b], [res])
                        for dst_ap, src_view in pairs:
                            dma("sp", dst_ap, src_view(buf), [res], ["outs"], sem_)

                    for ai, name in enumerate(("vb", "gb", "vc", "gc")):
                        b = nbank()
                        proj_tm(OFF[name], sti, b)
                        pb = "psb%d" % b
                        if name in ("vb", "vc"):
                            if name == "vb":
                                act(tm[k][0].rearrange("p (h v) -> p h v", h=4)[:, :, 0:128],
                                    bank(b).rearrange("p (h v) -> p h v", h=4), AF.Copy, [pb], [tmr])
                            else:
                                act(tm[k][2].rearrange("p (h v) -> p h v", h=8)[:, :, 0:64],
                                    bank(b).rearrange("p (h v) -> p h v", h=8), AF.Copy, [pb], [tmr])
                            if is_p:
                                if name == "vb":
                                    out_f32(b, [(o_dv[sq, l, :, tq:tq + 128, :].rearrange("h t v -> t h v"),
                                                 lambda a: a.rearrange("p (h v) -> p h v", h=4))])
                                else:
                                    out_f32(b, [(o_nv[sq, l, :, tq:tq + 128, :].rearrange("h t v -> t h v"),
                                                 lambda a: a.rearrange("p (h v) -> p h v", h=8))])
                        else:
                            kk = cnt["t"] % 2
                            cnt["t"] += 1
                            act(tC[kk], bank(b), AF.Tanh, [pb], ["tC%d" % kk], scale=0.5)
                            stt(tm[k][ai], tC[kk], 1.0, bank(b), ALU.add, ALU.mult, ["tC%d" % kk, pb], [tmr])
                    if is_p:
                        b = nbank()
                        proj_tm(OFF["kb"], sti, b)
                        out_f32(b, [(o_dk[sq, l, m_, :, tq:tq + 128, :].rearrange("h t d -> t h d"),
                                     (lambda a, m_=m_: a.rearrange("p (h m d) -> p h m d", h=4, m=2)[:, :, m_, :]))
                                    for m_ in range(2)])
                        b = nbank()
                        proj_tm(OFF["kc"], sti, b)
                        out_f32(b, [(o_nk[sq, l, :, tq:tq + 128, :].rearrange("h t v -> t h v"),
                                     lambda a: a.rearrange("p (h v) -> p h v", h=8))])
                    for ai, dd in enumerate((vb_d, gb_d, vc_d, gc_d)):
                        dma("sp", dd[t0:t0 + 128, :], tm[k][ai], [tmr], ["tmd%d_%d" % (ai, gi)], tm_s[k])

            if l == 0:
                print("arena P1 KB", A.off / 512.0)
            if stop_after < 2:
                break
            P.barrier()
            A.reset()
            P.reset_sems()
            KbT = A.alloc([128, 4, NKB * 128], BF16)
            Vb = A.alloc([128, NKB, 4 * 130], BF16)
            gsub = A.alloc([128, 128], F32)
            P.add("pool", lambda e, a=Vb[:, NKS:NKS + NKC, :]: e.memset(a, 1.0), (), ["Vb_c"])
            dma("sp", KbT[:, :, NT + PAST:NT + PAST + NP], kbT_d[:, :, NT:NTOT], ["kbT_d%d" % NG], ["KbT_p"], None)
            dma("sp", Vb[:, NKS + NKC:NKB, :], vb_d[NT:NTOT, :].rearrange("(k p) f -> p k f", p=128),
                ["tmd0_%d" % NG], ["Vb_p"], None)
            dma("sp", KbT[:, :, 0:NT], kbT_d[:, :, 0:NT], ["kbT_d%d" % g for g in range(NG)], ["KbT_s"], None)
            Vb4 = Vb.rearrange("p k (h v) -> p k h v", h=4)
            for k0 in range(0, NKS, 8):
                dma("sp", Vb[:, k0:k0 + 8, :], vb_d[k0 * 128:(k0 + 8) * 128, :].rearrange("(k p) f -> p k f", p=128),
                    ["tmd0_%d" % g for g in range(NG)], ["Vb_s%d" % (k0 // 8)], None)
            dma("sp", gsub, subg[l:l + 1, :].partition_broadcast(128), (), ["gsub"], None)
            ts("dve", gsub, gsub, (1.0 - lam_init) * 0.5, None, ALU.mult, None, ["gsub"], ["gsub"])
            for h_ in range(4):
                dma("pool", Vb4[:, NKS:NKS + NKC, h_, 0:128], cdv[l, h_].rearrange("(k p) v -> p k v", p=128),
                    (), ["Vb_c"], None)
            ckt = A.alloc([128, NKC, 4 * 128], BF16)
            ckt5 = ckt.rearrange("p k (h m d) -> p k h m d", h=4, m=2)
            for m in range(2):
                for h in range(4):
                    dma("pool", ckt5[:, :, h, m, :], cdk[l, m, h].rearrange("(k p) d -> p k d", p=128),
                        (), ["ckt"], None)
            def p2a_cache_k():
                for kb in range(NKC):
                    for h in range(4):
                        tr(bank16(7)[:, h * 128:(h + 1) * 128], ckt[:, kb, h * 128:(h + 1) * 128], ident[:],
                           ["ckt", "ident"], ["psb7"])
                    for h in range(4):
                        cp("dve", KbT[:, h, NT + kb * 128:NT + (kb + 1) * 128], bank16(7)[:, h * 128:(h + 1) * 128],
                           ["psb7"], ["KbT_c"])

            if l + 1 < depth:
                issue_wperm(l + 1)
            qt = [A.alloc([128, 4, 512], BF16) for _ in range(2)]
            qt_s = [P.new_sem() for _ in range(2)]
            gbt = [A.alloc([128, 4, 512], BF16) for _ in range(2)]
            pT2 = [A.alloc([128, 2, 512], BF16) for _ in range(3)]
            pT = [[pT2[i_][:, m_, :] for i_ in range(3)] for m_ in range(2)]
            ybt2 = [A.alloc([128, 4, 512], BF16) for _ in range(2)]
            ybT_st = A.alloc([128, 4, 512], BF16)
            ybT_s = P.new_sem()
            otmp = [A.alloc([128, 128], F32) for _ in range(8)]
            ofin = [A.alloc([128, 128], F32) for _ in range(8)]
            junk = A.alloc([128, 128], F32)
            sv_ = [A.alloc([128, 8], F32) for _ in range(8)]

            def kres(kb):
                return "KbT_s" if kb < NKS else ("KbT_c" if kb < NKS + NKC else "KbT_p")

            def vres(kb):
                return ("Vb_s%d" % (kb // 8)) if kb < NKS else ("Vb_c" if kb < NKS + NKC else "Vb_p")

            def acc_ap(i, n):
                return ps[:, (4 + i // 3) * 512 + (i % 3) * 130:(4 + i // 3) * 512 + (i % 3) * 130 + n]

            qtiles = [(NT + s * SEQ, SEQ, [NKS + NKC + 2 * s, NKS + NKC + 2 * s + 1], NG) for s in range(NPS)]
            qtiles += [(g * 512, 512, list(range(NKS + NKC)), g) for g in range(NG)]
            cntp = dict(s=0, p=0, f=0)
            def p2a_loads(ti_):
                tok0_, N_, _, gi_ = qtiles[ti_]
                k_ = ti_ % 2
                dma("sp", qt[k_][:, :, 0:N_], qbT_d[:, :, tok0_:tok0_ + N_], ["qbT_d%d" % gi_], ["qt%d" % k_], qt_s[k_])
                dma("sp", gbt[k_][:, 0:N_ // 128, :], gb_d[tok0_:tok0_ + N_, :].rearrange("(s p) f -> p s f", p=128),
                    ["tmd1_%d" % gi_], ["gbt%d" % k_], qt_s[k_])

            def p2a_tail(ti_):
                tok0_, N_, _, gi_ = qtiles[ti_]
                yb_, ybr_ = ybt2[ti_ % 2], "ybt%d" % (ti_ % 2)
                for hc in range(4):
                    for qs in range(N_ // 128):
                        tr(bank16(7)[:, qs * 128:(qs + 1) * 128], yb_[:, qs, hc * 128:(hc + 1) * 128], ident[:],
                           [ybr_, "ident"], ["psb7"])
                    cp("dve", ybT_st[:, hc, 0:N_], bank16(7)[:, 0:N_], ["psb7"], ["ybT_st"])
                dma("sp", ybT_d[:, tok0_:tok0_ + N_].rearrange("(c p) t -> p c t", p=128), ybT_st[:, :, 0:N_],
                    ["ybT_st"], ["ybT_d%d" % gi_], ybT_s)

            p2a_loads(0)
            for ti, (tok0, N, kblocks, gi) in enumerate(qtiles):
                k = ti % 2
                nsub = N // 128
                ybt, ybr = ybt2[ti % 2], "ybt%d" % (ti % 2)
                if ti + 1 < len(qtiles):
                    p2a_loads(ti + 1)
                items = [(h, kbi, kb) for h in range(4) for kbi, kb in enumerate(kblocks)]
                nlast = len(kblocks) - 1
                started = set()

                def qk_exp(idx):
                    h, kbi, kb = items[idx]
                    sp_ = (cntp["s"] + idx) % 2
                    pk = (cntp["p"] + idx) % 3
                    for m in range(2):
                        r0 = m * 64
                        b = sp_ * 2 + m
                        mm(bank(b)[:, 0:N], KbT[r0:r0 + 64, h, kb * 128:(kb + 1) * 128], qt[k][r0:r0 + 64, h, 0:N],
                           True, True, [kres(kb), "qt%d" % k], ["psb%d" % b])
                    for m in range(2):
                        b = sp_ * 2 + m
                        act(pT[m][pk][:, 0:N], bank(b)[:, 0:N], AF.Exp, ["psb%d" % b], ["pT%d_%d" % (m, pk)], scale=0.125)

                def finalize(h):
                    for qs in range(nsub):
                        f = qs + 4 * (h % 2)
                        i0, i1 = qs, 4 + qs
                        rd = ["psb%d" % (4 + i0 // 3), "psb%d" % (4 + i1 // 3)]
                        s = sv_[f]
                        sr = "sv%d" % f
                        P.add("dve", lambda e, s=s, a=acc_ap(i0, 130): e.reciprocal(out=s[:, 0:1], in_=a[:, 128:129]), rd, [sr + "a"])
                        P.add("dve", lambda e, s=s, a=acc_ap(i1, 130): e.reciprocal(out=s[:, 1:2], in_=a[:, 128:129]), rd, [sr + "b"])
                        tt("dve", s[:, 2:3], s[:, 1:2], neglam[:], ALU.mult, [sr + "b", "neglam"], [sr + "c"])
                        ts("dve", otmp[f], acc_ap(i1, 128), s[:, 2:3], None, ALU.mult, None, rd + [sr + "c"], ["otmp%d" % f])
                        stt(ofin[f], acc_ap(i0, 128), s[:, 0:1], otmp[f], ALU.mult, ALU.add, rd + [sr + "a", "otmp%d" % f],
                            ["ofin%d" % f])
                    for qs in range(nsub):
                        f = qs + 4 * (h % 2)
                        s = sv_[f]
                        sr = "sv%d" % f
                        act(junk, ofin[f], AF.Square, ["ofin%d" % f], ["junk", sr + "d"], accum=s[:, 3:4])
                        ts("dve", s[:, 4:5], s[:, 3:4], 1.0 / 128.0, EPS, ALU.mult, ALU.add, [sr + "d"], [sr + "e"])
                        rstd_pow(s[:, 5:6], s[:, 4:5], [sr + "e", "nhalf"], [sr + "f"])
                        stt(ofin[f], ofin[f], s[:, 5:6], gsub, ALU.mult, ALU.mult, ["ofin%d" % f, sr + "f", "gsub"], ["ofin%d" % f])
                        tt("pool", ybt[:, qs, h * 128:(h + 1) * 128], ofin[f], gbt[k][:, qs, h * 128:(h + 1) * 128], ALU.mult,
                           ["ofin%d" % f, "gbt%d" % k], [ybr])

                def pv(idx):
                    h, kbi, kb = items[idx]
                    pk = (cntp["p"] + idx) % 3
                    if kbi == 0:
                        started.clear()
                    for m in range(2):
                        for qs in range(nsub):
                            i = m * 4 + qs
                            bk = 4 + i // 3
                            first = bk not in started
                            started.add(bk)
                            mm(acc_ap(i, 130), pT[m][pk][:, qs * 128:(qs + 1) * 128], Vb[:, kb, h * 130:(h + 1) * 130],
                               first, kbi == nlast, ["pT%d_%d" % (m, pk), vres(kb)], ["psb%d" % bk])
                    if kbi == nlast:
                        finalize(h)

                qk_exp(0)
                for idx in range(len(items)):
                    if idx + 1 < len(items):
                        qk_exp(idx + 1)
                    pv(idx)
                    if idx == min(5, len(items) - 1) and ti > 0:
                        p2a_tail(ti - 1)
                    if idx == 3 and ti == NPS:
                        p2a_cache_k()
                cntp["s"] += len(items)
                cntp["p"] += len(items)
            p2a_tail(len(qtiles) - 1)

            if l == 0:
                print("arena P2a KB", A.off / 512.0)
            if stop_after < 3:
                break
            P.barrier()
            A.reset()
            P.reset_sems()
            Mt = A.alloc([128, 40, 640], BF16)
            kctx = A.alloc([128, 4, PAST + NP], BF16)
            vctx = A.alloc([128, NKC + NKP, 8 * 66], BF16)
            vctx4 = vctx.rearrange("p k (h v) -> p k h v", h=8)
            P.add("pool", lambda e, a=vctx: e.memset(a, 1.0), (), ["vctx"])
            dma("sp", kctx[:, :, PAST:PAST + NP], kcT_d[:, :, NT:NTOT], ["kcT_d%d" % NG], ["kctx"], None)
            dma("sp", vctx[:, NKC:NKC + NKP, :], vc_d[NT:NTOT, :].rearrange("(k p) f -> p k f", p=128),
                ["tmd2_%d" % NG], ["vctx"], None)
            for h_ in range(8):
                dma("pool", vctx4[:, 0:NKC, h_, 0:64], cnv[l, h_].rearrange("(k p) v -> p k v", p=128), (), ["vctx"], None)
            dma("sp", Mt[:].rearrange("p a b -> p (a b)"), M_d, ["M_d"], ["Mt"], None)
            cnt_ = A.alloc([128, NKC, 8 * 64], BF16)
            for h_ in range(8):
                dma("pool", cnt_.rearrange("p k (h d) -> p k h d", h=8)[:, :, h_, :], cnk[l, h_].rearrange("(k p) d -> p k d", p=128),
                    (), ["cnt"], None)
            for kb in range(NKC):
                for hp in range(4):
                    tr(bank16(7)[:, hp * 128:(hp + 1) * 128], cnt_[:, kb, hp * 128:(hp + 1) * 128], ident[:],
                       ["cnt", "ident"], ["psb7"])
                for hp in range(4):
                    cp("dve", kctx[:, hp, kb * 128:(kb + 1) * 128], bank16(7)[:, hp * 128:(hp + 1) * 128],
                       ["psb7"], ["kctx"])
            kw = [A.alloc([128, 4, 1024], BF16) for _ in range(2)]
            vw = [A.alloc([128, 8, 8 * 66], BF16) for _ in range(2)]
            qct = [A.alloc([128, 4, 512], BF16) for _ in range(2)]
            gct = [A.alloc([128, 4, 512], BF16) for _ in range(2)]
            w_s = [P.new_sem() for _ in range(2)]
            wkv_s = [P.new_sem() for _ in range(2)]
            pc = [A.alloc([128, 512], BF16) for _ in range(4)]
            pl = [A.alloc([128, 640], BF16) for _ in range(3)]
            slf = [A.alloc([128, 640], F32) for _ in range(3)]
            yct2 = [A.alloc([128, 4, 512], BF16) for _ in range(2)]
            ycT_st = A.alloc([128, 4, 512], BF16)
            ycT_s = P.new_sem()
            rv = [A.alloc([128, 4], F32) for _ in range(2)]
            cq = dict(s=0, p=0, l=0, f=0)

            def accc(hh, qs, n):
                return ps[:, (4 + hh) * 512 + qs * 66:(4 + hh) * 512 + qs * 66 + n]

            ntiles = NG + NPS

            def p2b_geom(ti_):
                if ti_ >= NPS:
                    g_ = ti_ - NPS
                    wb0_ = max(0, 4 * g_ - 2)
                    wb1_ = min(NKS - 1, 4 * g_ + 5)
                    return g_ * 512, 512, g_, wb0_, wb1_, list(range(NKC))
                s_ = ti_
                return NT + s_ * SEQ, SEQ, NG, 0, 0, [NKC + 2 * s_, NKC + 2 * s_ + 1]

            def p2b_loads(ti_):
                tok0_, N_, gi_, wb0_, wb1_, _ = p2b_geom(ti_)
                k_ = ti_ % 2
                if ti_ >= NPS:
                    nwb_ = wb1_ - wb0_ + 1
                    dma("sp", kw[k_][:, :, 0:nwb_ * 128], kcT_d[:, :, wb0_ * 128:(wb1_ + 1) * 128],
                        ["kcT_d%d" % x for x in range(NG)], ["kw%d" % k_], wkv_s[k_])
                    dma("sp", vw[k_][:, 0:nwb_, :],
                        vc_d[wb0_ * 128:(wb1_ + 1) * 128, :].rearrange("(k p) f -> p k f", p=128),
                        ["tmd2_%d" % x for x in range(NG)], ["vw%d" % k_], wkv_s[k_])
                dma("sp", qct[k_][:, :, 0:N_], qcT_d[:, :, tok0_:tok0_ + N_], ["qcT_d%d" % gi_], ["qct%d" % k_], w_s[k_])
                dma("sp", gct[k_][:, 0:N_ // 128, :], gc_d[tok0_:tok0_ + N_, :].rearrange("(s p) f -> p s f", p=128),
                    ["tmd3_%d" % gi_], ["gct%d" % k_], w_s[k_])

            def p2b_tail(ti_):
                tok0_, N_, gi_, _, _, _ = p2b_geom(ti_)
                yc_, ycr_ = yct2[ti_ % 2], "yct%d" % (ti_ % 2)
                for hc in range(4):
                    for qs in range(N_ // 128):
                        tr(bank16(7)[:, qs * 128:(qs + 1) * 128], yc_[:, qs, hc * 128:(hc + 1) * 128], ident[:],
                           [ycr_, "ident"], ["psb7"])
                    cp("dve", ycT_st[:, hc, 0:N_], bank16(7)[:, 0:N_], ["psb7"], ["ycT_st"])
                dma("sp", ycT_d[:, tok0_:tok0_ + N_].rearrange("(c p) t -> p c t", p=128), ycT_st[:, :, 0:N_],
                    ["ycT_st"], ["ycT_d%d" % gi_], ycT_s)

            p2b_loads(0)
            for ti in range(ntiles):
                k = ti % 2
                is_p = ti < NPS
                g = ti - NPS
                tok0, N, gi, wb0, wb1, cblocks = p2b_geom(ti)
                nsub = N // 128
                yct, ycr = yct2[ti % 2], "yct%d" % (ti % 2)
                if ti + 1 < ntiles:
                    p2b_loads(ti + 1)
                items = []
                for hp in range(4):
                    for hh in range(2):
                        hitems = [("c", hp, hh, ci, cb) for ci, cb in enumerate(cblocks)]
                        if not is_p:
                            hitems += [("l", hp, hh, qs, None) for qs in range(4)]
                        for ii, it in enumerate(hitems):
                            items.append(it + (ii == 0, ii == len(hitems) - 1))

                def local_geom(qs):
                    j = 4 * g + qs
                    jb = min(max(j - 2, 0), NPAIR - 5)
                    if j == 0:
                        v = 1
                    elif j == 1:
                        v = 2
                    elif j == NPAIR - 2:
                        v = 3
                    elif j == NPAIR - 1:
                        v = 4
                    else:
                        v = 0
                    return jb, v

                def front(idx):
                    kind, hp, hh, a1, a2, isf, isl = items[idx]
                    head = 2 * hp + hh
                    r0 = 64 * hh
                    slot = (cq["s"] + idx) % 3
                    b0 = (0, 2, 6)[slot]
                    if kind == "c":
                        cb = a2
                        pk = (cq["p"] + idx) % 4
                        mm(bank(b0)[:, 0:N], kctx[r0:r0 + 64, hp, cb * 128:(cb + 1) * 128], qct[k][r0:r0 + 64, hp, 0:N],
                           True, True, ["kctx", "qct%d" % k], ["psb%d" % b0])
                        act(pc[pk][:, 0:N], bank(b0)[:, 0:N], AF.Exp, ["psb%d" % b0], ["pc%d" % pk], scale=0.125)
                    else:
                        qs = a1
                        jb, v = local_geom(qs)
                        lk = (cq["l"] + idx) % 3
                        for blk in range(5):
                            wi = jb + blk - wb0
                            mm(ps[:, b0 * 512 + blk * 128:b0 * 512 + (blk + 1) * 128],
                               kw[k][r0:r0 + 64, hp, wi * 128:(wi + 1) * 128], qct[k][r0:r0 + 64, hp, qs * 128:(qs + 1) * 128],
                               True, True, ["kw%d" % k, "qct%d" % k], ["psb%d" % b0, "psb%d" % (b0 + 1)])
                        stt(slf[lk], ps[:, b0 * 512:b0 * 512 + 640], 0.125, Mt[:, head * 5 + v, :], ALU.mult, ALU.add,
                            ["psb%d" % b0, "psb%d" % (b0 + 1), "Mt"], ["slf%d" % lk])
                        act(pl[lk], slf[lk], AF.Exp, ["slf%d" % lk], ["pl%d" % lk])

                def back(idx):
                    kind, hp, hh, a1, a2, isf, isl = items[idx]
                    head = 2 * hp + hh
                    accr = "psb%d" % (4 + hh)
                    if kind == "c":
                        cb = a2
                        pk = (cq["p"] + idx) % 4
                        for qs in range(nsub):
                            mm(accc(hh, qs, 66), pc[pk][:, qs * 128:(qs + 1) * 128], vctx[:, cb, head * 66:(head + 1) * 66],
                               isf and qs == 0, isl, ["pc%d" % pk, "vctx"], [accr])
                    else:
                        qs = a1
                        jb, v = local_geom(qs)
                        lk = (cq["l"] + idx) % 3
                        for blk in range(5):
                            wi = jb + blk - wb0
                            mm(accc(hh, qs, 66), pl[lk][:, blk * 128:(blk + 1) * 128], vw[k][:, wi, head * 66:(head + 1) * 66],
                               False, blk == 4, ["pl%d" % lk, "vw%d" % k], [accr])
                    if isl:
                        for qs in range(nsub):
                            f = cq["f"] % 2
                            cq["f"] += 1
                            ts("dve", rv[f][:, 0:1], accc(hh, qs, 66)[:, 64:65], 2.0, None, ALU.mult, None, [accr], ["rv%da" % f])
                            P.add("dve", lambda e, r=rv[f]: e.reciprocal(out=r[:, 1:2], in_=r[:, 0:1]), ["rv%da" % f], ["rv%db" % f])
                            stt(yct[:, qs, head * 64:(head + 1) * 64], accc(hh, qs, 64), rv[f][:, 1:2],
                                gct[k][:, qs, head * 64:(head + 1) * 64], ALU.mult, ALU.mult, [accr, "rv%db" % f, "gct%d" % k], [ycr])

                front(0)
                if len(items) > 1:
                    front(1)
                for idx in range(len(items)):
                    if idx + 2 < len(items):
                        front(idx + 2)
                    back(idx)
                    if idx == min(5, len(items) - 1) and ti > 0:
                        p2b_tail(ti - 1)
                cq["s"] += len(items)
                cq["p"] += len(items)
                cq["l"] += len(items)
            p2b_tail(ntiles - 1)

            if l == 0:
                print("arena P2b KB", A.off / 512.0)
            if stop_after < 4:
                break
            P.barrier()
            A.reset()
            P.reset_sems()
            wmg = A.alloc([128, 8, 3 * D], BF16)
            wbr = A.alloc([128, 3, 4 * D], BF16)
            wo = A.alloc([128, 8, D], BF16)
            for c3 in range(3):
                dma("pool", wmg[:, :, c3 * D:(c3 + 1) * D], w_mg[l][:, c3 * D:(c3 + 1) * D].rearrange("(c p) n -> p c n", p=128),
                    (), ["wmg%d" % c3], None)
            for bi, wsrc_ in enumerate((w_bra, w_brb, w_brc)):
                dma("pool", wbr[:, bi, :].rearrange("p (c n) -> p c n", c=4), wsrc_[l].rearrange("(c p) n -> p c n", p=128),
                    (), ["wbr%d" % bi], None)
            dma("pool", wo, w_o[l].rearrange("(c p) n -> p c n", p=128), (), ["wo"], None)
            grep = A.alloc([128, 2, D], F32)
            lng = A.alloc([128, D], F32)
            lnb = A.alloc([128, D], F32)
            bmg = A.alloc([128, 24], F32)
            for cs in range(2):
                dma("sp", grep[:, cs, :], mod_d[l, cs:cs + 1, 2 * D:3 * D].partition_broadcast(128), ["mod_d"], ["grep"], None)
            ts("dve", grep, grep, 0.5, None, ALU.mult, None, ["grep"], ["grep"])
            dma("sp", lng, ln_g[l:l + 1, :].partition_broadcast(128), (), ["lng"], None)
            dma("sp", lnb, ln_b[l:l + 1, :].partition_broadcast(128), (), ["lnb"], None)
            dma("sp", bmg, b_mg[l].rearrange("(c p) -> p c", p=128), (), ["bmg"], None, slow=True)
            ts("dve", bmg, bmg, 0.5, None, ALU.mult, None, ["bmg"], ["bmg"])
            h3 = [A.alloc([128, 8, 512], BF16)] * 2
            y3 = [A.alloc([128, 12, 512], BF16)] * 2
            in_s = [P.new_sem()] * 2
            x3 = [A.alloc([128, D], F32) for _ in range(2)]
            x3_s = [P.new_sem() for _ in range(2)]
            tg = [A.alloc([128, 512], F32) for _ in range(3)]
            tp = [A.alloc([128, 512], F32) for _ in range(3)]
            mT = [A.alloc([128, 8, 512], BF16)] * 2
            z3 = [A.alloc([128, D], F32) for _ in range(2)]
            o3 = [A.alloc([128, D], F32) for _ in range(2)]
            o3_s = [P.new_sem() for _ in range(2)]
            st6 = A.alloc([128, 2, 6], F32)
            mv = A.alloc([128, 8], F32)
            c3 = dict(b=0, t=0, x=0)
            def p3_loads(gi_):
                tok0_ = groups[gi_][0]
                k_ = 0
                dma("sp", h3[k_], hT_d[:, tok0_:tok0_ + 512].rearrange("(c p) t -> p c t", p=128), ["hT_d%d" % gi_],
                    ["h3_%d" % k_], in_s[k_])
                for bi_, (dd, nm) in enumerate(((yaT_d, "yaT_d"), (ybT_d, "ybT_d"), (ycT_d, "ycT_d"))):
                    dma("sp", y3[k_][:, bi_ * 4:(bi_ + 1) * 4, :], dd[:, tok0_:tok0_ + 512].rearrange("(c p) t -> p c t", p=128),
                        ["%s%d" % (nm, gi_)], ["y3_%d" % k_], in_s[k_])

            def p3_xload(gi_, sti_):
                t0_ = groups[gi_][0] + sti_ * 128
                kx_ = sti_ % 2
                dma("sp", x3[kx_], x_src[t0_:t0_ + 128, :], ["x_d%d" % gi_], ["x3_%d" % kx_], x3_s[kx_])

            p3_loads(0)
            for gi, (tok0, is_p) in enumerate(groups):
                cs = 1 if is_p else 0
                k = 0
                for oc in range(8):
                    for bi in range(3):
                        pi = c3["b"] % 3
                        c3["b"] += 1
                        bg, bb = 2 * pi, 2 * pi + 1
                        for kc in range(8):
                            mm(bank(bg), wmg[:, kc, bi * D + oc * 128:bi * D + (oc + 1) * 128], h3[k][:, kc, :], kc == 0, kc == 7,
                               ["wmg%d" % bi, "h3_%d" % k], ["psb%d" % bg])
                        for kc in range(4):
                            mm(bank(bb), wbr[:, bi, kc * D + oc * 128:kc * D + (oc + 1) * 128], y3[k][:, bi * 4 + kc, :], kc == 0, kc == 3,
                               ["wbr%d" % bi, "y3_%d" % k], ["psb%d" % bb])
                        ti_ = c3["t"] % 3
                        c3["t"] += 1
                        act(tg[ti_], bank(bg), AF.Tanh, ["psb%d" % bg, "bmg"], ["tg%d" % ti_], scale=0.5,
                            bias=bmg[:, bi * 8 + oc:bi * 8 + oc + 1])
                        stt(tp[bi], tg[ti_], 1.0, bank(bb), ALU.add, ALU.mult, ["tg%d" % ti_, "psb%d" % bb], ["tp%d" % bi])
                    tt("dve", tp[0], tp[0], tp[1], ALU.add, ["tp0", "tp1"], ["tp0"])
                    tt("dve", mT[k][:, oc, :], tp[0], tp[2], ALU.add, ["tp0", "tp2"], ["mT%d" % k])
                p3_xload(gi, 0)
                p3_xload(gi, 1)
                if gi + 1 < len(groups):
                    p3_loads(gi + 1)
                for sti in range(4):
                    t0 = tok0 + sti * 128
                    kx = sti % 2
                    for half in range(2):
                        for kc in range(8):
                            mm(bank(6 + half), mT[k][:, kc, sti * 128:(sti + 1) * 128], wo[:, kc, half * 512:(half + 1) * 512],
                               kc == 0, kc == 7, ["mT%d" % k, "wo"], ["psb%d" % (6 + half)])
                    for half in range(2):
                        act(z3[kx][:, half * 512:(half + 1) * 512], bank(6 + half), AF.Copy, ["psb%d" % (6 + half)], ["z3_%d" % kx])
                    tt("pool", z3[kx], z3[kx], grep[:, cs, :], ALU.mult, ["z3_%d" % kx, "grep"], ["z3_%d" % kx])
                    stt(z3[kx], x3[kx], ALPHA_FULL, z3[kx], ALU.mult, ALU.add, ["x3_%d" % kx, "z3_%d" % kx], ["z3_%d" % kx])
                    zr = ["z3_%d" % kx]
                    P.add("dve", lambda e, a=z3[kx], o=st6[:, 0, :]: e.bn_stats(out=o, in_=a[:, 0:512]), zr, ["st6"])
                    P.add("dve", lambda e, a=z3[kx], o=st6[:, 1, :]: e.bn_stats(out=o, in_=a[:, 512:1024]), zr, ["st6"])
                    P.add("dve", lambda e, o=mv[:, 0:2], i=st6[:].rearrange("p a b -> p (a b)"): e.bn_aggr(out=o, in_=i), ["st6"], ["mv01"])
                    ts("dve", mv[:, 2:3], mv[:, 1:2], EPS, None, ALU.add, None, ["mv01"], ["mv2"])
                    rstd_pow(mv[:, 3:4], mv[:, 2:3], ["mv2", "nhalf"], ["mv3"])
                    ts("dve", z3[kx], z3[kx], mv[:, 0:1], mv[:, 3:4], ALU.subtract, ALU.mult, zr + ["mv01", "mv3"], zr)
                    tt("pool", z3[kx], z3[kx], lng, ALU.mult, zr + ["lng"], zr)
                    tt("pool", o3[kx], z3[kx], lnb, ALU.add, zr + ["lnb"], ["o3_%d" % kx])
                    if sti + 2 < 4:
                        p3_xload(gi, sti + 2)
                    dma("sp", x_dst[t0:t0 + 128, :], o3[kx], ["o3_%d" % kx], ["x_d%d" % gi], o3_s[kx])

        print("arena P3 KB", A.off / 512.0)
        P.emit()
    return nc


def _const_tables(nrows):
    NT = nrows * GW
    t = np.arange(NT)
    half = 32
    inv = (1.0 / (10000.0 ** (np.arange(0, half, 2, dtype=np.float32) / np.float32(half)))).astype(np.float32)
    ang_r = (t // GW).astype(np.float32)[:, None] * inv
    ang_c = (t % GW).astype(np.float32)[:, None] * inv
    cos_r, sin_r, cos_c, sin_c = np.cos(ang_r), np.sin(ang_r), np.cos(ang_c), np.sin(ang_c)
    C = np.zeros((128, NT), np.float32)
    S = np.zeros((128, NT), np.float32)
    for p in range(128):
        d = p % 64
        ax, f = d // 32, d % 16
        C[p] = (cos_r if ax == 0 else cos_c)[:, f]
        S[p] = (sin_r if ax == 0 else sin_c)[:, f]
    npair = nrows // 2
    reps = [2, 0, 1, npair - 2, npair - 1]
    cpos = np.arange(GW)
    cstart = np.clip(cpos - 8, 0, GW - 16)
    colok = (cpos[None, :] >= cstart[:, None]) & (cpos[None, :] < cstart[:, None] + 16)
    mask = np.full((5, 128, 640), NEG, np.float32)
    for v, j in enumerate(reps):
        jb = min(max(j - 2, 0), npair - 5)
        for blk in range(5):
            for kr_ in range(2):
                krow = 2 * (jb + blk) + kr_
                for qr_ in range(2):
                    r = 2 * j + qr_
                    rs = min(max(r - 4, 0), nrows - 8)
                    if rs <= krow < rs + 8:
                        sub = np.where(colok.T, 0.0, NEG).astype(np.float32)
                        mask[v, kr_ * 64:(kr_ + 1) * 64, blk * 128 + qr_ * 64:blk * 128 + (qr_ + 1) * 64] = sub
    return C, S, mask


def _tb_gather(rel_bias):
    o = np.arange(-4, 5)
    kr = np.arange(128) // 64
    kc = np.arange(128) % 64
    dy = np.clip(2 * o[:, None, None] + kr[None, :, None] - kr[None, None, :] + 7, 0, 14)
    dx = np.clip(kc[:, None] - kc[None, :], -15, 15) + 15
    dx = np.broadcast_to(dx[None], dy.shape)
    return np.ascontiguousarray(rel_bias[:, :, dy, dx])


_PROG_CACHE = {}


def _run(nrows, depth, inputs):
    key = (nrows, depth)
    if key not in _PROG_CACHE:
        _PROG_CACHE[key] = build_program(nrows, depth)
    nc = _PROG_CACHE[key]
    f = lambda a: np.ascontiguousarray(np.asarray(a, dtype=np.float32))
    I = {k: f(v) for k, v in inputs.items()}
    ncores = I["x_sample"].shape[0]
    C, S, mask = _const_tables(nrows)
    tbg = _tb_gather(I["na_rel_bias"])
    lamv = np.ascontiguousarray(np.stack([I["lambda_q1"], I["lambda_k1"], I["lambda_q2"], I["lambda_k2"]], axis=1))
    shared = dict(w_ada=I["w_ada"], b_ada=I["b_ada"], w_in=I["w_in"], sg_norm_g=I["sg_norm_g"], sg_norm_b=I["sg_norm_b"],
                  w_spatial=I["w_spatial"], b_spatial=I["b_spatial"], lamv=lamv, diff_subln_g=I["diff_subln_g"],
                  tb=tbg, maskc=mask, ropec=C, ropes=S, w_br_a=I["w_br_a"], w_br_b=I["w_br_b"], w_br_c=I["w_br_c"],
                  w_mgate=I["w_mgate"], b_mgate=I["b_mgate"], w_out=I["w_out"], ln_g=I["ln_g"], ln_b=I["ln_b"])
    in_maps = []
    for b in range(ncores):
        m = dict(shared)
        m["xin"] = np.ascontiguousarray(np.concatenate(
            [I["x_sample"][b], I["x_prompt"][NPS * b:NPS * (b + 1)].reshape(NP, D)], axis=0))
        m["cvec"] = np.ascontiguousarray(np.stack([I["c"][b], I["c_ctx"]], axis=0))
        m["cdk"] = I["cache_diff_k"][b]
        m["cdv"] = I["cache_diff_v"][b]
        m["cnk"] = I["cache_na_k"][b]
        m["cnv"] = I["cache_na_v"][b]
        in_maps.append(m)
    res = run_bass_kernel_spmd(nc, in_maps, core_ids=list(range(ncores)))
    R = res.results
    NT = nrows * GW
    y_s = np.stack([R[b]["y"][:NT] for b in range(ncores)], axis=0)
    y_p = np.concatenate([R[b]["y"][NT:].reshape(NPS, SEQ, D) for b in range(ncores)], axis=0)
    cat = lambda n: np.concatenate([R[b][n] for b in range(ncores)], axis=0)
    return (y_p.astype(np.float32), y_s.astype(np.float32), cat("o_dk").astype(np.float32), cat("o_dv").astype(np.float32),
            cat("o_nk").astype(np.float32), cat("o_nv").astype(np.float32))


def kernel(**inputs):
    return _run(64, 4, inputs)
```
